# Optimizing a Trainium2 kernel written in Bass

```python
import math, functools
import jax, jax.numpy as jnp
from jax import lax
import numpy as np

D_MODEL = 2048
BATCH = 4
SEQ = 4096
DEPTH = 2

CTX_LEN = 256
GRID_W = 64
N_BRANCH = 4
BR_WIDTH = D_MODEL // N_BRANCH
HEAD_DIM = 128
N_HEADS = BR_WIDTH // HEAD_DIM
S5_GROUP = 16
S5_GROUPS = BR_WIDTH // S5_GROUP
S5_STATE = 64
S5_DT_MIN = 1e-3
S5_DT_MAX = 1e-1
D_FF = 4 * D_MODEL
CHUNK = 64
CONV_K = 3
NORM_EPS = 1e-6
NEG_BIG = -1e30
F_TINY = 1e-30
RET_DECAY_EXP0 = 5.0
MLSTM_F_BIAS_LO = 3.0
MLSTM_F_BIAS_HI = 6.0
N_MOD = 6

IN_SPLITS = (
    BR_WIDTH, BR_WIDTH, BR_WIDTH, BR_WIDTH, BR_WIDTH,
    BR_WIDTH, BR_WIDTH, BR_WIDTH, BR_WIDTH,
    BR_WIDTH,
    2 * BR_WIDTH, BR_WIDTH, BR_WIDTH, 4 * N_HEADS,
    N_BRANCH * D_MODEL,
)
IN_WIDTH = sum(IN_SPLITS)

kernel_name = "hybrid_gated_recurrent_diffusion_block"


def _split_cols(p):
    idx = np.cumsum(np.array(IN_SPLITS))[:-1].tolist()
    return jnp.split(p, idx, axis=-1)


def rmsnorm(x, g):
    xf = x.astype(jnp.float32)
    y = xf * lax.rsqrt(jnp.mean(jnp.square(xf), axis=-1, keepdims=True) + NORM_EPS)
    return (y * g.astype(jnp.float32)).astype(x.dtype)


def head_rmsnorm(o, g):
    b, l, h, d = o.shape
    return rmsnorm(o, g.reshape(h, d)).reshape(b, l, h * d)


def heads(t):
    return t.reshape(t.shape[:2] + (N_HEADS, HEAD_DIM))


def modulate(h, shift, scale):
    return h * (1.0 + scale) + shift


def _chunks(t):
    b, l, h = t.shape[:3]
    t = t.astype(jnp.float32).reshape((b, l // CHUNK, CHUNK, h) + t.shape[3:])
    return jnp.moveaxis(t, (1, 2), (0, 3))


def _unchunk(t, dtype):
    n, b, h, c = t.shape[:4]
    t = jnp.moveaxis(t, (0, 3), (1, 2))
    return t.reshape((b, n * c, h) + t.shape[4:]).astype(dtype)


def hgrn2_forget(z, lb):
    zf = z.astype(jnp.float32)
    lbf = lb.astype(jnp.float32)
    f = lbf + (1.0 - lbf) * jax.nn.sigmoid(zf)
    log_f = jnp.log(jnp.maximum(f, F_TINY))
    k = (1.0 - lbf) * jax.nn.sigmoid(-zf)
    return log_f, k


def hgrn2_scan(q, k, v, log_f, state):
    dtype = q.dtype
    tril = jnp.tril(jnp.ones((CHUNK, CHUNK), dtype=bool))[:, :, None]

    def step(s, blk):
        qb, kb, vb, fb = blk
        cum = jnp.cumsum(fb, axis=2)
        diff = cum[:, :, :, None, :] - cum[:, :, None, :, :]
        decay = jnp.where(tril, jnp.exp(jnp.where(tril, diff, 0.0)), 0.0)
        scores = jnp.einsum("bhtd,bhsd,bhtsd->bhts", qb, kb, decay)
        out = (jnp.einsum("bhts,bhsv->bhtv", scores, vb)
               + jnp.einsum("bhtd,bhdv->bhtv", qb * jnp.exp(cum), s))
        tail = cum[:, :, -1:, :]
        s = (jnp.exp(tail[:, :, 0, :, None]) * s
             + jnp.einsum("bhsd,bhsv->bhdv", kb * jnp.exp(tail - cum), vb))
        return s, out

    state, out = lax.scan(step, state, (_chunks(q), _chunks(k), _chunks(v), _chunks(log_f)))
    return _unchunk(out, dtype), state


def retention_scan(q, k, v, state, *, log_gamma):
    dtype = q.dtype
    lg = log_gamma.astype(jnp.float32)[:, None]
    t = jnp.arange(CHUNK, dtype=jnp.float32)
    rel = t[:, None] - t[None, :]
    intra = jnp.where(rel >= 0, jnp.exp(lg[:, :, None] * jnp.maximum(rel, 0.0)), 0.0)
    from_state = jnp.exp(lg * (t + 1.0))
    into_state = jnp.exp(lg * (CHUNK - 1.0 - t))
    chunk_decay = jnp.exp(lg * CHUNK)

    def step(s, blk):
        qb, kb, vb = blk
        scores = jnp.einsum("bhtd,bhsd->bhts", qb, kb) * intra
        out = (jnp.einsum("bhts,bhsv->bhtv", scores, vb)
               + from_state[:, :, None] * jnp.einsum("bhtd,bhdv->bhtv", qb, s))
        s = (chunk_decay[:, :, None] * s
             + jnp.einsum("bhsd,bhsv->bhdv", kb * into_state[:, :, None], vb))
        return s, out

    state, out = lax.scan(step, state, (_chunks(q), _chunks(k), _chunks(v)))
    return _unchunk(out, dtype), state


def mlstm_scan(q, k, v, log_i, log_f, state):
    dtype = q.dtype
    tril = jnp.tril(jnp.ones((CHUNK, CHUNK), dtype=bool))

    def step(carry, blk):
        cmat, nvec, m = carry
        qb, kb, vb, ib, fb = blk
        cum = jnp.cumsum(fb, axis=-1)
        logw = jnp.where(tril, cum[..., :, None] - cum[..., None, :] + ib[..., None, :], NEG_BIG)
        from_state = cum + m[..., None]
        m_t = jnp.maximum(from_state, jnp.max(logw, axis=-1))
        w = jnp.exp(logw - m_t[..., None])
        w_state = jnp.exp(from_state - m_t)
        scores = jnp.einsum("bhtd,bhsd->bhts", qb, kb) * w
        num = (jnp.einsum("bhts,bhsv->bhtv", scores, vb)
               + w_state[..., None] * jnp.einsum("bhtd,bhdv->bhtv", qb, cmat))
        den = jnp.sum(scores, axis=-1) + w_state * jnp.einsum("bhtd,bhd->bht", qb, nvec)
        h = num / jnp.maximum(jnp.abs(den), jnp.exp(-m_t))[..., None]
        total = cum[..., -1]
        logw_end = total[..., None] - cum + ib
        m_new = jnp.maximum(total + m, jnp.max(logw_end, axis=-1))
        w_end = jnp.exp(logw_end - m_new[..., None])
        keep = jnp.exp(total + m - m_new)
        cmat = keep[..., None, None] * cmat + jnp.einsum("bhs,bhsd,bhsv->bhdv", w_end, kb, vb)
        nvec = keep[..., None] * nvec + jnp.einsum("bhs,bhsd->bhd", w_end, kb)
        return (cmat, nvec, m_new), h

    state, out = lax.scan(step, state, (_chunks(q), _chunks(k), _chunks(v), _chunks(log_i), _chunks(log_f)))
    return _unchunk(out, dtype), state


def s5_discretize(a_re, a_im, log_dt, b_re, b_im, c_re, c_im):
    a_re = a_re.astype(jnp.float32)
    a_im = a_im.astype(jnp.float32)
    dt = jnp.exp(log_dt.astype(jnp.float32))[:, None]
    mag = jnp.exp(a_re * dt)
    lam_re = mag * jnp.cos(a_im * dt)
    lam_im = mag * jnp.sin(a_im * dt)
    den = a_re * a_re + a_im * a_im
    num_re, num_im = lam_re - 1.0, lam_im
    fr = (num_re * a_re + num_im * a_im) / den
    fi = (num_im * a_re - num_re * a_im) / den
    b_re = b_re.astype(jnp.float32)
    b_im = b_im.astype(jnp.float32)
    bb_re = fr[..., None] * b_re - fi[..., None] * b_im
    bb_im = fr[..., None] * b_im + fi[..., None] * b_re
    return dict(lam_re=lam_re, lam_im=lam_im, bb_re=bb_re, bb_im=bb_im,
                c_re=c_re.astype(jnp.float32), c_im=c_im.astype(jnp.float32))


def _complex_affine_combine(e1, e2):
    a1r, a1i, b1r, b1i = e1
    a2r, a2i, b2r, b2i = e2
    return (a1r * a2r - a1i * a2i, a1r * a2i + a1i * a2r,
            a2r * b1r - a2i * b1i + b2r, a2r * b1i + a2i * b1r + b2i)


def s5_scan(u, state, *, lam_re, lam_im, bb_re, bb_im, c_re, c_im):
    dtype = u.dtype
    b, l, _ = u.shape
    ug = jnp.swapaxes(u.astype(jnp.float32).reshape(b, l, S5_GROUPS, S5_GROUP), 0, 1)
    bu_re = jnp.einsum("lbgh,gph->lbgp", ug, bb_re)
    bu_im = jnp.einsum("lbgh,gph->lbgp", ug, bb_im)
    x0_re, x0_im = state
    bu_re = bu_re.at[0].add(lam_re * x0_re - lam_im * x0_im)
    bu_im = bu_im.at[0].add(lam_re * x0_im + lam_im * x0_re)
    a_re = jnp.broadcast_to(lam_re, (l, 1) + lam_re.shape)
    a_im = jnp.broadcast_to(lam_im, (l, 1) + lam_im.shape)
    _, _, x_re, x_im = lax.associative_scan(_complex_affine_combine, (a_re, a_im, bu_re, bu_im), axis=0)
    y = jnp.einsum("lbgp,ghp->blgh", x_re, c_re) - jnp.einsum("lbgp,ghp->blgh", x_im, c_im)
    return y.reshape(b, l, BR_WIDTH).astype(dtype), (x_re[-1], x_im[-1])


def bidirectional(scan_fwd, scan_bwd, ctx_fwd, lat_fwd, ctx_bwd, lat_bwd, state0):
    oc_f, st = scan_fwd(*ctx_fwd, state0)
    ox_f, _ = scan_fwd(*lat_fwd, st)
    rev = lambda ts: [jnp.flip(t, axis=1) for t in ts]
    oc_b, st = scan_bwd(*rev(ctx_bwd), state0)
    ox_b, _ = scan_bwd(*rev(lat_bwd), st)
    return oc_f + jnp.flip(oc_b, axis=1), ox_f + jnp.flip(ox_b, axis=1)


def grid_dwconv(t, w, bias, rows, cols):
    b, l, ch = t.shape
    img = t.reshape(b, rows, cols, ch)
    out = lax.conv_general_dilated(img, w[:, :, None, :].astype(t.dtype), (1, 1), "SAME",
                                   dimension_numbers=("NHWC", "HWIO", "NHWC"),
                                   feature_group_count=ch)
    return (out + bias).reshape(b, l, ch)


def gated_merge(branch_outs, gate_pre, w_branch, w_out):
    g = gate_pre.reshape(gate_pre.shape[:2] + (N_BRANCH, D_MODEL))
    y = jax.nn.sigmoid(g[:, :, 0]) * (branch_outs[0] @ w_branch[0])
    for j in range(1, N_BRANCH):
        y = y + jax.nn.sigmoid(g[:, :, j]) * (branch_outs[j] @ w_branch[j])
    return y @ w_out


def token_mixer(hc, hx, lp, lb, with_ctx):
    bsz = hx.shape[0]
    rows = hx.shape[1] // GRID_W
    cc = _split_cols(hc @ lp["w_in"])
    cx = _split_cols(hx @ lp["w_in"])
    zeros_s = jnp.zeros((bsz, N_HEADS, HEAD_DIM, HEAD_DIM), jnp.float32)

    def hgrn_in(cols, d):
        log_f, k = hgrn2_forget(cols[1 + d], lb[d])
        return (heads(jax.nn.silu(cols[0])), heads(k), heads(cols[3]), heads(log_f))
    raw_a = bidirectional(hgrn2_scan, hgrn2_scan, hgrn_in(cc, 0), hgrn_in(cx, 0),
                          hgrn_in(cc, 1), hgrn_in(cx, 1), zeros_s)

    log_gamma = jnp.log1p(-jnp.exp(lp["ret_decay"].astype(jnp.float32)))
    ret_in = lambda cols: (heads(cols[5]), heads(cols[6]) * HEAD_DIM ** -0.5, heads(cols[7]))
    raw_b = bidirectional(functools.partial(retention_scan, log_gamma=log_gamma[0]),
                          functools.partial(retention_scan, log_gamma=log_gamma[1]),
                          ret_in(cc), ret_in(cx), ret_in(cc), ret_in(cx), zeros_s)

    s5_fns = [functools.partial(s5_scan, **s5_discretize(
        lp["s5_a_re"][d], lp["s5_a_im"][d], lp["s5_log_dt"][d], lp["s5_b_re"][d],
        lp["s5_b_im"][d], lp["s5_c_re"][d], lp["s5_c_im"][d])) for d in (0, 1)]
    x0 = (jnp.zeros((bsz, S5_GROUPS, S5_STATE), jnp.float32),
          jnp.zeros((bsz, S5_GROUPS, S5_STATE), jnp.float32))
    raw_c = bidirectional(s5_fns[0], s5_fns[1], (cc[9],), (cx[9],), (cc[9],), (cx[9],), x0)

    def mlstm_in(cols, n_rows, n_cols):
        qk = jax.nn.silu(grid_dwconv(cols[10], lp["mlstm_conv_w"], lp["mlstm_conv_b"], n_rows, n_cols))
        q, k = jnp.split(qk, 2, axis=-1)
        g = (cols[13] + lp["mlstm_gate_b"].reshape(-1)).astype(jnp.float32)
        g = g.reshape(g.shape[:2] + (2, 2, N_HEADS))
        base = (heads(q), heads(k) * HEAD_DIM ** -0.5, heads(cols[11]))
        return [base + (g[:, :, d, 0], jax.nn.log_sigmoid(g[:, :, d, 1])) for d in (0, 1)]
    mc = mlstm_in(cc, 1, hc.shape[1])
    mx = mlstm_in(cx, rows, GRID_W)
    st0 = (zeros_s, jnp.zeros((bsz, N_HEADS, HEAD_DIM), jnp.float32),
           jnp.zeros((bsz, N_HEADS), jnp.float32))
    raw_d = bidirectional(mlstm_scan, mlstm_scan, mc[0], mx[0], mc[1], mx[1], st0)

    def finish(cols, s):
        oa = head_rmsnorm(raw_a[s], lp["hgrn_norm"]) * jax.nn.silu(cols[4])
        ob = head_rmsnorm(raw_b[s], lp["ret_norm"]) * jax.nn.silu(cols[8])
        yc = jax.nn.gelu(raw_c[s] + lp["s5_d"] * cols[9], approximate=False)
        oc = yc * jax.nn.sigmoid(yc @ lp["s5_glu_w"] + lp["s5_glu_b"])
        od = head_rmsnorm(raw_d[s], lp["mlstm_norm"]) * jax.nn.silu(cols[12])
        return gated_merge((oa, ob, oc, od), cols[14], lp["w_branch"], lp["w_out"])

    out_x = finish(cx, 1)
    out_c = finish(cc, 0) if with_ctx else None
    return out_c, out_x


def sq_relu_mlp(h, w1, w2):
    return jnp.square(jax.nn.relu(h @ w1)) @ w2


def trunk_layer(xc, xx, c, c_ctx, lp, lb, with_ctx):
    mod_x = jnp.split((jax.nn.silu(c) @ lp["w_mod"] + lp["b_mod"])[:, None, :], N_MOD, axis=-1)
    mod_c = jnp.split((jax.nn.silu(c_ctx)[None] @ lp["w_mod"] + lp["b_mod"])[:, None, :], N_MOD, axis=-1)
    hx = modulate(rmsnorm(xx, lp["norm_mix"]), mod_x[0], mod_x[1])
    hc = modulate(rmsnorm(xc, lp["norm_mix"]), mod_c[0], mod_c[1])
    mix_c, mix_x = token_mixer(hc, hx, lp, lb, with_ctx)
    xx = xx + mod_x[2] * mix_x
    xx = xx + mod_x[5] * sq_relu_mlp(modulate(rmsnorm(xx, lp["norm_mlp"]), mod_x[3], mod_x[4]),
                                     lp["w_ff1"], lp["w_ff2"])
    if with_ctx:
        xc = xc + mod_c[2] * mix_c
        xc = xc + mod_c[5] * sq_relu_mlp(modulate(rmsnorm(xc, lp["norm_mlp"]), mod_c[3], mod_c[4]),
                                         lp["w_ff1"], lp["w_ff2"])
    return xc, xx


def setup_inputs(seed: int = 0) -> dict:
    key = jax.random.key(seed)
    keys = iter(jax.random.split(key, 48))
    f32 = jnp.float32
    nrm = lambda shape, scale: scale * jax.random.normal(next(keys), shape, f32)
    L = DEPTH
    G, P, Hg = S5_GROUPS, S5_STATE, S5_GROUP
    x = nrm((BATCH, SEQ, D_MODEL), 1.0)
    c = nrm((BATCH, D_MODEL), 1.0)
    ctx = nrm((BATCH, CTX_LEN, D_MODEL), 1.0)
    c_ctx = nrm((D_MODEL,), 1.0)
    w_mod = nrm((L, D_MODEL, N_MOD * D_MODEL), 0.5 * D_MODEL ** -0.5)
    b_mod = nrm((L, N_MOD * D_MODEL), 0.02)
    norm_mix = 1.0 + nrm((L, D_MODEL), 0.02)
    norm_mlp = 1.0 + nrm((L, D_MODEL), 0.02)
    w_in = nrm((L, D_MODEL, IN_WIDTH), D_MODEL ** -0.5)
    hgrn_lb_logits = nrm((L, 2, BR_WIDTH), 0.5)
    hgrn_norm = 1.0 + nrm((L, BR_WIDTH), 0.02)
    ret_base = -(RET_DECAY_EXP0 + jnp.arange(N_HEADS, dtype=f32)) * math.log(2.0)
    ret_decay = ret_base + nrm((L, 2, N_HEADS), 0.05)
    ret_norm = 1.0 + nrm((L, BR_WIDTH), 0.02)
    s5_a_re = -0.5 * jnp.exp(nrm((L, 2, G, P), 0.05))
    s5_a_im = math.pi * jnp.arange(P, dtype=f32) + nrm((L, 2, G, P), 0.01)
    u = jax.random.uniform(next(keys), (L, 2, G), f32)
    s5_log_dt = math.log(S5_DT_MIN) + u * (math.log(S5_DT_MAX) - math.log(S5_DT_MIN))
    s5_b_re = nrm((L, 2, G, P, Hg), (2 * Hg) ** -0.5)
    s5_b_im = nrm((L, 2, G, P, Hg), (2 * Hg) ** -0.5)
    s5_c_re = nrm((L, 2, G, Hg, P), (2 * P) ** -0.5)
    s5_c_im = nrm((L, 2, G, Hg, P), (2 * P) ** -0.5)
    s5_d = nrm((L, BR_WIDTH), 0.5)
    s5_glu_w = nrm((L, BR_WIDTH, BR_WIDTH), BR_WIDTH ** -0.5)
    s5_glu_b = nrm((L, BR_WIDTH), 0.02)
    mlstm_conv_w = nrm((L, CONV_K, CONV_K, 2 * BR_WIDTH), 1.0 / CONV_K)
    mlstm_conv_b = nrm((L, 2 * BR_WIDTH), 0.02)
    i_bias = nrm((L, 2, N_HEADS), 0.1)
    f_bias = jnp.linspace(MLSTM_F_BIAS_LO, MLSTM_F_BIAS_HI, N_HEADS, dtype=f32) + nrm((L, 2, N_HEADS), 0.1)
    mlstm_gate_b = jnp.stack([i_bias, f_bias], axis=2)
    mlstm_norm = 1.0 + nrm((L, BR_WIDTH), 0.02)
    w_branch = nrm((L, N_BRANCH, BR_WIDTH, D_MODEL), BR_WIDTH ** -0.5)
    w_out = nrm((L, D_MODEL, D_MODEL), D_MODEL ** -0.5)
    w_ff1 = nrm((L, D_MODEL, D_FF), D_MODEL ** -0.5)
    w_ff2 = nrm((L, D_FF, D_MODEL), D_FF ** -0.5)
    final_norm = 1.0 + nrm((D_MODEL,), 0.02)
    return {"x": x, "c": c, "ctx": ctx, "c_ctx": c_ctx, "w_mod": w_mod, "b_mod": b_mod,
            "norm_mix": norm_mix, "norm_mlp": norm_mlp, "w_in": w_in,
            "hgrn_lb_logits": hgrn_lb_logits, "hgrn_norm": hgrn_norm,
            "ret_decay": ret_decay, "ret_norm": ret_norm,
            "s5_a_re": s5_a_re, "s5_a_im": s5_a_im, "s5_log_dt": s5_log_dt,
            "s5_b_re": s5_b_re, "s5_b_im": s5_b_im, "s5_c_re": s5_c_re, "s5_c_im": s5_c_im,
            "s5_d": s5_d, "s5_glu_w": s5_glu_w, "s5_glu_b": s5_glu_b,
            "mlstm_conv_w": mlstm_conv_w, "mlstm_conv_b": mlstm_conv_b,
            "mlstm_gate_b": mlstm_gate_b, "mlstm_norm": mlstm_norm,
            "w_branch": w_branch, "w_out": w_out, "w_ff1": w_ff1, "w_ff2": w_ff2,
            "final_norm": final_norm}


def reference(x, c, ctx, c_ctx, w_mod, b_mod, norm_mix, norm_mlp, w_in, hgrn_lb_logits, hgrn_norm,
              ret_decay, ret_norm, s5_a_re, s5_a_im, s5_log_dt, s5_b_re, s5_b_im, s5_c_re, s5_c_im,
              s5_d, s5_glu_w, s5_glu_b, mlstm_conv_w, mlstm_conv_b, mlstm_gate_b, mlstm_norm,
              w_branch, w_out, w_ff1, w_ff2, final_norm):
    p_lb = jax.nn.softmax(hgrn_lb_logits.astype(jnp.float32), axis=0)
    lower_bounds = jnp.cumsum(p_lb, axis=0) - p_lb[0]
    hc, hx = ctx, x
    for l in range(DEPTH):
        lp = dict(w_mod=w_mod[l], b_mod=b_mod[l], norm_mix=norm_mix[l], norm_mlp=norm_mlp[l],
                  w_in=w_in[l], hgrn_norm=hgrn_norm[l], ret_decay=ret_decay[l], ret_norm=ret_norm[l],
                  s5_a_re=s5_a_re[l], s5_a_im=s5_a_im[l], s5_log_dt=s5_log_dt[l],
                  s5_b_re=s5_b_re[l], s5_b_im=s5_b_im[l], s5_c_re=s5_c_re[l], s5_c_im=s5_c_im[l],
                  s5_d=s5_d[l], s5_glu_w=s5_glu_w[l], s5_glu_b=s5_glu_b[l],
                  mlstm_conv_w=mlstm_conv_w[l], mlstm_conv_b=mlstm_conv_b[l],
                  mlstm_gate_b=mlstm_gate_b[l], mlstm_norm=mlstm_norm[l],
                  w_branch=w_branch[l], w_out=w_out[l], w_ff1=w_ff1[l], w_ff2=w_ff2[l])
        hc, hx = trunk_layer(hc, hx, c, c_ctx, lp, lower_bounds[l], l < DEPTH - 1)
    return rmsnorm(hx, final_norm)
```

```python
import contextlib
import math
import numpy as np
import concourse.bass as bass
import concourse.mybir as mybir
from concourse.bass_utils import run_bass_kernel_spmd

F32 = mybir.dt.float32
BF16 = mybir.dt.bfloat16
I32 = mybir.dt.int32
AF = mybir.ActivationFunctionType
ALU = mybir.AluOpType

ENGS = ("pe", "act", "dve", "pool", "sp")
KSEM = 8
NDQ = 12

D = 2048
BR = 512
DFF = 8192
INW = 15376
CTX = 256
EPS = 1e-6


class Tile:
    def __init__(self, t, name=""):
        self.t = t
        self.name = name
        self.writers = {}
        self.readers = {}

    def __getitem__(self, k):
        return self.t[k]


class Prog:
    def __init__(self, nc):
        self.nc = nc
        self.es = contextlib.ExitStack()
        self.streams = {e: [] for e in ENGS}
        self.count = {e: 0 for e in ENGS}
        self.sems = {}
        for e in ENGS:
            self.sems[e] = [self.es.enter_context(nc.semaphore(f"s_{e}{k}")) for k in range(KSEM)]
        self.dq = {}
        for q in ("sp", "pool", "act"):
            self.dq[q] = [self.es.enter_context(nc.semaphore(f"d_{q}{k}")) for k in range(NDQ)]
        self.dq_count = {q: [0] * NDQ for q in self.dq}
        self.dq_next = {q: 0 for q in self.dq}
        self.seen = {e: {} for e in ENGS}
        self.n_sb = 0

    def sb(self, shape, dtype, name=None):
        self.n_sb += 1
        name = "sb_" + (name or f"t{self.n_sb}")
        t = self.es.enter_context(self.nc.sbuf_tensor(name, list(shape), dtype))
        return Tile(t, name)

    def arena_init(self, words):
        self.arena = self.es.enter_context(self.nc.sbuf_tensor("arena", [128, words], F32))
        self.arena_words = words
        self.arena_off = 0

    def arena_reset(self):
        self.arena_off = 0

    def ar(self, shape, dtype, name=None, at=None):
        esz = 4 if dtype in (F32, I32) else 2
        nfree = 1
        for d_ in shape[1:]:
            nfree *= d_
        words = (nfree * esz + 31) // 32 * 8
        if at is None:
            assert self.arena_off + words <= self.arena_words, (name, self.arena_off, words)
            off = self.arena_off
            self.arena_off += words
        else:
            off = at
        v = self.arena[0:shape[0], off:off + words]
        if dtype != F32:
            v = v.bitcast(dtype)
        v = v[:, 0:nfree]
        if len(shape) == 3:
            v = v.rearrange("p (a b) -> p a b", a=shape[1])
        elif len(shape) == 4:
            v = v.rearrange("p (a b c) -> p a b c", a=shape[1], b=shape[2])
        tl_ = Tile(v, name or "ar")
        tl_.off, tl_.words = off, words
        return tl_

    def ps(self, shape, dtype=F32, name=None):
        self.n_sb += 1
        name = name or f"ps{self.n_sb}"
        t = self.es.enter_context(self.nc.psum_tensor(name, list(shape), dtype))
        return Tile(t, name)

    def dram(self, name, shape, dtype):
        t = self.nc.dram_tensor(name, list(shape), dtype)
        return Tile(t.ap(), name)

    def _sem_of(self, stream, idx):
        if stream in ENGS:
            return self.sems[stream][idx % KSEM], idx // KSEM + 1
        q, k = stream
        return self.dq[q][k], 16 * (idx + 1)

    def _need(self, eng, stream, idx):
        if stream == "pe" and eng == "pe":
            return
        if self.seen[eng].get(stream, -1) >= idx:
            return
        self.seen[eng][stream] = idx
        sem, val = self._sem_of(stream, idx)
        self.streams[eng].append(lambda e, sem=sem, val=val: e.wait_ge(sem, val))

    def _deps(self, eng, reads, writes):
        for t in reads:
            for s, i in t.writers.items():
                self._need(eng, s, i)
        for t in writes:
            for s, i in t.writers.items():
                self._need(eng, s, i)
            for s, i in t.readers.items():
                self._need(eng, s, i)

    def op(self, eng, fn, reads=(), writes=()):
        self._deps(eng, reads, writes)
        idx = self.count[eng]
        self.count[eng] += 1
        sem = self.sems[eng][idx % KSEM]
        self.streams[eng].append(lambda e, fn=fn, sem=sem: fn(e).then_inc(sem, 1))
        for t in reads:
            t.readers[eng] = idx
        for t in writes:
            t.writers = {eng: idx}
            t.readers = {}
        return idx

    def dma(self, q, out, in_, reads=(), writes=(), **kw):
        self._deps(q, reads, writes)
        k = self.dq_next[q]
        self.dq_next[q] = (k + 1) % NDQ
        j = self.dq_count[q][k]
        if j > 0:
            self._need(q, (q, k), j - 1)
        self.dq_count[q][k] += 1
        sem = self.dq[q][k]
        self.streams[q].append(
            lambda e, sem=sem, out=out, in_=in_, kw=kw: e.dma_start(out=out, in_=in_, **kw).then_inc(sem, 16))
        st = (q, k)
        for t in reads:
            t.readers[st] = j
        for t in writes:
            t.writers = {st: j}
            t.readers = {}

    def barrier(self):
        for e in ENGS:
            for s in ENGS:
                if s != e and self.count[s] > 0:
                    self._need(e, s, self.count[s] - 1)
            for q in self.dq:
                for k in range(NDQ):
                    if self.dq_count[q][k] > 0:
                        self._need(e, (q, k), self.dq_count[q][k] - 1)

    def finish(self):
        self.barrier()
        nc = self.nc
        with nc.Block() as block:
            @block.tensor
            def _(e):
                for f in self.streams["pe"]:
                    f(e)

            @block.scalar
            def _(e):
                for f in self.streams["act"]:
                    f(e)

            @block.vector
            def _(e):
                for f in self.streams["dve"]:
                    f(e)

            @block.gpsimd
            def _(e):
                for f in self.streams["pool"]:
                    f(e)

            @block.sync
            def _(e):
                for f in self.streams["sp"]:
                    f(e)
        self.es.close()


def host_consts():
    t = np.arange(64, dtype=np.float32)
    c = {}
    c["ident"] = np.eye(128, dtype=np.float32)
    triU = (t[:, None] <= t[None, :]).astype(np.float32)
    triL = (t[:, None] >= t[None, :]).astype(np.float32)
    c["tri"] = np.stack([triU, triL])
    c["iotaF"] = np.stack([np.tile(t + 1, (128, 1)), np.tile(64 - t, (128, 1))]).astype(np.float32)
    c["iotaK"] = np.stack([np.tile((t + 1)[:, None], (1, 128)), np.tile((64 - t)[:, None], (1, 128))]).astype(np.float32)
    rst = np.ones((128, 16, 64), np.float32)
    rst[:, :, 0] = 0.0
    c["rst"] = rst
    return c


def build(NLAT, dump=(), stop_after=None):
    nc = bass.Bass("TRN2", target_bir_lowering=False)
    P = Prog(nc)
    NT = CTX + NLAT
    NSC = NT // 256
    ROWS = NLAT // 64

    def din(name, shape):
        return nc.dram_tensor(name, list(shape), F32, kind="ExternalInput").ap()

    xin = din("xin", [NT, D])
    c2 = din("c2", [2, D])
    w_mod = din("w_mod", [2, D, 6 * D])
    b_mod = din("b_mod", [2, 6 * D])
    norm_mix = din("norm_mix", [2, D])
    norm_mlp = din("norm_mlp", [2, D])
    w_in = din("w_in", [2, D, INW])
    lb_logits = din("hgrn_lb_logits", [2, 2, BR])
    hgrn_norm = din("hgrn_norm", [2, BR])
    ret_decay = din("ret_decay", [2, 8])
    ret_norm = din("ret_norm", [2, BR])
    s5_a_re = din("s5_a_re", [2, 2, 32, 64])
    s5_a_im = din("s5_a_im", [2, 2, 32, 64])
    s5_log_dt = din("s5_log_dt", [2, 2, 32])
    s5_b_re = din("s5_b_re", [2, 2, 32, 64, 16])
    s5_b_im = din("s5_b_im", [2, 2, 32, 64, 16])
    s5_c_re = din("s5_c_re", [2, 2, 32, 16, 64])
    s5_c_im = din("s5_c_im", [2, 2, 32, 16, 64])
    s5_d = din("s5_d", [2, BR])
    s5_glu_w = din("s5_glu_w", [2, BR, BR])
    s5_glu_b = din("s5_glu_b", [2, BR])
    conv_w = din("mlstm_conv_w", [2, 3, 3, 2 * BR])
    conv_b = din("mlstm_conv_b", [2, 2 * BR])
    gate_b = din("mlstm_gate_b", [2, 16])
    mlstm_norm = din("mlstm_norm", [2, BR])
    w_branch = din("w_branch", [2, 4, BR, D])
    w_out = din("w_out", [2, D, D])
    w_ff1 = din("w_ff1", [2, D, DFF])
    w_ff2 = din("w_ff2", [2, DFF, D])
    final_norm = din("final_norm", [1, D])
    k_ident = din("ident", [128, 128])
    k_tri = din("tri", [2, 64, 64])
    k_iotaF = din("iotaF", [2, 128, 64])
    k_iotaK = din("iotaK", [2, 64, 128])
    k_rst = din("rst", [128, 16, 64])
    out = nc.dram_tensor("out", [NLAT, D], F32, kind="ExternalOutput").ap()

    SCR = {}

    def scratch(name, shape, dtype):
        t = nc.dram_tensor(name, list(shape), dtype).ap()
        SCR[name] = (t, shape, dtype)
        return t

    xres = scratch("xres", [NT, D], F32)
    hF = scratch("hF", [D, NT], BF16)
    scF = scratch("scF", [D, 2], BF16)
    mod_d = scratch("mod_d", [2, 6 * D], F32)
    FB = {}
    for nm in ("hq", "hkF0", "hkF1", "hog", "rq", "rkF", "rog", "uF", "mz", "mq", "mk"):
        FB[nm] = scratch(nm, [BR, NT], BF16)
    mqk = scratch("mqk", [2 * BR, NT], F32)
    gF = scratch("gF", [4 * D, NT], BF16)
    KB = {}
    for nm in ("hkK0", "hkK1", "hv", "rkK", "rv", "mv"):
        KB[nm] = scratch(nm, [NT, BR], BF16)
    hlogf = [scratch("hlogf0", [NT, BR], F32), scratch("hlogf1", [NT, BR], F32)]
    mg = scratch("mg", [NT, 16], F32)
    rawf = [scratch(f"rawf{j}", [BR, NT], F32) for j in range(4)]
    oF = scratch("oF", [4 * BR, NT], BF16)
    yF = scratch("yF", [D, NT], BF16)
    yaccd = scratch("yaccd", [D, NT], F32)
    aF = scratch("aF", [DFF, NT], BF16)

    PB = [P.ps([128, 1024], F32, name=f"pb{i}") for i in range(4)]
    slot_order = [0, 2, 4, 6, 1, 3, 5, 7]
    st = {"slot": 0, "w": 0, "sf": 0, "sb": 0}

    def next_ps():
        s = slot_order[st["slot"] % 8]
        st["slot"] += 1
        tl = PB[s // 2]
        return tl, tl.t[:, (s % 2) * 512:(s % 2) * 512 + 512]

    stg_f = [P.sb([128, 512], F32, f"stgf{i}") for i in range(4)]
    stg_b = [P.sb([128, 512], BF16, f"stgb{i}") for i in range(4)]

    def sf():
        st["sf"] += 1
        return stg_f[st["sf"] % 4]

    def sbb():
        st["sb"] += 1
        return stg_b[st["sb"] % 4]

    ident = P.sb([128, 128], F32, "ident")
    ident_b = P.sb([128, 128], BF16, "identb")
    ones_f = P.sb([128, 128], F32, "onesf")
    ones_b = P.sb([128, 128], BF16, "onesb")
    tri = [P.sb([64, 64], F32, f"tri{d}") for d in range(2)]
    ntri = [P.sb([64, 64], F32, f"ntri{d}") for d in range(2)]
    iotaF = [P.sb([128, 64], F32, f"iotaF{d}") for d in range(2)]
    iotaK = [P.sb([64, 128], F32, f"iotaK{d}") for d in range(2)]
    rst = P.sb([128, 16, 64], F32, "rst")
    modcol = P.sb([128, 2, 6, 16], F32, "modcol")
    gcolA = P.sb([128, 2, 16], F32, "gcolA")
    ngc = [P.sb([128, 16], F32, f"ngc{i}") for i in range(2)]
    lbrow = [P.sb([128, BR], F32, f"lbrow{d}") for d in range(2)]
    omlrow = [P.sb([128, BR], F32, f"omlrow{d}") for d in range(2)]
    omlcol = [P.sb([128, 4], F32, f"omlcol{d}") for d in range(2)]
    hn_col = [P.sb([128, 4], F32, f"hncol{j}") for j in range(3)]
    gb_row = P.sb([128, 16], F32, "gbrow")
    lgB = P.sb([128, 8], F32, "lgB")
    nlgB = P.sb([128, 8], F32, "nlgB")
    cw = P.sb([128, 8, 9], F32, "cw")
    cbias = P.sb([128, 8], F32, "cbias")
    s5d_col = P.sb([128, 4], F32, "s5dcol")
    glub_col = P.sb([128, 4], F32, "glubcol")
    gluw = P.sb([128, 4, BR], BF16, "gluw")
    ssq = P.sb([128, 4], F32, "ssq")
    P.arena_init(44500)

    def act(out_ap, in_ap, func, reads, writes, **kw):
        P.op("act", lambda e: e.activation(out_ap, in_ap, func, **kw), reads=reads, writes=writes)

    def tt(out_ap, a, b, op, reads, writes, eng="dve"):
        P.op(eng, lambda e: e.tensor_tensor(out_ap, a, b, op), reads=reads, writes=writes)

    def ts(out_ap, a, s1, s2, op0, op1, reads, writes):
        if op1 is None:
            P.op("dve", lambda e: e.tensor_scalar(out_ap, a, s1, None, op0), reads=reads, writes=writes)
        else:
            P.op("dve", lambda e: e.tensor_scalar(out_ap, a, s1, s2, op0, op1), reads=reads, writes=writes)

    def cp(out_ap, in_ap, reads, writes, eng="dve"):
        P.op(eng, lambda e: e.tensor_copy(out_ap, in_ap), reads=reads, writes=writes)

    def mm(out_ap, lhsT, rhs, start, stop, reads, writes):
        P.op("pe", lambda e: e.matmul(out_ap, lhsT=lhsT, rhs=rhs, start=start, stop=stop), reads=reads, writes=writes)

    def memset(tile_, ap, val, eng="dve"):
        P.op(eng, lambda e: e.memset(ap, val), writes=[tile_])

    def recip(tile_, ap):
        P.op("dve", lambda e: e.reciprocal(ap, ap), reads=[tile_], writes=[tile_])

    def small_dma(out_ap, in_ap, tile_, q="sp"):
        P.dma(q, out_ap, in_ap, writes=[tile_], allow_slow_non_contiguous=True)

    P.dma("sp", ident[:], k_ident, writes=[ident])
    cp(ident_b[:], ident[:], [ident], [ident_b])
    memset(ones_f, ones_f[:], 1.0)
    memset(ones_b, ones_b[:], 1.0)
    for d in range(2):
        P.dma("sp", tri[d][:], k_tri[d], writes=[tri[d]])
        ts(ntri[d][:], tri[d][:], -1.0, None, ALU.mult, None, [tri[d]], [ntri[d]])
        P.dma("sp", iotaF[d][:], k_iotaF[d], writes=[iotaF[d]])
        P.dma("sp", iotaK[d][:], k_iotaK[d], writes=[iotaK[d]])
    P.dma("sp", rst[:], k_rst, writes=[rst])
    P.dma("sp", xres, xin)
    P.barrier()

    def gemm(A, Kd, ntok, Wap, jobs, TBLK):
        P.arena_reset()
        ablk = P.ar([128, 36864], BF16, "ablk")
        wbuf = [P.ar([128, 8192], BF16, f"wbuf{i}") for i in range(2)]
        KC = Kd // 128
        for tb0 in range(0, ntok, TBLK):
            tbn = min(TBLK, ntok - tb0)
            av = ablk.t[:, 0:KC * tbn].rearrange("p (k n) -> p k n", k=KC)
            P.dma("sp", av, A[:, tb0:tb0 + tbn].rearrange("(k p) n -> p k n", p=128), writes=[ablk])
            for (col0, ncols, mode, epi) in jobs:
                wt = wbuf[st["w"] % 2]
                st["w"] += 1
                wv = wt.t[:, 0:KC * ncols].rearrange("p (k n) -> p k n", k=KC)
                P.dma("pool", wv, Wap[:, col0:col0 + ncols].rearrange("(k p) n -> p k n", p=128), writes=[wt])
                if mode == "F":
                    for cg in range(0, ncols, 128):
                        cgn = min(128, ncols - cg)
                        for t0 in range(0, tbn, 512):
                            tn = min(512, tbn - t0)
                            pt, pv = next_ps()
                            for k in range(KC):
                                mm(pv[0:cgn, 0:tn], wv[:, k, cg:cg + cgn], av[:, k, t0:t0 + tn], k == 0, k == KC - 1,
                                   [wt, ablk], [pt])
                            epi(pt, pv, cg, cgn, tb0 + t0, tn)
                else:
                    for t0 in range(0, tbn, 128):
                        tn = min(128, tbn - t0)
                        pt, pv = next_ps()
                        for k in range(KC):
                            mm(pv[0:tn, 0:ncols], av[:, k, t0:t0 + tn], wv[:, k, 0:ncols], k == 0, k == KC - 1,
                               [wt, ablk], [pt])
                        epi(pt, pv, tb0 + t0, tn)
        P.barrier()

    def epiF_simple(dst, row0, func, scale=1.0, dtype=BF16):
        def epi(pt, pv, cg, cgn, tok0, tn):
            s = sbb() if dtype == BF16 else sf()
            act(s[0:cgn, 0:tn], pv[0:cgn, 0:tn], func, [pt], [s], scale=scale)
            P.dma("sp", dst[row0 + cg:row0 + cg + cgn, tok0:tok0 + tn], s[0:cgn, 0:tn], reads=[s])
        return epi

    def epiK_simple(dst, func, scale=1.0, dtype=BF16):
        def epi(pt, pv, tok0, tn):
            s = sbb() if dtype == BF16 else sf()
            nco = dst.shape[1]
            act(s[0:tn, 0:nco], pv[0:tn, 0:nco], func, [pt], [s], scale=scale)
            P.dma("sp", dst[tok0:tok0 + tn, :], s[0:tn, 0:nco], reads=[s])
        return epi

    def layer_setup(l):
        cc = sf()
        ccv = cc[:, 0:32].rearrange("p (j v) -> p j v", v=2)
        for v in range(2):
            small_dma(ccv[:, :, v], c2[v].rearrange("(j p) -> p j", p=128), cc)
        cb = sbb()
        cbv = cb[:, 0:32].rearrange("p (j v) -> p j v", v=2)
        act(cbv, ccv, AF.Silu, [cc], [cb])
        P.dma("sp", scF.rearrange("(j p) v -> p j v", p=128), cbv, reads=[cb], allow_slow_non_contiguous=True)
        P.barrier()

        def epi_mod_factory(col0):
            def epi(pt, pv, tok0, tn):
                bm = sf()
                P.dma("act", bm[0:2, :], b_mod[l:l + 1, col0:col0 + 512].partition_broadcast(2), writes=[bm])
                s = sf()
                tt(s[0:2, 0:512], pv[0:2, 0:512], bm[0:2, :], ALU.add, [pt, bm], [s])
                P.dma("sp", mod_d[:, col0:col0 + 512], s[0:2, 0:512], reads=[s])
            return epi
        jobs = [(c0, 512, "K", epi_mod_factory(c0)) for c0 in range(0, 6 * D, 512)]
        gemm(scF, D, 2, w_mod[l], jobs, 128)
        for v in range(2):
            for m_ in range(6):
                small_dma(modcol[:, v, m_, :], mod_d[v, m_ * D:(m_ + 1) * D].rearrange("(j p) -> p j", p=128), modcol)
        small_dma(ngc[0][:], norm_mix[l].rearrange("(j p) -> p j", p=128), ngc[0])
        small_dma(ngc[1][:], norm_mlp[l].rearrange("(j p) -> p j", p=128), ngc[1])
        for d in range(2):
            if l == 0:
                memset(lbrow[d], lbrow[d][:], 0.0)
                memset(omlrow[d], omlrow[d][:], 1.0)
                memset(omlcol[d], omlcol[d][:], 1.0)
            else:
                a0 = sf()
                a1 = sf()
                P.dma("sp", a0[:], lb_logits[0, d:d + 1, :].partition_broadcast(128), writes=[a0])
                P.dma("sp", a1[:], lb_logits[1, d:d + 1, :].partition_broadcast(128), writes=[a1])
                tt(a1[:], a1[:], a0[:], ALU.subtract, [a0, a1], [a1])
                act(lbrow[d][:], a1[:], AF.Sigmoid, [a1], [lbrow[d]])
                ts(omlrow[d][:], lbrow[d][:], -1.0, 1.0, ALU.mult, ALU.add, [lbrow[d]], [omlrow[d]])
                c0_ = sf()
                small_dma(c0_[:, 0:4], lb_logits[0, d].rearrange("(h p) -> p h", p=128), c0_)
                small_dma(c0_[:, 4:8], lb_logits[1, d].rearrange("(h p) -> p h", p=128), c0_)
                tt(c0_[:, 8:12], c0_[:, 0:4], c0_[:, 4:8], ALU.subtract, [c0_], [c0_])
                act(omlcol[d][:], c0_[:, 8:12], AF.Sigmoid, [c0_], [omlcol[d]])
        for j, g in enumerate((hgrn_norm, ret_norm, mlstm_norm)):
            small_dma(hn_col[j][:], g[l].rearrange("(h p) -> p h", p=128), hn_col[j])
        P.dma("sp", gb_row[:], gate_b[l:l + 1, :].partition_broadcast(128), writes=[gb_row])
        P.dma("sp", lgB[:], ret_decay[l:l + 1, :].partition_broadcast(128), writes=[lgB])
        act(lgB[:], lgB[:], AF.Exp, [lgB], [lgB])
        ts(lgB[:], lgB[:], -1.0, 1.0, ALU.mult, ALU.add, [lgB], [lgB])
        act(lgB[:], lgB[:], AF.Ln, [lgB], [lgB])
        ts(nlgB[:], lgB[:], -1.0, None, ALU.mult, None, [lgB], [nlgB])
        for cc_ in range(8):
            small_dma(cw[:, cc_, :], conv_w[l][:, :, cc_ * 128:(cc_ + 1) * 128].rearrange("a b p -> p (a b)"), cw)
        small_dma(cbias[:], conv_b[l].rearrange("(cc p) -> p cc", p=128), cbias)
        small_dma(s5d_col[:], s5_d[l].rearrange("(cc p) -> p cc", p=128), s5d_col)
        small_dma(glub_col[:], s5_glu_b[l].rearrange("(cc p) -> p cc", p=128), glub_col)
        P.dma("pool", gluw[:], s5_glu_w[l].rearrange("(k p) n -> p k n", p=128), writes=[gluw])
        P.barrier()

    def rstd_of(tile_x, junk):
        memset(ssq, ssq[:, 0:1], 0.0)
        act(junk[:], tile_x[:], AF.Square, [tile_x, ssq], [junk, ssq], accum_out=ssq[:, 0:1])
        act(ssq[:, 1:2], ssq[:, 0:1], AF.Ln, [ssq], [ssq], scale=1.0 / D, bias=EPS)
        act(ssq[:, 2:3], ssq[:, 1:2], AF.Exp, [ssq], [ssq], scale=-0.5)
        return ssq[:, 2:3]

    def norm_phase(which):
        P.arena_reset()
        xt = [P.ar([128, D], F32, f"xt{i}") for i in range(2)]
        xn = P.ar([128, D], F32, "xn")
        hT = [P.ar([128, 16, 128], BF16, f"hT{i}") for i in range(2)]
        junk = P.ar([128, D], BF16, "junk")
        m_shift, m_scale = (0, 1) if which == 0 else (3, 4)
        for v in range(2):
            ts(gcolA[:, v, :], modcol[:, v, m_scale, :], 1.0, None, ALU.add, None, [modcol], [gcolA])
            tt(gcolA[:, v, :], gcolA[:, v, :], ngc[which][:], ALU.mult, [gcolA, ngc[which]], [gcolA])
        for ti in range(NT // 128):
            v = 1 if ti < 2 else 0
            x_ = xt[ti % 2]
            P.dma("sp", x_[:], xres[ti * 128:(ti + 1) * 128, :], writes=[x_])
            r = rstd_of(x_, junk)
            ts(xn[:], x_[:], r, None, ALU.mult, None, [x_, ssq], [xn])
            h_ = hT[ti % 2]
            for j in range(16):
                pt, pv = next_ps()
                mm(pv[:, 0:128], xn[:, j * 128:(j + 1) * 128], ident[:], True, True, [xn, ident], [pt])
                act(h_[:, j, :], pv[:, 0:128], AF.Identity, [pt, gcolA, modcol], [h_],
                    scale=gcolA[:, v, j:j + 1], bias=modcol[:, v, m_shift, j:j + 1])
            P.dma("sp", hF[:, ti * 128:(ti + 1) * 128].rearrange("(j p) n -> p j n", p=128), h_[:], reads=[h_])
        P.barrier()

    def win_phase(l):
        jobs = []
        jobs.append((0, 512, "F", epiF_simple(FB["hq"], 0, AF.Silu)))

        def epi_hk(d):
            def epi(pt, pv, cg, cgn, tok0, tn):
                s = sf()
                act(s[:, 0:tn], pv[:, 0:tn], AF.Sigmoid, [pt], [s], scale=-1.0)
                sb_ = sbb()
                h = cg // 128
                ts(sb_[:, 0:tn], s[:, 0:tn], omlcol[d][:, h:h + 1], None, ALU.mult, None, [s, omlcol[d]], [sb_])
                P.dma("sp", FB[f"hkF{d}"][cg:cg + 128, tok0:tok0 + tn], sb_[:, 0:tn], reads=[sb_])
            return epi
        jobs.append((512, 512, "F", epi_hk(0)))
        jobs.append((1024, 512, "F", epi_hk(1)))
        jobs.append((2048, 512, "F", epiF_simple(FB["hog"], 0, AF.Silu)))
        jobs.append((2560, 512, "F", epiF_simple(FB["rq"], 0, AF.Identity)))
        jobs.append((3072, 512, "F", epiF_simple(FB["rkF"], 0, AF.Identity, scale=128 ** -0.5)))
        jobs.append((4096, 512, "F", epiF_simple(FB["rog"], 0, AF.Silu)))
        jobs.append((4608, 512, "F", epiF_simple(FB["uF"], 0, AF.Identity)))
        jobs.append((5120, 512, "F", epiF_simple(mqk, 0, AF.Identity, dtype=F32)))
        jobs.append((5632, 512, "F", epiF_simple(mqk, 512, AF.Identity, dtype=F32)))
        jobs.append((6656, 512, "F", epiF_simple(FB["mz"], 0, AF.Silu)))
        for gblk in range(16):
            jobs.append((7184 + gblk * 512, 512, "F", epiF_simple(gF, gblk * 512, AF.Sigmoid)))

        def epi_hf(d):
            def epi(pt, pv, tok0, tn):
                s = sf()
                act(s[0:tn, :], pv[0:tn, :], AF.Sigmoid, [pt], [s])
                tt(s[0:tn, :], s[0:tn, :], omlrow[d][0:tn, :], ALU.mult, [s, omlrow[d]], [s])
                tt(s[0:tn, :], s[0:tn, :], lbrow[d][0:tn, :], ALU.add, [s, lbrow[d]], [s])
                kb = sbb()
                ts(kb[0:tn, :], s[0:tn, :], -1.0, 1.0, ALU.mult, ALU.add, [s], [kb])
                P.dma("sp", KB[f"hkK{d}"][tok0:tok0 + tn, :], kb[0:tn, :], reads=[kb])
                ts(s[0:tn, :], s[0:tn, :], 1e-30, None, ALU.max, None, [s], [s])
                s2 = sf()
                act(s2[0:tn, :], s[0:tn, :], AF.Ln, [s], [s2])
                P.dma("sp", hlogf[d][tok0:tok0 + tn, :], s2[0:tn, :], reads=[s2])
            return epi
        jobs.append((512, 512, "K", epi_hf(0)))
        jobs.append((1024, 512, "K", epi_hf(1)))
        jobs.append((1536, 512, "K", epiK_simple(KB["hv"], AF.Identity)))
        jobs.append((3072, 512, "K", epiK_simple(KB["rkK"], AF.Identity, scale=128 ** -0.5)))
        jobs.append((3584, 512, "K", epiK_simple(KB["rv"], AF.Identity)))
        jobs.append((6144, 512, "K", epiK_simple(KB["mv"], AF.Identity)))

        def epi_mg(pt, pv, tok0, tn):
            s = sf()
            tt(s[0:tn, 0:16], pv[0:tn, 0:16], gb_row[0:tn, :], ALU.add, [pt, gb_row], [s])
            sv = s[0:tn, 0:16].rearrange("p (d i h) -> p d i h", d=2, i=2)
            s2 = sf()
            s2v = s2[0:tn, 0:16].rearrange("p (d i h) -> p d i h", d=2, i=2)
            act(s2v[:, :, 1, :], sv[:, :, 1, :], AF.Sigmoid, [s], [s2])
            act(sv[:, :, 1, :], s2v[:, :, 1, :], AF.Ln, [s2], [s])
            P.dma("sp", mg[tok0:tok0 + tn, :], s[0:tn, 0:16], reads=[s])
        jobs.append((7168, 16, "K", epi_mg))
        TB = 2176 if NT >= 2176 else NT
        gemm(hF, D, NT, w_in[l], jobs, TB)

    def conv_phase():
        P.arena_reset()
        cin = P.ar([128, NT], F32, "cin")
        cout = P.ar([128, NT], F32, "cout")
        cob = P.ar([128, NT], BF16, "cob")
        for cc in range(8):
            P.dma("sp", cin[:], mqk[cc * 128:(cc + 1) * 128, :], writes=[cin])

            def w(a, b, cc=cc):
                return cw[:, cc, a * 3 + b:a * 3 + b + 1]
            ts(cout[:], cin[:], w(1, 1), cbias[:, cc:cc + 1], ALU.mult, ALU.add, [cin, cw, cbias], [cout])

            def acc(o_ap, i_ap, wap):
                P.op("dve", lambda e: e.scalar_tensor_tensor(o_ap, i_ap, wap, o_ap, ALU.mult, ALU.add),
                     reads=[cin, cout, cw], writes=[cout])
            acc(cout[:, 1:CTX], cin[:, 0:CTX - 1], w(1, 0))
            acc(cout[:, 0:CTX - 1], cin[:, 1:CTX], w(1, 2))
            ci = cin[:, CTX:NT].rearrange("p (r c) -> p r c", c=64)
            co = cout[:, CTX:NT].rearrange("p (r c) -> p r c", c=64)
            for dr in (-1, 0, 1):
                for dc in (-1, 0, 1):
                    if dr == 0 and dc == 0:
                        continue
                    ro = slice(max(0, -dr), ROWS - max(0, dr))
                    ri = slice(max(0, dr), ROWS - max(0, -dr))
                    co_ = slice(max(0, -dc), 64 - max(0, dc))
                    ci_ = slice(max(0, dc), 64 - max(0, -dc))
                    acc(co[:, ro, co_], ci[:, ri, ci_], w(dr + 1, dc + 1))
            act(cob[:], cout[:], AF.Silu, [cout], [cob])
            if cc < 4:
                P.dma("sp", FB["mq"][cc * 128:(cc + 1) * 128, :], cob[:], reads=[cob])
            else:
                ts(cob[:], cob[:], 128 ** -0.5, None, ALU.mult, None, [cob], [cob])
                P.dma("sp", FB["mk"][(cc - 4) * 128:(cc - 3) * 128, :], cob[:], reads=[cob])
        P.barrier()

    def sc_order(d):
        if d == 0:
            return list(range(NSC))
        return [0] + list(range(NSC - 1, 0, -1))

    def cmul(or_, oi_, ar, ai, br, bi, t1, t2, reads, writes):
        tt(t1[1], ar, br, ALU.mult, reads, [t1[0]])
        tt(t2[1], ai, bi, ALU.mult, reads, [t2[0]])
        tt(or_, t1[1], t2[1], ALU.subtract, [t1[0], t2[0]], writes[0:1])
        tt(t1[1], ar, bi, ALU.mult, reads, [t1[0]])
        tt(t2[1], ai, br, ALU.mult, reads, [t2[0]])
        tt(oi_, t1[1], t2[1], ALU.add, [t1[0], t2[0]], writes[1:2])

    def scan_phase(l, d):
        P.arena_reset()
        tl = 63 if d == 0 else 0
        A = P.ar
        qF = A([128, 4, 256], BF16, "qF")
        kF = A([128, 4, 256], BF16, "kF")
        kK = A([64, 4, 512], BF16, "kK")
        vK = A([64, 4, 4, 129], BF16, "vK")
        lf = A([64, 4, 512], F32, "lf")
        g = A([64, 4, 16], F32, "gK")
        EpF = A([128, 4, 256], F32, "EpF")
        EmF = A([128, 4, 256], F32, "EmF")
        EkK = A([64, 4, 512], F32, "EkK")
        Ekm = A([64, 4, 4], F32, "Ekm")
        Et = A([128, 4, 4], F32, "Et")
        qp = A([128, 4, 256], BF16, "qp")
        kpF = A([128, 4, 256], BF16, "kpF")
        kpK = A([64, 4, 512], BF16, "kpK")
        PT = A([64, 16, 64], BF16, "PT")
        Sst = A([128, 4, 129], F32, "Sst")
        Sbf = A([128, 4, 129], BF16, "Sbf")
        nbc = A([128, 4, 128], BF16, "nbc")
        Otile = A([128, 4, 256], F32, "Otile")
        Dtile = A([128, 4, 256], F32, "Dtile")
        rawp = A([128, 4, 256], F32, "rawp")
        sq_t = A([128, 4, 256], F32, "sq_t")
        gate_t = A([128, 4, 256], BF16, "gate_t")
        o_bf = A([128, 4, 256], BF16, "o_bf")
        CPt = A([128, 4, 16], F32, "CPt")
        stri = A([64, 64], F32, "stri")
        rEp = A([128, 4, 256], F32, "rEp")
        rEm = A([128, 4, 256], F32, "rEm")
        rEk = A([64, 4, 512], F32, "rEk")
        rEt = A([128, 4, 4], F32, "rEt")
        qpb = A([128, 4, 256], BF16, "qpb", at=kpF.off)
        kpI = [A([128, 4, 256], BF16, "kpI0", at=rEp.off), A([128, 4, 256], BF16, "kpI1", at=rEp.off + 512),
               A([128, 4, 256], BF16, "kpI2", at=rEm.off), A([128, 4, 256], BF16, "kpI3", at=rEm.off + 512)]
        LPr = A([128, 16, 64], F32, "LPr")
        LPi = A([128, 16, 64], F32, "LPi")
        LMr = A([128, 16, 64], F32, "LMr")
        LMi = A([128, 16, 64], F32, "LMi")
        Bpad = A([128, 16, 2, 128], BF16, "Bpad")
        Cpad = A([128, 16, 2, 128], BF16, "Cpad")
        Zr = A([128, 16, 64], F32, "Zr")
        Zi = A([128, 16, 64], F32, "Zi")
        Wr = A([128, 16, 64], F32, "Wr")
        Wi = A([128, 16, 64], F32, "Wi")
        T1 = A([128, 16, 64], F32, "T1")
        T2 = A([128, 16, 64], F32, "T2")
        Xrb = A([128, 16, 64], BF16, "Xrb")
        Xib = A([128, 16, 64], BF16, "Xib")
        xpr = A([128, 16], F32, "xpr")
        xpi = A([128, 16], F32, "xpi")
        tot = A([128, 16], F32, "tot")
        u = A([128, 4, 256], BF16, "u")
        yc_b = A([128, 4, 256], BF16, "yc_b")
        sm = [A([128, 16], F32, f"sm{i}") for i in range(14)]
        smi = A([128, 16], I32, "smi")

        def ret_tables():
            for h in range(4):
                c_ = d * 4 + h
                for c in range(4):
                    act(rEp[:, h, c * 64:(c + 1) * 64], iotaF[d][:], AF.Exp, [iotaF[d], lgB], [rEp], scale=lgB[:, c_:c_ + 1])
                    act(rEm[:, h, c * 64:(c + 1) * 64], iotaF[d][:], AF.Exp, [iotaF[d], nlgB], [rEm], scale=nlgB[:, c_:c_ + 1])
                    act(rEk[:, c, h * 128:(h + 1) * 128], iotaK[d][:], AF.Exp, [iotaK[d], nlgB], [rEk], scale=nlgB[0:64, c_:c_ + 1])
                    cp(rEt[:, c, h:h + 1], rEp[:, h, tl:tl + 1], [rEp], [rEt])

        tt(stri[:], tri[1 - d][:], ident[0:64, 0:64], ALU.subtract, [tri[1 - d], ident], [stri])

        def sin_of(dst_t, theta_t, tmp):
            ts(tmp[:], theta_t[:], 1.0 / (2 * math.pi), None, ALU.mult, None, [theta_t], [tmp])
            cp(smi[:], tmp[:], [tmp], [smi])
            cp(tmp[:], smi[:], [smi], [tmp])
            P.op("dve", lambda e: e.scalar_tensor_tensor(tmp[:], tmp[:], -2 * math.pi, theta_t[:], ALU.mult, ALU.add),
                 reads=[tmp, theta_t], writes=[tmp])
            act(dst_t[:], tmp[:], AF.Sin, [tmp], [dst_t])

        def pow_table(Tr, Ti, br, bi, rev):
            c0 = 63 if rev else 0
            cp(Tr[:, :, c0], br[:], [br], [Tr])
            cp(Ti[:, :, c0], bi[:], [bi], [Ti])
            n = 1
            while n < 64:
                if not rev:
                    src, dst, top = slice(0, n), slice(n, 2 * n), n - 1
                else:
                    src, dst, top = slice(64 - n, 64), slice(64 - 2 * n, 64 - n), 64 - n
                fr_ = Tr[:, :, top:top + 1].to_broadcast([128, 16, n])
                fi_ = Ti[:, :, top:top + 1].to_broadcast([128, 16, n])
                cmul(Tr[:, :, dst], Ti[:, :, dst], Tr[:, :, src], Ti[:, :, src], fr_, fi_,
                     (T1, T1[:, :, 0:n]), (T2, T2[:, :, 0:n]), [Tr, Ti], [Tr, Ti])
                n *= 2

        are, aim, dt_, t0_, t1_, lre, lim, den, fr, fi, ire, iim, t2_, t3_ = sm
        for (tile_, src) in ((are, s5_a_re), (aim, s5_a_im)):
            v_ = src[l, d].rearrange("(j two) p -> two p j", two=2)
            for two in range(2):
                small_dma(tile_[64 * two:64 * two + 64, :], v_[two], tile_)
        ldt = s5_log_dt[l, d].rearrange("(j two) -> two j", two=2)
        for two in range(2):
            small_dma(dt_[64 * two:64 * two + 64, :], ldt[two:two + 1, :].partition_broadcast(64), dt_)
        act(dt_[:], dt_[:], AF.Exp, [dt_], [dt_])
        tt(t0_[:], are[:], dt_[:], ALU.mult, [are, dt_], [t0_])
        act(t0_[:], t0_[:], AF.Exp, [t0_], [t0_])
        tt(t1_[:], aim[:], dt_[:], ALU.mult, [aim, dt_], [t1_])
        sin_of(lim, t1_, t2_)
        ts(t3_[:], t1_[:], math.pi / 2, None, ALU.add, None, [t1_], [t3_])
        sin_of(lre, t3_, t2_)
        tt(lre[:], lre[:], t0_[:], ALU.mult, [lre, t0_], [lre])
        tt(lim[:], lim[:], t0_[:], ALU.mult, [lim, t0_], [lim])
        tt(den[:], are[:], are[:], ALU.mult, [are], [den])
        tt(t2_[:], aim[:], aim[:], ALU.mult, [aim], [t2_])
        tt(den[:], den[:], t2_[:], ALU.add, [den, t2_], [den])
        recip(den, den[:])
        ts(t0_[:], lre[:], -1.0, None, ALU.add, None, [lre], [t0_])
        tt(fr[:], t0_[:], are[:], ALU.mult, [t0_, are], [fr])
        tt(t2_[:], lim[:], aim[:], ALU.mult, [lim, aim], [t2_])
        tt(fr[:], fr[:], t2_[:], ALU.add, [fr, t2_], [fr])
        tt(fr[:], fr[:], den[:], ALU.mult, [fr, den], [fr])
        tt(fi[:], lim[:], are[:], ALU.mult, [lim, are], [fi])
        tt(t2_[:], t0_[:], aim[:], ALU.mult, [t0_, aim], [t2_])
        tt(fi[:], fi[:], t2_[:], ALU.subtract, [fi, t2_], [fi])
        tt(fi[:], fi[:], den[:], ALU.mult, [fi, den], [fi])
        tt(t0_[:], lre[:], lre[:], ALU.mult, [lre], [t0_])
        tt(t2_[:], lim[:], lim[:], ALU.mult, [lim], [t2_])
        tt(t0_[:], t0_[:], t2_[:], ALU.add, [t0_, t2_], [t0_])
        recip(t0_, t0_[:])
        tt(ire[:], lre[:], t0_[:], ALU.mult, [lre, t0_], [ire])
        tt(iim[:], lim[:], t0_[:], ALU.mult, [lim, t0_], [iim])
        ts(iim[:], iim[:], -1.0, None, ALU.mult, None, [iim], [iim])
        rev = (d == 1)
        pow_table(LPr, LPi, lre, lim, rev)
        pow_table(LMr, LMi, ire, iim, rev)
        bre = Tile(Zr.t[:, :, 0:16], "bre")
        bim = Tile(Zr.t[:, :, 16:32], "bim")
        bbr = Tile(Zr.t[:, :, 32:48], "bbr")
        bbi = Tile(Zr.t[:, :, 48:64], "bbi")
        tA = Tile(Zi.t[:, :, 0:16], "tA")
        tB = Tile(Zi.t[:, :, 16:32], "tB")
        bpadf = A([128, 16, 128], F32, "bpadf", at=Wr.off)
        assert Wi.off == Wr.off + 1024
        for (tile_, src) in ((bre, s5_b_re), (bim, s5_b_im)):
            v_ = src[l, d].rearrange("(j two) p h -> two p j h", two=2)
            for two in range(2):
                small_dma(tile_[64 * two:64 * two + 64, :, :], v_[two], tile_)
        frb = fr[:].unsqueeze(2).to_broadcast([128, 16, 16])
        fib = fi[:].unsqueeze(2).to_broadcast([128, 16, 16])
        cmul(bbr[:], bbi[:], bre[:], bim[:], frb, fib, (tA, tA[:]), (tB, tB[:]), [bre, bim, fr, fi], [bbr, bbi])
        for part, src_t in enumerate((bbr, bbi)):
            memset(bpadf, bpadf[:], 0.0)
            for half in range(2):
                for jj in range(4):
                    cp(bpadf[64 * half:64 * half + 64, jj::4, 32 * jj + 16 * half:32 * jj + 16 * half + 16],
                       src_t[64 * half:64 * half + 64, jj::4, :], [src_t], [bpadf])
            for j in range(16):
                pt, pv = next_ps()
                mm(pv[:, 0:128], bpadf[:, j, :], ident[:], True, True, [bpadf, ident], [pt])
                cp(Bpad[:, j, part, :], pv[:, 0:128], [pt], [Bpad])
        for part, src in enumerate((s5_c_re, s5_c_im)):
            for two in range(2):
                for j_ in range(16):
                    small_dma(bre[64 * two:64 * two + 64, j_, :], src[l, d, 2 * j_ + two].rearrange("h p -> p h"), bre)
            memset(bpadf, bpadf[:], 0.0)
            for half in range(2):
                for jj in range(4):
                    cp(bpadf[64 * half:64 * half + 64, jj::4, 32 * jj + 16 * half:32 * jj + 16 * half + 16],
                       bre[64 * half:64 * half + 64, jj::4, :], [bre], [bpadf])
            if part == 0:
                cp(Cpad[:, :, 0, :], bpadf[:], [bpadf], [Cpad])
            else:
                ts(Cpad[:, :, 1, :], bpadf[:], -1.0, None, ALU.mult, None, [bpadf], [Cpad])
        for al in (bre, bim, bbr, bbi):
            Zr.readers.update(al.readers); Zr.writers.update(al.writers)
        for al in (tA, tB):
            Zi.readers.update(al.readers); Zi.writers.update(al.writers)

        P.barrier()
        if stop_after == ("s_tab" + ("1" if d else ""), l):
            return True

        def gla_scan(mix, j_out, gcol):
            memset(Sst, Sst[:], 0.0)
            memset(Sbf, Sbf[:], 0.0)
            memset(nbc, nbc[:], 0.0)
            memset(vK, vK[:], 1.0)
            if mix == "h":
                qs, kFs, kKs, vs, gs = FB["hq"], FB[f"hkF{d}"], KB[f"hkK{d}"], KB["hv"], FB["hog"]
            elif mix == "r":
                qs, kFs, kKs, vs, gs = FB["rq"], FB["rkF"], KB["rkK"], KB["rv"], FB["rog"]
            else:
                qs, kFs, kKs, vs, gs = FB["mq"], FB["mk"], None, KB["mv"], FB["mz"]
            for sc in sc_order(d):
                n0 = sc * 256
                P.dma("sp", qF[:], qs[:, n0:n0 + 256].rearrange("(h p) n -> p h n", p=128), writes=[qF])
                P.dma("sp", kF[:], kFs[:, n0:n0 + 256].rearrange("(h p) n -> p h n", p=128), writes=[kF])
                vsrc = vs[n0:n0 + 256, :].rearrange("(c s) (h v) -> s c h v", s=64, v=128)
                for c in range(4):
                    P.dma("act", vK[:, c, :, 0:128], vsrc[:, c], writes=[vK])
                if kKs is not None:
                    P.dma("act", kK[:], kKs[n0:n0 + 256, :].rearrange("(c s) x -> s c x", s=64), writes=[kK])
                else:
                    for c in range(4):
                        pt, pv = next_ps()
                        for h in range(4):
                            mm(pv[0:64, h * 128:(h + 1) * 128], kF[:, h, c * 64:(c + 1) * 64], ident_b[:], True, True, [kF, ident_b], [pt])
                        act(kK[:, c, :], pv[0:64, :], AF.Identity, [pt], [kK])
                if mix == "r":
                    EP, EM, EK, ET = rEp, rEm, rEk, rEt
                    EK_ap = rEk[:]
                elif mix == "h":
                    P.dma("act", lf[:], hlogf[d][n0:n0 + 256, :].rearrange("(c s) x -> s c x", s=64), writes=[lf])
                    pe_ = PB[2]
                    for h in range(4):
                        for c in range(4):
                            mm(pe_[:, h * 256 + c * 64:h * 256 + c * 64 + 64], lf[:, c, h * 128:(h + 1) * 128], tri[d][:], True, True, [lf, tri[d]], [pe_])
                    act(EpF[:].rearrange("p h n -> p (h n)"), pe_[:], AF.Exp, [pe_], [EpF])
                    cumS = sq_t
                    act(cumS[:].rearrange("p h n -> p (h n)"), pe_[:], AF.Identity, [pe_], [cumS])
                    for cp_ in range(2):
                        pk = PB[3]
                        for c2_ in range(2):
                            c = cp_ * 2 + c2_
                            mm(pk[0:64, c2_ * 512:(c2_ + 1) * 512], stri[:], lf[:, c, :], True, True, [lf, stri], [pk])
                        act(EkK[:, cp_ * 2:cp_ * 2 + 2, :].rearrange("p c x -> p (c x)"), pk[0:64, :], AF.Exp, [pk], [EkK])
                    cp(Et[:].rearrange("p c h -> p h c"), EpF[:, :, tl::64], [EpF], [Et])
                    memset(CPt, CPt[:], 0.0)
                    c5 = cumS[:].rearrange("p h (c b t) -> p h c b t", c=4, b=4)
                    cpv = CPt[:].rearrange("p h (c b) -> p h c b", c=4)
                    if d == 0:
                        cp(cpv[:, :, :, 1:4], c5[:, :, :, 0:3, 15], [cumS], [CPt])
                    else:
                        cp(cpv[:, :, :, 0:3], c5[:, :, :, 1:4, 0], [cumS], [CPt])
                    Rt = Dtile
                    cp(Rt[:].rearrange("p h (k t) -> p h k t", t=16), CPt[:].unsqueeze(3).to_broadcast([128, 4, 16, 16]), [CPt], [Rt])
                    tt(Rt[:], cumS[:], Rt[:], ALU.subtract, [cumS, Rt], [Rt])
                    act(Rt[:], Rt[:], AF.Exp, [Rt], [Rt])
                    tt(qpb[:], qF[:], Rt[:], ALU.mult, [qF, Rt], [qpb])
                    for I in range(4):
                        tmpE = EmF
                        tt(tmpE[:].rearrange("p h (c t) -> p h c t", t=64),
                           cpv[:, :, :, I:I + 1].to_broadcast([128, 4, 4, 64]),
                           cumS[:].rearrange("p h (c t) -> p h c t", t=64), ALU.subtract, [CPt, cumS], [tmpE])
                        ts(tmpE[:], tmpE[:], 80.0, None, ALU.min, None, [tmpE], [tmpE])
                        act(tmpE[:], tmpE[:], AF.Exp, [tmpE], [tmpE])
                        tt(kpI[I][:], kF[:], tmpE[:], ALU.mult, [kF, tmpE], [kpI[I]])
                    EP, EM, EK, ET = EpF, EmF, EkK, Et
                    EK_ap = EkK[:]
                else:
                    P.dma("act", g[:], mg[n0:n0 + 256, :].rearrange("(c s) x -> s c x", s=64), writes=[g])
                    pe_ = PB[2]
                    pm_ = PB[3]
                    for h in range(4):
                        fcol = d * 8 + 4 + h
                        icol = d * 8 + h
                        for c in range(4):
                            sl = slice(h * 256 + c * 64, h * 256 + c * 64 + 64)
                            mm(pe_[:, sl], g[:, c, fcol:fcol + 1].to_broadcast([64, 128]), tri[d][:], True, True, [g, tri[d]], [pe_])
                            mm(pm_[:, sl], g[:, c, fcol:fcol + 1].to_broadcast([64, 128]), ntri[d][:], True, False, [g, ntri[d]], [pm_])
                            mm(pm_[:, sl], g[:, c, icol:icol + 1].to_broadcast([64, 128]), ident[0:64, 0:64], False, True, [g, ident], [pm_])
                    act(EpF[:].rearrange("p h n -> p (h n)"), pe_[:], AF.Exp, [pe_], [EpF])
                    act(EmF[:].rearrange("p h n -> p (h n)"), pm_[:], AF.Exp, [pm_], [EmF])
                    pt, pv = next_ps()
                    for c in range(4):
                        mm(pv[0:64, c * 16:(c + 1) * 16], tri[d][:], g[:, c, :], True, True, [g, tri[d]], [pt])
                    pvv = pv[0:64, 0:64].rearrange("p (c x) -> p c x", x=16)
                    tt(Ekm[:], g[:, :, d * 8:d * 8 + 4], pvv[:, :, d * 8 + 4:d * 8 + 8], ALU.subtract, [g, pt], [Ekm])
                    act(Ekm[:], Ekm[:], AF.Exp, [Ekm], [Ekm])
                    cp(Et[:].rearrange("p c h -> p h c"), EpF[:, :, tl::64], [EpF], [Et])
                    EP, EM, EK, ET = EpF, EmF, Ekm, Et
                    EK_ap = Ekm[:].unsqueeze(3).to_broadcast([64, 4, 4, 128])
                tt(qp[:], qF[:], EP[:], ALU.mult, [qF, EP], [qp])
                if mix != "h":
                    tt(kpF[:], kF[:], EM[:], ALU.mult, [kF, EM], [kpF])
                if mix == "m":
                    tt(kpK[:].rearrange("p c (h x) -> p c h x", x=128), kK[:].rearrange("p c (h x) -> p c h x", x=128), EK_ap, ALU.mult, [kK, EK], [kpK])
                else:
                    tt(kpK[:], kK[:], EK_ap, ALU.mult, [kK, EK], [kpK])
                pa = PB[0]
                for c in range(4):
                    for h in range(4):
                        blk = c * 4 + h
                        if mix == "h":
                            for I in range(4):
                                mm(pa[0:64, blk * 64 + 16 * I:blk * 64 + 16 * I + 16], kpI[I][:, h, c * 64:(c + 1) * 64],
                                   qpb[:, h, c * 64 + 16 * I:c * 64 + 16 * I + 16], True, True, [kpI[I], qpb], [pa])
                        else:
                            mm(pa[0:64, blk * 64:(blk + 1) * 64], kpF[:, h, c * 64:(c + 1) * 64], qp[:, h, c * 64:(c + 1) * 64], True, True, [kpF, qp], [pa])
                tt(PT[:], pa[0:64, :].rearrange("p (b t) -> p b t", t=64), tri[d][:].unsqueeze(1).to_broadcast([64, 16, 64]), ALU.mult, [pa, tri[d]], [PT])
                po = PB[1]
                pd = PB[2]
                corder = range(4) if d == 0 else range(3, -1, -1)
                for c in corder:
                    for h in range(4):
                        sl = slice(h * 256 + c * 64, h * 256 + c * 64 + 64)
                        mm(po[:, sl], vK[:, c, h, 0:128], PT[:, c * 4 + h, :], True, False, [vK, PT], [po])
                        mm(po[:, sl], Sbf[:, h, 0:128], qp[:, h, c * 64:(c + 1) * 64], False, True, [Sbf, qp], [po])
                        if mix == "m":
                            mm(pd[:, sl], ones_b[0:64, :], PT[:, c * 4 + h, :], True, False, [ones_b, PT], [pd])
                            mm(pd[:, sl], nbc[:, h, :], qp[:, h, c * 64:(c + 1) * 64], False, True, [nbc, qp], [pd])
                    pu = PB[3]
                    for h in range(4):
                        mm(pu[:, h * 256:h * 256 + 129], kpK[:, c, h * 128:(h + 1) * 128], vK[:, c, h, :], True, True, [kpK, vK], [pu])
                    puv = pu[:].rearrange("p (h x) -> p h x", x=256)[:, :, 0:129]
                    if mix == "h":
                        tt(Sst[:], Sst[:], ET[:, c, :].unsqueeze(2).to_broadcast([128, 4, 129]), ALU.mult, [Sst, ET], [Sst])
                        tt(Sst[:], Sst[:], puv, ALU.add, [Sst, pu], [Sst])
                    else:
                        tt(Sst[:], Sst[:], puv, ALU.add, [Sst, pu], [Sst])
                        tt(Sst[:], Sst[:], ET[:, c, :].unsqueeze(2).to_broadcast([128, 4, 129]), ALU.mult, [Sst, ET], [Sst])
                    act(Sbf[:], Sst[:], AF.Identity, [Sst], [Sbf])
                    if mix == "m":
                        cp(nbc[:], Sst[:, :, 128:129].to_broadcast([128, 4, 128]), [Sst], [nbc])
                Ov = Otile[:].rearrange("p h n -> p (h n)")
                if mix == "m":
                    act(Dtile[:].rearrange("p h n -> p (h n)"), pd[:], AF.Abs, [pd], [Dtile])
                    ts(Dtile[:], Dtile[:], 1.0, None, ALU.max, None, [Dtile], [Dtile])
                    recip(Dtile, Dtile[:])
                    tt(Ov, po[:], Dtile[:].rearrange("p h n -> p (h n)"), ALU.mult, [po, Dtile], [Otile])
                else:
                    act(Ov, po[:], AF.Identity, [po], [Otile])
                dst = rawf[j_out][:, n0:n0 + 256].rearrange("(h p) n -> p h n", p=128)
                if d == 0:
                    P.dma("sp", dst, Otile[:], reads=[Otile])
                else:
                    P.dma("sp", rawp[:], dst, writes=[rawp])
                    P.dma("sp", gate_t[:], gs[:, n0:n0 + 256].rearrange("(h p) n -> p h n", p=128), writes=[gate_t])
                    tt(rawp[:], rawp[:], Otile[:], ALU.add, [rawp, Otile], [rawp])
                    tt(sq_t[:], rawp[:], rawp[:], ALU.mult, [rawp], [sq_t])
                    pss = PB[0]
                    sqv = sq_t[:].rearrange("p h n -> p (h n)")
                    for hf in range(2):
                        mm(pss[:, hf * 512:(hf + 1) * 512], ones_f[:], sqv[:, hf * 512:(hf + 1) * 512], True, True, [ones_f, sq_t], [pss])
                    act(sqv, pss[:], AF.Ln, [pss], [sq_t], scale=1.0 / 128, bias=EPS)
                    act(sqv, sqv, AF.Exp, [sq_t], [sq_t], scale=-0.5)
                    tt(rawp[:], rawp[:], sq_t[:], ALU.mult, [rawp, sq_t], [rawp])
                    tt(rawp[:], rawp[:], gcol[:].unsqueeze(2).to_broadcast([128, 4, 256]), ALU.mult, [rawp, gcol], [rawp])
                    tt(o_bf[:], rawp[:], gate_t[:], ALU.mult, [rawp, gate_t], [o_bf])
                    P.dma("sp", oF[j_out * BR:(j_out + 1) * BR, n0:n0 + 256].rearrange("(h p) n -> p h n", p=128), o_bf[:], reads=[o_bf])

        def s5_scan():
            memset(xpr, xpr[:], 0.0)
            memset(xpi, xpi[:], 0.0)
            Xr, Xi = Zr, Zi
            for sc in sc_order(d):
                n0 = sc * 256
                P.dma("sp", u[:], FB["uF"][:, n0:n0 + 256].rearrange("(cc p) n -> p cc n", p=128), writes=[u])
                py = PB[1]
                corder = range(4) if d == 0 else range(3, -1, -1)
                for c in corder:
                    pbr, pbi = PB[2], PB[3]
                    for j in range(16):
                        cc = j // 4
                        mm(pbr[:, j * 64:(j + 1) * 64], Bpad[:, j, 0, :], u[:, cc, c * 64:(c + 1) * 64], True, True, [Bpad, u], [pbr])
                        mm(pbi[:, j * 64:(j + 1) * 64], Bpad[:, j, 1, :], u[:, cc, c * 64:(c + 1) * 64], True, True, [Bpad, u], [pbi])
                    bur = pbr[:].rearrange("p (j t) -> p j t", t=64)
                    bui = pbi[:].rearrange("p (j t) -> p j t", t=64)
                    cmul(Zr[:], Zi[:], bur, bui, LMr[:], LMi[:], (T1, T1[:]), (T2, T2[:]), [pbr, pbi, LMr, LMi], [Zr, Zi])
                    for (W_, Z_) in ((Wr, Zr), (Wi, Zi)):
                        P.op("dve", lambda e, W_=W_, Z_=Z_: e.tensor_tensor_scan(
                            W_[:].rearrange("p j t -> p (j t)"), rst[:].rearrange("p j t -> p (j t)"),
                            Z_[:].rearrange("p j t -> p (j t)"), 0.0, ALU.mult, ALU.add), reads=[rst, Z_], writes=[W_])
                        if d == 1:
                            cp(tot[:], W_[:, :, 63], [W_], [tot])
                            tt(W_[:], Z_[:], W_[:], ALU.subtract, [Z_, W_], [W_])
                            tt(W_[:], W_[:], tot[:].unsqueeze(2).to_broadcast([128, 16, 64]), ALU.add, [W_, tot], [W_])
                    tt(Wr[:], Wr[:], xpr[:].unsqueeze(2).to_broadcast([128, 16, 64]), ALU.add, [Wr, xpr], [Wr])
                    tt(Wi[:], Wi[:], xpi[:].unsqueeze(2).to_broadcast([128, 16, 64]), ALU.add, [Wi, xpi], [Wi])
                    cmul(Xr[:], Xi[:], Wr[:], Wi[:], LPr[:], LPi[:], (T1, T1[:]), (T2, T2[:]), [Wr, Wi, LPr, LPi], [Xr, Xi])
                    cp(xpr[:], Xr[:, :, tl], [Xr], [xpr])
                    cp(xpi[:], Xi[:, :, tl], [Xi], [xpi])
                    act(Xrb[:], Xr[:], AF.Identity, [Xr], [Xrb])
                    act(Xib[:], Xi[:], AF.Identity, [Xi], [Xib])
                    for cc in range(4):
                        sl = slice(cc * 256 + c * 64, cc * 256 + c * 64 + 64)
                        k_ = 0
                        for jj in range(4):
                            j = cc * 4 + jj
                            for part, X_ in enumerate((Xrb, Xib)):
                                mm(py[:, sl], Cpad[:, j, part, :], X_[:, j, :], k_ == 0, k_ == 7, [Cpad, X_], [py])
                                k_ += 1
                Ov = Otile[:].rearrange("p h n -> p (h n)")
                act(Ov, py[:], AF.Identity, [py], [Otile])
                dst = rawf[2][:, n0:n0 + 256].rearrange("(h p) n -> p h n", p=128)
                if d == 0:
                    P.dma("sp", dst, Otile[:], reads=[Otile])
                else:
                    P.dma("sp", rawp[:], dst, writes=[rawp])
                    tt(rawp[:], rawp[:], Otile[:], ALU.add, [rawp, Otile], [rawp])
                    cp(Dtile[:], u[:], [u], [Dtile])
                    tt(Dtile[:], Dtile[:], s5d_col[:].unsqueeze(2).to_broadcast([128, 4, 256]), ALU.mult, [Dtile, s5d_col], [Dtile])
                    tt(rawp[:], rawp[:], Dtile[:], ALU.add, [rawp, Dtile], [rawp])
                    act(rawp[:], rawp[:], AF.Gelu, [rawp], [rawp])
                    cp(yc_b[:], rawp[:], [rawp], [yc_b])
                    pg = PB[0]
                    for cco in range(4):
                        for cci in range(4):
                            mm(pg[:, cco * 256:(cco + 1) * 256], gluw[:, cci, cco * 128:(cco + 1) * 128], yc_b[:, cci, :], cci == 0, cci == 3, [gluw, yc_b], [pg])
                    for cco in range(4):
                        act(sq_t[:, cco, :], pg[:, cco * 256:(cco + 1) * 256], AF.Sigmoid, [pg, glub_col], [sq_t], bias=glub_col[:, cco:cco + 1])
                    tt(o_bf[:], rawp[:], sq_t[:], ALU.mult, [rawp, sq_t], [o_bf])
                    P.dma("sp", oF[2 * BR:3 * BR, n0:n0 + 256].rearrange("(h p) n -> p h n", p=128), o_bf[:], reads=[o_bf])

        gla_scan("h", 0, hn_col[0])
        if stop_after == ("s_h" + ("1" if d else ""), l):
            P.barrier()
            return True
        P.barrier()
        ret_tables()
        gla_scan("r", 1, hn_col[1])
        if stop_after == ("s_r" + ("1" if d else ""), l):
            P.barrier()
            return True
        s5_scan()
        if stop_after == ("s_s5" + ("1" if d else ""), l):
            P.barrier()
            return True
        gla_scan("m", 3, hn_col[2])
        P.barrier()
        return False

    def merge_phase(l):
        for j in range(4):
            def epi_cb(cb, j=j):
                def e_(pt, pv, cg, cgn, tok0, tn):
                    row = cb * 512 + cg
                    gt = sbb()
                    P.dma("act", gt[:, 0:tn], gF[j * D + row:j * D + row + 128, tok0:tok0 + tn], writes=[gt])
                    s = sf()
                    tt(s[:, 0:tn], pv[:, 0:tn], gt[:, 0:tn], ALU.mult, [pt, gt], [s])
                    dsl = yaccd[row:row + 128, tok0:tok0 + tn]
                    if j > 0:
                        s2 = sf()
                        P.dma("act", s2[:, 0:tn], dsl, writes=[s2])
                        tt(s[:, 0:tn], s[:, 0:tn], s2[:, 0:tn], ALU.add, [s, s2], [s])
                    if j < 3:
                        P.dma("sp", dsl, s[:, 0:tn], reads=[s])
                    else:
                        sb_ = sbb()
                        cp(sb_[:, 0:tn], s[:, 0:tn], [s], [sb_])
                        P.dma("sp", yF[row:row + 128, tok0:tok0 + tn], sb_[:, 0:tn], reads=[sb_])
                return e_
            jobs = [(cb * 512, 512, "F", epi_cb(cb)) for cb in range(4)]
            gemm(oF[j * BR:(j + 1) * BR, :], BR, NT, w_branch[l, j], jobs, 4352 if NT >= 4352 else NT)

    def resid_epi(gate_m, c0, ncols):
        def e_(pt, pv, tok0, tn):
            v = 1 if tok0 < CTX else 0
            gr = sf()
            P.dma("act", gr[:, 0:ncols], mod_d[v:v + 1, gate_m * D + c0:gate_m * D + c0 + ncols].partition_broadcast(128), writes=[gr])
            s = sf()
            tt(s[0:tn, 0:ncols], pv[0:tn, 0:ncols], gr[0:tn, 0:ncols], ALU.mult, [pt, gr], [s])
            s2 = sf()
            P.dma("act", s2[0:tn, 0:ncols], xres[tok0:tok0 + tn, c0:c0 + ncols], writes=[s2])
            tt(s[0:tn, 0:ncols], s[0:tn, 0:ncols], s2[0:tn, 0:ncols], ALU.add, [s, s2], [s])
            P.dma("sp", xres[tok0:tok0 + tn, c0:c0 + ncols], s[0:tn, 0:ncols], reads=[s])
        return e_

    def wout_phase(l):
        jobs = [(c0, 512, "K", resid_epi(2, c0, 512)) for c0 in range(0, D, 512)]
        TB = 2176 if NT >= 2176 else NT
        gemm(yF, D, NT, w_out[l], jobs, TB)

    def ffn_phase(l):
        def epi1(row0):
            def e_(pt, pv, cg, cgn, tok0, tn):
                s = sf()
                act(s[:, 0:tn], pv[:, 0:tn], AF.Relu, [pt], [s])
                sb_ = sbb()
                tt(sb_[:, 0:tn], s[:, 0:tn], s[:, 0:tn], ALU.mult, [s], [sb_])
                P.dma("sp", aF[row0 + cg:row0 + cg + 128, tok0:tok0 + tn], sb_[:, 0:tn], reads=[sb_])
            return e_
        jobs = [(c0, 512, "F", epi1(c0)) for c0 in range(0, DFF, 512)]
        TB = 2176 if NT >= 2176 else NT
        gemm(hF, D, NT, w_ff1[l], jobs, TB)
        jobs = [(c0, 128, "K", resid_epi(5, c0, 128)) for c0 in range(0, D, 128)]
        gemm(aF, DFF, NT, w_ff2[l], jobs, 512)

    def final_phase():
        P.arena_reset()
        fn = P.ar([128, D], F32, "fn")
        xt = [P.ar([128, D], F32, f"fxt{i}") for i in range(2)]
        xn = [P.ar([128, D], F32, f"fxn{i}") for i in range(2)]
        junk = P.ar([128, D], BF16, "fjunk")
        P.dma("sp", fn[:], final_norm.partition_broadcast(128), writes=[fn])
        for ti in range(2, NT // 128):
            x_ = xt[ti % 2]
            xn_ = xn[ti % 2]
            P.dma("sp", x_[:], xres[ti * 128:(ti + 1) * 128, :], writes=[x_])
            r = rstd_of(x_, junk)
            ts(xn_[:], x_[:], r, None, ALU.mult, None, [x_, ssq], [xn_])
            tt(xn_[:], xn_[:], fn[:], ALU.mult, [xn_, fn], [xn_])
            P.dma("sp", out[(ti - 2) * 128:(ti - 1) * 128, :], xn_[:], reads=[xn_])

    def program():
        for l in range(2):
            layer_setup(l)
            if stop_after == ("setup", l):
                return
            norm_phase(0)
            if stop_after == ("norm", l):
                return
            win_phase(l)
            if stop_after == ("win", l):
                return
            conv_phase()
            if stop_after == ("conv", l):
                return
            for d in range(2):
                if scan_phase(l, d):
                    return
                if stop_after == ("s_d0", l):
                    return
            if stop_after == ("scan", l):
                return
            merge_phase(l)
            if stop_after == ("merge", l):
                return
            wout_phase(l)
            if stop_after == ("wout", l):
                return
            norm_phase(1)
            ffn_phase(l)
            if stop_after == ("ffn", l):
                return
        final_phase()

    program()
    P.barrier()
    for name in dump:
        t, shape, dtype = SCR[name]
        o = nc.dram_tensor("dbg_" + name, list(shape), dtype, kind="ExternalOutput").ap()
        P.dma("sp", o, t)
    P.finish()
    return nc


_NC_CACHE = {}


def kernel(**inputs):
    x = np.asarray(inputs["x"], np.float32)
    B, NLAT, _ = x.shape
    ctx = np.asarray(inputs["ctx"], np.float32)
    c = np.asarray(inputs["c"], np.float32)
    c_ctx = np.asarray(inputs["c_ctx"], np.float32)
    if NLAT not in _NC_CACHE:
        _NC_CACHE[NLAT] = build(NLAT)
    nc = _NC_CACHE[NLAT]
    consts = host_consts()
    shared = {}
    for k in ("w_mod", "b_mod", "norm_mix", "norm_mlp", "w_in", "hgrn_lb_logits", "hgrn_norm", "ret_decay",
              "ret_norm", "s5_a_re", "s5_a_im", "s5_log_dt", "s5_b_re", "s5_b_im", "s5_c_re", "s5_c_im", "s5_d",
              "s5_glu_w", "s5_glu_b", "mlstm_conv_w", "mlstm_conv_b", "mlstm_norm", "w_branch", "w_out",
              "w_ff1", "w_ff2"):
        shared[k] = np.ascontiguousarray(np.asarray(inputs[k], np.float32))
    shared["ret_decay"] = np.ascontiguousarray(np.asarray(inputs["ret_decay"], np.float32).reshape(2, 8))
    shared["mlstm_gate_b"] = np.ascontiguousarray(np.asarray(inputs["mlstm_gate_b"], np.float32).reshape(2, 16))
    shared["final_norm"] = np.ascontiguousarray(np.asarray(inputs["final_norm"], np.float32).reshape(1, D))
    shared.update(consts)
    in_maps = []
    for core in range(8):
        b = core % B
        m = dict(shared)
        m["xin"] = np.ascontiguousarray(np.concatenate([ctx[b], x[b]], axis=0))
        m["c2"] = np.ascontiguousarray(np.stack([c[b], c_ctx]))
        in_maps.append(m)
    res = run_bass_kernel_spmd(nc, in_maps, core_ids=list(range(8)))
    outs = [np.asarray(res.results[b]["out"], np.float32) for b in range(B)]
    return np.stack(outs, axis=0)
```

```python
import contextlib
import math
import numpy as np
import concourse.bass as bass
import concourse.mybir as mybir
from concourse.bass_utils import run_bass_kernel_spmd

F32 = mybir.dt.float32
BF16 = mybir.dt.bfloat16
I32 = mybir.dt.int32
AF = mybir.ActivationFunctionType
ALU = mybir.AluOpType

ENGS = ("pe", "act", "dve", "pool", "sp")
KSEM = 8
NDQ = 12
SAME_ENGINE_WAITS = True

D = 2048
BR = 512
DFF = 8192
INW = 15376
CTX = 256
EPS = 1e-6


class Tile:
    def __init__(self, t, name=""):
        self.t = t
        self.name = name
        self.writers = {}
        self.readers = {}

    def __getitem__(self, k):
        return self.t[k]


class Prog:
    def __init__(self, nc):
        self.nc = nc
        self.es = contextlib.ExitStack()
        self.streams = {e: [] for e in ENGS}
        self.count = {e: 0 for e in ENGS}
        self.sems = {}
        for e in ENGS:
            self.sems[e] = [self.es.enter_context(nc.semaphore(f"s_{e}{k}")) for k in range(KSEM)]
        self.dq = {}
        for q in ("sp", "pool", "act"):
            self.dq[q] = [self.es.enter_context(nc.semaphore(f"d_{q}{k}")) for k in range(NDQ)]
        self.dq_count = {q: [0] * NDQ for q in self.dq}
        self.dq_next = {q: 0 for q in self.dq}
        self.seen = {e: {} for e in ENGS}
        self.n_sb = 0

    def sb(self, shape, dtype, name=None):
        self.n_sb += 1
        name = "sb_" + (name or f"t{self.n_sb}")
        t = self.es.enter_context(self.nc.sbuf_tensor(name, list(shape), dtype))
        return Tile(t, name)

    def arena_init(self, words):
        self.arena = self.es.enter_context(self.nc.sbuf_tensor("arena", [128, words], F32))
        self.arena_words = words
        self.arena_off = 0

    def arena_reset(self):
        self.arena_off = 0

    def ar(self, shape, dtype, name=None, at=None):
        esz = 4 if dtype in (F32, I32) else 2
        nfree = 1
        for d_ in shape[1:]:
            nfree *= d_
        words = (nfree * esz + 31) // 32 * 8
        if at is None:
            assert self.arena_off + words <= self.arena_words, (name, self.arena_off, words)
            off = self.arena_off
            self.arena_off += words
        else:
            off = at
        v = self.arena[0:shape[0], off:off + words]
        if dtype != F32:
            v = v.bitcast(dtype)
        v = v[:, 0:nfree]
        if len(shape) == 3:
            v = v.rearrange("p (a b) -> p a b", a=shape[1])
        elif len(shape) == 4:
            v = v.rearrange("p (a b c) -> p a b c", a=shape[1], b=shape[2])
        tl_ = Tile(v, name or "ar")
        tl_.off, tl_.words = off, words
        return tl_

    def ps(self, shape, dtype=F32, name=None):
        self.n_sb += 1
        name = name or f"ps{self.n_sb}"
        t = self.es.enter_context(self.nc.psum_tensor(name, list(shape), dtype))
        return Tile(t, name)

    def dram(self, name, shape, dtype):
        t = self.nc.dram_tensor(name, list(shape), dtype)
        return Tile(t.ap(), name)

    def _sem_of(self, stream, idx):
        if stream in ENGS:
            return self.sems[stream][idx % KSEM], idx // KSEM + 1
        q, k = stream
        return self.dq[q][k], 16 * (idx + 1)

    def _need(self, eng, stream, idx, is_dma=False):
        if stream == eng and (eng == "pe" or not (SAME_ENGINE_WAITS or is_dma)):
            return
        if self.seen[eng].get(stream, -1) >= idx:
            return
        self.seen[eng][stream] = idx
        sem, val = self._sem_of(stream, idx)
        self.streams[eng].append(lambda e, sem=sem, val=val: e.wait_ge(sem, val))

    def _deps(self, eng, reads, writes, is_dma=False):
        for t in reads:
            for s, i in t.writers.items():
                self._need(eng, s, i, is_dma)
        for t in writes:
            for s, i in t.writers.items():
                self._need(eng, s, i, is_dma)
            for s, i in t.readers.items():
                self._need(eng, s, i, is_dma)

    def op(self, eng, fn, reads=(), writes=()):
        self._deps(eng, reads, writes)
        idx = self.count[eng]
        self.count[eng] += 1
        sem = self.sems[eng][idx % KSEM]
        self.streams[eng].append(lambda e, fn=fn, sem=sem: fn(e).then_inc(sem, 1))
        for t in reads:
            t.readers[eng] = idx
        for t in writes:
            t.writers = {eng: idx}
            t.readers = {}
        return idx

    def dma(self, q, out, in_, reads=(), writes=(), **kw):
        self._deps(q, reads, writes, True)
        k = self.dq_next[q]
        self.dq_next[q] = (k + 1) % NDQ
        j = self.dq_count[q][k]
        if j > 0:
            self._need(q, (q, k), j - 1)
        self.dq_count[q][k] += 1
        sem = self.dq[q][k]
        self.streams[q].append(
            lambda e, sem=sem, out=out, in_=in_, kw=kw: e.dma_start(out=out, in_=in_, **kw).then_inc(sem, 16))
        st = (q, k)
        for t in reads:
            t.readers[st] = j
        for t in writes:
            t.writers = {st: j}
            t.readers = {}

    def barrier(self):
        for e in ENGS:
            for s in ENGS:
                if s != e and self.count[s] > 0:
                    self._need(e, s, self.count[s] - 1)
            for q in self.dq:
                for k in range(NDQ):
                    if self.dq_count[q][k] > 0:
                        self._need(e, (q, k), self.dq_count[q][k] - 1)

    def finish(self):
        self.barrier()
        nc = self.nc
        with nc.Block() as block:
            @block.tensor
            def _(e):
                for f in self.streams["pe"]:
                    f(e)

            @block.scalar
            def _(e):
                for f in self.streams["act"]:
                    f(e)

            @block.vector
            def _(e):
                for f in self.streams["dve"]:
                    f(e)

            @block.gpsimd
            def _(e):
                for f in self.streams["pool"]:
                    f(e)

            @block.sync
            def _(e):
                for f in self.streams["sp"]:
                    f(e)
        self.es.close()


def host_consts():
    t = np.arange(64, dtype=np.float32)
    c = {}
    c["ident"] = np.eye(128, dtype=np.float32)
    triU = (t[:, None] <= t[None, :]).astype(np.float32)
    triL = (t[:, None] >= t[None, :]).astype(np.float32)
    c["tri"] = np.stack([triU, triL])
    c["iotaF"] = np.stack([np.tile(t + 1, (128, 1)), np.tile(64 - t, (128, 1))]).astype(np.float32)
    c["iotaK"] = np.stack([np.tile((t + 1)[:, None], (1, 128)), np.tile((64 - t)[:, None], (1, 128))]).astype(np.float32)
    rst = np.ones((128, 16, 64), np.float32)
    rst[:, :, 0] = 0.0
    c["rst"] = rst
    return c


def build(NLAT, dump=(), stop_after=None):
    nc = bass.Bass("TRN2", target_bir_lowering=False)
    P = Prog(nc)
    NT = CTX + NLAT
    NSC = NT // 256
    ROWS = NLAT // 64

    def din(name, shape):
        return nc.dram_tensor(name, list(shape), F32, kind="ExternalInput").ap()

    xin = din("xin", [NT, D])
    c2 = din("c2", [2, D])
    w_mod = din("w_mod", [2, D, 6 * D])
    b_mod = din("b_mod", [2, 6 * D])
    norm_mix = din("norm_mix", [2, D])
    norm_mlp = din("norm_mlp", [2, D])
    w_in = din("w_in", [2, D, INW])
    lb_logits = din("hgrn_lb_logits", [2, 2, BR])
    hgrn_norm = din("hgrn_norm", [2, BR])
    ret_decay = din("ret_decay", [2, 8])
    ret_norm = din("ret_norm", [2, BR])
    s5_a_re = din("s5_a_re", [2, 2, 32, 64])
    s5_a_im = din("s5_a_im", [2, 2, 32, 64])
    s5_log_dt = din("s5_log_dt", [2, 2, 32])
    s5_b_re = din("s5_b_re", [2, 2, 32, 64, 16])
    s5_b_im = din("s5_b_im", [2, 2, 32, 64, 16])
    s5_c_re = din("s5_c_re", [2, 2, 32, 16, 64])
    s5_c_im = din("s5_c_im", [2, 2, 32, 16, 64])
    s5_d = din("s5_d", [2, BR])
    s5_glu_w = din("s5_glu_w", [2, BR, BR])
    s5_glu_b = din("s5_glu_b", [2, BR])
    conv_w = din("mlstm_conv_w", [2, 3, 3, 2 * BR])
    conv_b = din("mlstm_conv_b", [2, 2 * BR])
    gate_b = din("mlstm_gate_b", [2, 16])
    mlstm_norm = din("mlstm_norm", [2, BR])
    w_branch = din("w_branch", [2, 4, BR, D])
    w_out = din("w_out", [2, D, D])
    w_ff1 = din("w_ff1", [2, D, DFF])
    w_ff2 = din("w_ff2", [2, DFF, D])
    final_norm = din("final_norm", [1, D])
    k_ident = din("ident", [128, 128])
    k_tri = din("tri", [2, 64, 64])
    k_iotaF = din("iotaF", [2, 128, 64])
    k_iotaK = din("iotaK", [2, 64, 128])
    k_rst = din("rst", [128, 16, 64])
    out = nc.dram_tensor("out", [NLAT, D], F32, kind="ExternalOutput").ap()

    SCR = {}

    def scratch(name, shape, dtype):
        t = nc.dram_tensor(name, list(shape), dtype).ap()
        SCR[name] = (t, shape, dtype)
        return t

    xres = scratch("xres", [NT, D], F32)
    hF = scratch("hF", [D, NT], BF16)
    scF = scratch("scF", [D, 2], BF16)
    mod_d = scratch("mod_d", [2, 6 * D], F32)
    FB = {}
    for nm in ("hq", "hkF0", "hkF1", "hog", "rq", "rkF", "rog", "uF", "mz", "mq", "mk"):
        FB[nm] = scratch(nm, [BR, NT], BF16)
    mqk = scratch("mqk", [2 * BR, NT], F32)
    gF = scratch("gF", [4 * D, NT], BF16)
    KB = {}
    for nm in ("hkK0", "hkK1", "hv", "rkK", "rv", "mv"):
        KB[nm] = scratch(nm, [NT, BR], BF16)
    hlogf = [scratch("hlogf0", [NT, BR], F32), scratch("hlogf1", [NT, BR], F32)]
    mg = scratch("mg", [NT, 16], F32)
    rawf = [scratch(f"rawf{j}", [BR, NT], F32) for j in range(4)]
    oF = scratch("oF", [4 * BR, NT], BF16)
    yF = scratch("yF", [D, NT], BF16)
    yaccd = scratch("yaccd", [D, NT], F32)
    aF = scratch("aF", [DFF, NT], BF16)

    PB = [P.ps([128, 1024], F32, name=f"pb{i}") for i in range(4)]
    slot_order = [0, 2, 4, 6, 1, 3, 5, 7]
    st = {"slot": 0, "w": 0, "sf": 0, "sb": 0}

    def next_ps():
        s = slot_order[st["slot"] % 8]
        st["slot"] += 1
        tl = PB[s // 2]
        return tl, tl.t[:, (s % 2) * 512:(s % 2) * 512 + 512]

    stg_f = [P.sb([128, 512], F32, f"stgf{i}") for i in range(4)]
    stg_b = [P.sb([128, 512], BF16, f"stgb{i}") for i in range(4)]

    def sf():
        st["sf"] += 1
        return stg_f[st["sf"] % 4]

    def sbb():
        st["sb"] += 1
        return stg_b[st["sb"] % 4]

    ident = P.sb([128, 128], F32, "ident")
    ident_b = P.sb([128, 128], BF16, "identb")
    ones_f = P.sb([128, 128], F32, "onesf")
    ones_b = P.sb([128, 128], BF16, "onesb")
    tri = [P.sb([64, 64], F32, f"tri{d}") for d in range(2)]
    ntri = [P.sb([64, 64], F32, f"ntri{d}") for d in range(2)]
    iotaF = [P.sb([128, 64], F32, f"iotaF{d}") for d in range(2)]
    iotaK = [P.sb([64, 128], F32, f"iotaK{d}") for d in range(2)]
    rst = P.sb([128, 16, 64], F32, "rst")
    modcol = P.sb([128, 2, 6, 16], F32, "modcol")
    gcolA = P.sb([128, 2, 16], F32, "gcolA")
    ngc = [P.sb([128, 16], F32, f"ngc{i}") for i in range(2)]
    lbrow = [P.sb([128, BR], F32, f"lbrow{d}") for d in range(2)]
    omlrow = [P.sb([128, BR], F32, f"omlrow{d}") for d in range(2)]
    omlcol = [P.sb([128, 4], F32, f"omlcol{d}") for d in range(2)]
    hn_col = [P.sb([128, 4], F32, f"hncol{j}") for j in range(3)]
    gb_row = P.sb([128, 16], F32, "gbrow")
    lgB = P.sb([128, 8], F32, "lgB")
    nlgB = P.sb([128, 8], F32, "nlgB")
    cw = P.sb([128, 8, 9], F32, "cw")
    cbias = P.sb([128, 8], F32, "cbias")
    s5d_col = P.sb([128, 4], F32, "s5dcol")
    glub_col = P.sb([128, 4], F32, "glubcol")
    gluw = P.sb([128, 4, BR], BF16, "gluw")
    ssq = P.sb([128, 4], F32, "ssq")
    P.arena_init(44500)

    def act(out_ap, in_ap, func, reads, writes, **kw):
        P.op("act", lambda e: e.activation(out_ap, in_ap, func, **kw), reads=reads, writes=writes)

    def tt(out_ap, a, b, op, reads, writes, eng="dve"):
        P.op(eng, lambda e: e.tensor_tensor(out_ap, a, b, op), reads=reads, writes=writes)

    def ts(out_ap, a, s1, s2, op0, op1, reads, writes):
        if op1 is None:
            P.op("dve", lambda e: e.tensor_scalar(out_ap, a, s1, None, op0), reads=reads, writes=writes)
        else:
            P.op("dve", lambda e: e.tensor_scalar(out_ap, a, s1, s2, op0, op1), reads=reads, writes=writes)

    def cp(out_ap, in_ap, reads, writes, eng="dve"):
        P.op(eng, lambda e: e.tensor_copy(out_ap, in_ap), reads=reads, writes=writes)

    def mm(out_ap, lhsT, rhs, start, stop, reads, writes):
        P.op("pe", lambda e: e.matmul(out_ap, lhsT=lhsT, rhs=rhs, start=start, stop=stop), reads=reads, writes=writes)

    def memset(tile_, ap, val, eng="dve"):
        P.op(eng, lambda e: e.memset(ap, val), writes=[tile_])

    def recip(tile_, ap):
        P.op("dve", lambda e: e.reciprocal(ap, ap), reads=[tile_], writes=[tile_])

    def small_dma(out_ap, in_ap, tile_, q="sp"):
        P.dma(q, out_ap, in_ap, writes=[tile_], allow_slow_non_contiguous=True)

    P.dma("sp", ident[:], k_ident, writes=[ident])
    cp(ident_b[:], ident[:], [ident], [ident_b])
    memset(ones_f, ones_f[:], 1.0)
    memset(ones_b, ones_b[:], 1.0)
    for d in range(2):
        P.dma("sp", tri[d][:], k_tri[d], writes=[tri[d]])
        ts(ntri[d][:], tri[d][:], -1.0, None, ALU.mult, None, [tri[d]], [ntri[d]])
        P.dma("sp", iotaF[d][:], k_iotaF[d], writes=[iotaF[d]])
        P.dma("sp", iotaK[d][:], k_iotaK[d], writes=[iotaK[d]])
    P.dma("sp", rst[:], k_rst, writes=[rst])
    P.dma("sp", xres, xin)
    P.barrier()

    def gemm(A, Kd, ntok, Wap, jobs, TBLK):
        P.arena_reset()
        ablk = P.ar([128, 36864], BF16, "ablk")
        wbuf = [P.ar([128, 8192], BF16, f"wbuf{i}") for i in range(2)]
        KC = Kd // 128
        for tb0 in range(0, ntok, TBLK):
            tbn = min(TBLK, ntok - tb0)
            av = ablk.t[:, 0:KC * tbn].rearrange("p (k n) -> p k n", k=KC)
            P.dma("sp", av, A[:, tb0:tb0 + tbn].rearrange("(k p) n -> p k n", p=128), writes=[ablk])
            for (col0, ncols, mode, epi) in jobs:
                wt = wbuf[st["w"] % 2]
                st["w"] += 1
                wv = wt.t[:, 0:KC * ncols].rearrange("p (k n) -> p k n", k=KC)
                P.dma("pool", wv, Wap[:, col0:col0 + ncols].rearrange("(k p) n -> p k n", p=128), writes=[wt])
                if mode == "F":
                    for cg in range(0, ncols, 128):
                        cgn = min(128, ncols - cg)
                        for t0 in range(0, tbn, 512):
                            tn = min(512, tbn - t0)
                            pt, pv = next_ps()
                            for k in range(KC):
                                mm(pv[0:cgn, 0:tn], wv[:, k, cg:cg + cgn], av[:, k, t0:t0 + tn], k == 0, k == KC - 1,
                                   [wt, ablk], [pt])
                            epi(pt, pv, cg, cgn, tb0 + t0, tn)
                else:
                    for t0 in range(0, tbn, 128):
                        tn = min(128, tbn - t0)
                        pt, pv = next_ps()
                        for k in range(KC):
                            mm(pv[0:tn, 0:ncols], av[:, k, t0:t0 + tn], wv[:, k, 0:ncols], k == 0, k == KC - 1,
                               [wt, ablk], [pt])
                        epi(pt, pv, tb0 + t0, tn)
        P.barrier()

    def epiF_simple(dst, row0, func, scale=1.0, dtype=BF16):
        def epi(pt, pv, cg, cgn, tok0, tn):
            s = sbb() if dtype == BF16 else sf()
            act(s[0:cgn, 0:tn], pv[0:cgn, 0:tn], func, [pt], [s], scale=scale)
            P.dma("sp", dst[row0 + cg:row0 + cg + cgn, tok0:tok0 + tn], s[0:cgn, 0:tn], reads=[s])
        return epi

    def epiK_simple(dst, func, scale=1.0, dtype=BF16):
        def epi(pt, pv, tok0, tn):
            s = sbb() if dtype == BF16 else sf()
            nco = dst.shape[1]
            act(s[0:tn, 0:nco], pv[0:tn, 0:nco], func, [pt], [s], scale=scale)
            P.dma("sp", dst[tok0:tok0 + tn, :], s[0:tn, 0:nco], reads=[s])
        return epi

    def layer_setup(l):
        cc = sf()
        ccv = cc[:, 0:32].rearrange("p (j v) -> p j v", v=2)
        for v in range(2):
            small_dma(ccv[:, :, v], c2[v].rearrange("(j p) -> p j", p=128), cc)
        cb = sbb()
        cbv = cb[:, 0:32].rearrange("p (j v) -> p j v", v=2)
        act(cbv, ccv, AF.Silu, [cc], [cb])
        P.dma("sp", scF.rearrange("(j p) v -> p j v", p=128), cbv, reads=[cb], allow_slow_non_contiguous=True)
        P.barrier()

        def epi_mod_factory(col0):
            def epi(pt, pv, tok0, tn):
                bm = sf()
                P.dma("act", bm[0:2, :], b_mod[l:l + 1, col0:col0 + 512].partition_broadcast(2), writes=[bm])
                s = sf()
                tt(s[0:2, 0:512], pv[0:2, 0:512], bm[0:2, :], ALU.add, [pt, bm], [s])
                P.dma("sp", mod_d[:, col0:col0 + 512], s[0:2, 0:512], reads=[s])
            return epi
        jobs = [(c0, 512, "K", epi_mod_factory(c0)) for c0 in range(0, 6 * D, 512)]
        gemm(scF, D, 2, w_mod[l], jobs, 128)
        for v in range(2):
            for m_ in range(6):
                small_dma(modcol[:, v, m_, :], mod_d[v, m_ * D:(m_ + 1) * D].rearrange("(j p) -> p j", p=128), modcol)
        small_dma(ngc[0][:], norm_mix[l].rearrange("(j p) -> p j", p=128), ngc[0])
        small_dma(ngc[1][:], norm_mlp[l].rearrange("(j p) -> p j", p=128), ngc[1])
        for d in range(2):
            if l == 0:
                memset(lbrow[d], lbrow[d][:], 0.0)
                memset(omlrow[d], omlrow[d][:], 1.0)
                memset(omlcol[d], omlcol[d][:], 1.0)
            else:
                a0 = sf()
                a1 = sf()
                P.dma("sp", a0[:], lb_logits[0, d:d + 1, :].partition_broadcast(128), writes=[a0])
                P.dma("sp", a1[:], lb_logits[1, d:d + 1, :].partition_broadcast(128), writes=[a1])
                tt(a1[:], a1[:], a0[:], ALU.subtract, [a0, a1], [a1])
                act(lbrow[d][:], a1[:], AF.Sigmoid, [a1], [lbrow[d]])
                ts(omlrow[d][:], lbrow[d][:], -1.0, 1.0, ALU.mult, ALU.add, [lbrow[d]], [omlrow[d]])
                c0_ = sf()
                small_dma(c0_[:, 0:4], lb_logits[0, d].rearrange("(h p) -> p h", p=128), c0_)
                small_dma(c0_[:, 4:8], lb_logits[1, d].rearrange("(h p) -> p h", p=128), c0_)
                tt(c0_[:, 8:12], c0_[:, 0:4], c0_[:, 4:8], ALU.subtract, [c0_], [c0_])
                act(omlcol[d][:], c0_[:, 8:12], AF.Sigmoid, [c0_], [omlcol[d]])
        for j, g in enumerate((hgrn_norm, ret_norm, mlstm_norm)):
            small_dma(hn_col[j][:], g[l].rearrange("(h p) -> p h", p=128), hn_col[j])
        P.dma("sp", gb_row[:], gate_b[l:l + 1, :].partition_broadcast(128), writes=[gb_row])
        P.dma("sp", lgB[:], ret_decay[l:l + 1, :].partition_broadcast(128), writes=[lgB])
        act(lgB[:], lgB[:], AF.Exp, [lgB], [lgB])
        ts(lgB[:], lgB[:], -1.0, 1.0, ALU.mult, ALU.add, [lgB], [lgB])
        act(lgB[:], lgB[:], AF.Ln, [lgB], [lgB])
        ts(nlgB[:], lgB[:], -1.0, None, ALU.mult, None, [lgB], [nlgB])
        for cc_ in range(8):
            small_dma(cw[:, cc_, :], conv_w[l][:, :, cc_ * 128:(cc_ + 1) * 128].rearrange("a b p -> p (a b)"), cw)
        small_dma(cbias[:], conv_b[l].rearrange("(cc p) -> p cc", p=128), cbias)
        small_dma(s5d_col[:], s5_d[l].rearrange("(cc p) -> p cc", p=128), s5d_col)
        small_dma(glub_col[:], s5_glu_b[l].rearrange("(cc p) -> p cc", p=128), glub_col)
        P.dma("pool", gluw[:], s5_glu_w[l].rearrange("(k p) n -> p k n", p=128), writes=[gluw])
        P.barrier()

    def rstd_of(tile_x, junk):
        memset(ssq, ssq[:, 0:1], 0.0)
        act(junk[:], tile_x[:], AF.Square, [tile_x, ssq], [junk, ssq], accum_out=ssq[:, 0:1])
        act(ssq[:, 1:2], ssq[:, 0:1], AF.Ln, [ssq], [ssq], scale=1.0 / D, bias=EPS)
        act(ssq[:, 2:3], ssq[:, 1:2], AF.Exp, [ssq], [ssq], scale=-0.5)
        return ssq[:, 2:3]

    def norm_phase(which):
        P.arena_reset()
        xt = [P.ar([128, D], F32, f"xt{i}") for i in range(2)]
        xn = P.ar([128, D], F32, "xn")
        hT = [P.ar([128, 16, 128], BF16, f"hT{i}") for i in range(2)]
        junk = P.ar([128, D], BF16, "junk")
        m_shift, m_scale = (0, 1) if which == 0 else (3, 4)
        for v in range(2):
            ts(gcolA[:, v, :], modcol[:, v, m_scale, :], 1.0, None, ALU.add, None, [modcol], [gcolA])
            tt(gcolA[:, v, :], gcolA[:, v, :], ngc[which][:], ALU.mult, [gcolA, ngc[which]], [gcolA])
        for ti in range(NT // 128):
            v = 1 if ti < 2 else 0
            x_ = xt[ti % 2]
            P.dma("sp", x_[:], xres[ti * 128:(ti + 1) * 128, :], writes=[x_])
            r = rstd_of(x_, junk)
            ts(xn[:], x_[:], r, None, ALU.mult, None, [x_, ssq], [xn])
            h_ = hT[ti % 2]
            for j in range(16):
                pt, pv = next_ps()
                mm(pv[:, 0:128], xn[:, j * 128:(j + 1) * 128], ident[:], True, True, [xn, ident], [pt])
                act(h_[:, j, :], pv[:, 0:128], AF.Identity, [pt, gcolA, modcol], [h_],
                    scale=gcolA[:, v, j:j + 1], bias=modcol[:, v, m_shift, j:j + 1])
            P.dma("sp", hF[:, ti * 128:(ti + 1) * 128].rearrange("(j p) n -> p j n", p=128), h_[:], reads=[h_])
        P.barrier()

    def win_phase(l):
        jobs = []
        jobs.append((0, 512, "F", epiF_simple(FB["hq"], 0, AF.Silu)))

        def epi_hk(d):
            def epi(pt, pv, cg, cgn, tok0, tn):
                s = sf()
                act(s[:, 0:tn], pv[:, 0:tn], AF.Sigmoid, [pt], [s], scale=-1.0)
                sb_ = sbb()
                h = cg // 128
                ts(sb_[:, 0:tn], s[:, 0:tn], omlcol[d][:, h:h + 1], None, ALU.mult, None, [s, omlcol[d]], [sb_])
                P.dma("sp", FB[f"hkF{d}"][cg:cg + 128, tok0:tok0 + tn], sb_[:, 0:tn], reads=[sb_])
            return epi
        jobs.append((512, 512, "F", epi_hk(0)))
        jobs.append((1024, 512, "F", epi_hk(1)))
        jobs.append((2048, 512, "F", epiF_simple(FB["hog"], 0, AF.Silu)))
        jobs.append((2560, 512, "F", epiF_simple(FB["rq"], 0, AF.Identity)))
        jobs.append((3072, 512, "F", epiF_simple(FB["rkF"], 0, AF.Identity, scale=128 ** -0.5)))
        jobs.append((4096, 512, "F", epiF_simple(FB["rog"], 0, AF.Silu)))
        jobs.append((4608, 512, "F", epiF_simple(FB["uF"], 0, AF.Identity)))
        jobs.append((5120, 512, "F", epiF_simple(mqk, 0, AF.Identity, dtype=F32)))
        jobs.append((5632, 512, "F", epiF_simple(mqk, 512, AF.Identity, dtype=F32)))
        jobs.append((6656, 512, "F", epiF_simple(FB["mz"], 0, AF.Silu)))
        for gblk in range(16):
            jobs.append((7184 + gblk * 512, 512, "F", epiF_simple(gF, gblk * 512, AF.Sigmoid)))

        def epi_hf(d):
            def epi(pt, pv, tok0, tn):
                s = sf()
                act(s[0:tn, :], pv[0:tn, :], AF.Sigmoid, [pt], [s])
                tt(s[0:tn, :], s[0:tn, :], omlrow[d][0:tn, :], ALU.mult, [s, omlrow[d]], [s])
                tt(s[0:tn, :], s[0:tn, :], lbrow[d][0:tn, :], ALU.add, [s, lbrow[d]], [s])
                kb = sbb()
                ts(kb[0:tn, :], s[0:tn, :], -1.0, 1.0, ALU.mult, ALU.add, [s], [kb])
                P.dma("sp", KB[f"hkK{d}"][tok0:tok0 + tn, :], kb[0:tn, :], reads=[kb])
                ts(s[0:tn, :], s[0:tn, :], 1e-30, None, ALU.max, None, [s], [s])
                s2 = sf()
                act(s2[0:tn, :], s[0:tn, :], AF.Ln, [s], [s2])
                P.dma("sp", hlogf[d][tok0:tok0 + tn, :], s2[0:tn, :], reads=[s2])
            return epi
        jobs.append((512, 512, "K", epi_hf(0)))
        jobs.append((1024, 512, "K", epi_hf(1)))
        jobs.append((1536, 512, "K", epiK_simple(KB["hv"], AF.Identity)))
        jobs.append((3072, 512, "K", epiK_simple(KB["rkK"], AF.Identity, scale=128 ** -0.5)))
        jobs.append((3584, 512, "K", epiK_simple(KB["rv"], AF.Identity)))
        jobs.append((6144, 512, "K", epiK_simple(KB["mv"], AF.Identity)))

        def epi_mg(pt, pv, tok0, tn):
            s = sf()
            tt(s[0:tn, 0:16], pv[0:tn, 0:16], gb_row[0:tn, :], ALU.add, [pt, gb_row], [s])
            sv = s[0:tn, 0:16].rearrange("p (d i h) -> p d i h", d=2, i=2)
            s2 = sf()
            s2v = s2[0:tn, 0:16].rearrange("p (d i h) -> p d i h", d=2, i=2)
            act(s2v[:, :, 1, :], sv[:, :, 1, :], AF.Sigmoid, [s], [s2])
            act(sv[:, :, 1, :], s2v[:, :, 1, :], AF.Ln, [s2], [s])
            P.dma("sp", mg[tok0:tok0 + tn, :], s[0:tn, 0:16], reads=[s])
        jobs.append((7168, 16, "K", epi_mg))
        TB = 2176 if NT >= 2176 else NT
        gemm(hF, D, NT, w_in[l], jobs, TB)

    def conv_phase():
        P.arena_reset()
        cin = P.ar([128, NT], F32, "cin")
        cout = P.ar([128, NT], F32, "cout")
        cob = P.ar([128, NT], BF16, "cob")
        for cc in range(8):
            P.dma("sp", cin[:], mqk[cc * 128:(cc + 1) * 128, :], writes=[cin])

            def w(a, b, cc=cc):
                return cw[:, cc, a * 3 + b:a * 3 + b + 1]
            ts(cout[:], cin[:], w(1, 1), cbias[:, cc:cc + 1], ALU.mult, ALU.add, [cin, cw, cbias], [cout])

            def acc(o_ap, i_ap, wap):
                P.op("dve", lambda e: e.scalar_tensor_tensor(o_ap, i_ap, wap, o_ap, ALU.mult, ALU.add),
                     reads=[cin, cout, cw], writes=[cout])
            acc(cout[:, 1:CTX], cin[:, 0:CTX - 1], w(1, 0))
            acc(cout[:, 0:CTX - 1], cin[:, 1:CTX], w(1, 2))
            ci = cin[:, CTX:NT].rearrange("p (r c) -> p r c", c=64)
            co = cout[:, CTX:NT].rearrange("p (r c) -> p r c", c=64)
            for dr in (-1, 0, 1):
                for dc in (-1, 0, 1):
                    if dr == 0 and dc == 0:
                        continue
                    ro = slice(max(0, -dr), ROWS - max(0, dr))
                    ri = slice(max(0, dr), ROWS - max(0, -dr))
                    co_ = slice(max(0, -dc), 64 - max(0, dc))
                    ci_ = slice(max(0, dc), 64 - max(0, -dc))
                    acc(co[:, ro, co_], ci[:, ri, ci_], w(dr + 1, dc + 1))
            act(cob[:], cout[:], AF.Silu, [cout], [cob])
            if cc < 4:
                P.dma("sp", FB["mq"][cc * 128:(cc + 1) * 128, :], cob[:], reads=[cob])
            else:
                ts(cob[:], cob[:], 128 ** -0.5, None, ALU.mult, None, [cob], [cob])
                P.dma("sp", FB["mk"][(cc - 4) * 128:(cc - 3) * 128, :], cob[:], reads=[cob])
        P.barrier()

    def sc_order(d):
        if d == 0:
            return list(range(NSC))
        return [0] + list(range(NSC - 1, 0, -1))

    def cmul(or_, oi_, ar, ai, br, bi, t1, t2, reads, writes):
        wr, wi = writes[0:1], writes[1:2]
        if writes[0] is writes[1]:
            tt(t1[1], ar, br, ALU.mult, reads, [t1[0]])
            tt(t2[1], ai, bi, ALU.mult, reads, [t2[0]])
            tt(or_, t1[1], t2[1], ALU.subtract, [t1[0], t2[0]], wr)
            tt(t1[1], ar, bi, ALU.mult, reads, [t1[0]])
            tt(t2[1], ai, br, ALU.mult, reads, [t2[0]])
            tt(oi_, t1[1], t2[1], ALU.add, [t1[0], t2[0]], wi)
            return
        tt(or_, ar, br, ALU.mult, reads, wr)
        tt(t1[1], ai, bi, ALU.mult, reads, [t1[0]])
        tt(oi_, ar, bi, ALU.mult, reads, wi)
        tt(t2[1], ai, br, ALU.mult, reads, [t2[0]])
        tt(or_, or_, t1[1], ALU.subtract, [writes[0], t1[0]], wr)
        tt(oi_, oi_, t2[1], ALU.add, [writes[1], t2[0]], wi)

    def scan_phase(l, d):
        P.arena_reset()
        tl = 63 if d == 0 else 0
        A = P.ar
        qF = A([128, 4, 256], BF16, "qF")
        kF = A([128, 4, 256], BF16, "kF")
        kK = A([64, 4, 512], BF16, "kK")
        vK = A([64, 4, 4, 129], BF16, "vK")
        lf = A([64, 4, 512], F32, "lf")
        g = A([64, 4, 16], F32, "gK")
        EpF = A([128, 4, 256], F32, "EpF")
        EmF = A([128, 4, 256], F32, "EmF")
        EkK = A([64, 4, 512], F32, "EkK")
        Ekm = A([64, 4, 4], F32, "Ekm")
        Et = A([128, 4, 4], F32, "Et")
        qp = A([128, 4, 256], BF16, "qp")
        kpF = A([128, 4, 256], BF16, "kpF")
        kpK = A([64, 4, 512], BF16, "kpK")
        PT = A([64, 16, 64], BF16, "PT")
        Sst = A([128, 4, 129], F32, "Sst")
        Sbf = A([128, 4, 129], BF16, "Sbf")
        nbc = A([128, 4, 128], BF16, "nbc")
        Otile = A([128, 4, 256], F32, "Otile")
        Dtile = A([128, 4, 256], F32, "Dtile")
        rawp = A([128, 4, 256], F32, "rawp")
        sq_t = A([128, 4, 256], F32, "sq_t")
        gate_t = A([128, 4, 256], BF16, "gate_t")
        o_bf = A([128, 4, 256], BF16, "o_bf")
        CPt = A([128, 4, 16], F32, "CPt")
        stri = A([64, 64], F32, "stri")
        rEp = A([128, 4, 256], F32, "rEp")
        rEm = A([128, 4, 256], F32, "rEm")
        rEk = A([64, 4, 512], F32, "rEk")
        rEt = A([128, 4, 4], F32, "rEt")
        qpb = A([128, 4, 256], BF16, "qpb", at=kpF.off)
        kpI = [A([128, 4, 256], BF16, "kpI0", at=rEp.off), A([128, 4, 256], BF16, "kpI1", at=rEp.off + 512),
               A([128, 4, 256], BF16, "kpI2", at=rEm.off), A([128, 4, 256], BF16, "kpI3", at=rEm.off + 512)]
        LPr = A([128, 16, 64], F32, "LPr")
        LPi = A([128, 16, 64], F32, "LPi")
        LMr = A([128, 16, 64], F32, "LMr")
        LMi = A([128, 16, 64], F32, "LMi")
        Bpad = A([128, 16, 2, 128], BF16, "Bpad")
        Cpad = A([128, 16, 2, 128], BF16, "Cpad")
        Zr = A([128, 16, 64], F32, "Zr")
        Zi = A([128, 16, 64], F32, "Zi")
        Wr = A([128, 16, 64], F32, "Wr")
        Wi = A([128, 16, 64], F32, "Wi")
        T1 = A([128, 16, 64], F32, "T1")
        T2 = A([128, 16, 64], F32, "T2")
        Xrb = A([128, 16, 64], BF16, "Xrb")
        Xib = A([128, 16, 64], BF16, "Xib")
        xpr = A([128, 16], F32, "xpr")
        xpi = A([128, 16], F32, "xpi")
        tot = A([128, 16], F32, "tot")
        u = A([128, 4, 256], BF16, "u")
        yc_b = A([128, 4, 256], BF16, "yc_b")
        sm = [A([128, 16], F32, f"sm{i}") for i in range(14)]
        smi = A([128, 16], I32, "smi")

        def ret_tables():
            for h in range(4):
                c_ = d * 4 + h
                for c in range(4):
                    act(rEp[:, h, c * 64:(c + 1) * 64], iotaF[d][:], AF.Exp, [iotaF[d], lgB], [rEp], scale=lgB[:, c_:c_ + 1])
                    act(rEm[:, h, c * 64:(c + 1) * 64], iotaF[d][:], AF.Exp, [iotaF[d], nlgB], [rEm], scale=nlgB[:, c_:c_ + 1])
                    act(rEk[:, c, h * 128:(h + 1) * 128], iotaK[d][:], AF.Exp, [iotaK[d], nlgB], [rEk], scale=nlgB[0:64, c_:c_ + 1])
                    cp(rEt[:, c, h:h + 1], rEp[:, h, tl:tl + 1], [rEp], [rEt])

        tt(stri[:], tri[1 - d][:], ident[0:64, 0:64], ALU.subtract, [tri[1 - d], ident], [stri])

        def sin_of(dst_t, theta_t, tmp):
            ts(tmp[:], theta_t[:], 1.0 / (2 * math.pi), None, ALU.mult, None, [theta_t], [tmp])
            cp(smi[:], tmp[:], [tmp], [smi])
            cp(tmp[:], smi[:], [smi], [tmp])
            P.op("dve", lambda e: e.scalar_tensor_tensor(tmp[:], tmp[:], -2 * math.pi, theta_t[:], ALU.mult, ALU.add),
                 reads=[tmp, theta_t], writes=[tmp])
            act(dst_t[:], tmp[:], AF.Sin, [tmp], [dst_t])

        def pow_table(Tr, Ti, br, bi, rev):
            c0 = 63 if rev else 0
            cp(Tr[:, :, c0], br[:], [br], [Tr])
            cp(Ti[:, :, c0], bi[:], [bi], [Ti])
            n = 1
            while n < 64:
                if not rev:
                    src, dst, top = slice(0, n), slice(n, 2 * n), n - 1
                else:
                    src, dst, top = slice(64 - n, 64), slice(64 - 2 * n, 64 - n), 64 - n
                fr_ = Tr[:, :, top:top + 1].to_broadcast([128, 16, n])
                fi_ = Ti[:, :, top:top + 1].to_broadcast([128, 16, n])
                cmul(Tr[:, :, dst], Ti[:, :, dst], Tr[:, :, src], Ti[:, :, src], fr_, fi_,
                     (T1, T1[:, :, 0:n]), (T2, T2[:, :, 0:n]), [Tr, Ti], [Tr, Ti])
                n *= 2

        are, aim, dt_, t0_, t1_, lre, lim, den, fr, fi, ire, iim, t2_, t3_ = sm
        for (tile_, src) in ((are, s5_a_re), (aim, s5_a_im)):
            v_ = src[l, d].rearrange("(j two) p -> two p j", two=2)
            for two in range(2):
                small_dma(tile_[64 * two:64 * two + 64, :], v_[two], tile_)
        ldt = s5_log_dt[l, d].rearrange("(j two) -> two j", two=2)
        for two in range(2):
            small_dma(dt_[64 * two:64 * two + 64, :], ldt[two:two + 1, :].partition_broadcast(64), dt_)
        act(dt_[:], dt_[:], AF.Exp, [dt_], [dt_])
        tt(t0_[:], are[:], dt_[:], ALU.mult, [are, dt_], [t0_])
        act(t0_[:], t0_[:], AF.Exp, [t0_], [t0_])
        tt(t1_[:], aim[:], dt_[:], ALU.mult, [aim, dt_], [t1_])
        sin_of(lim, t1_, t2_)
        ts(t3_[:], t1_[:], math.pi / 2, None, ALU.add, None, [t1_], [t3_])
        sin_of(lre, t3_, t2_)
        tt(lre[:], lre[:], t0_[:], ALU.mult, [lre, t0_], [lre])
        tt(lim[:], lim[:], t0_[:], ALU.mult, [lim, t0_], [lim])
        tt(den[:], are[:], are[:], ALU.mult, [are], [den])
        tt(t2_[:], aim[:], aim[:], ALU.mult, [aim], [t2_])
        tt(den[:], den[:], t2_[:], ALU.add, [den, t2_], [den])
        recip(den, den[:])
        ts(t0_[:], lre[:], -1.0, None, ALU.add, None, [lre], [t0_])
        tt(fr[:], t0_[:], are[:], ALU.mult, [t0_, are], [fr])
        tt(t2_[:], lim[:], aim[:], ALU.mult, [lim, aim], [t2_])
        tt(fr[:], fr[:], t2_[:], ALU.add, [fr, t2_], [fr])
        tt(fr[:], fr[:], den[:], ALU.mult, [fr, den], [fr])
        tt(fi[:], lim[:], are[:], ALU.mult, [lim, are], [fi])
        tt(t2_[:], t0_[:], aim[:], ALU.mult, [t0_, aim], [t2_])
        tt(fi[:], fi[:], t2_[:], ALU.subtract, [fi, t2_], [fi])
        tt(fi[:], fi[:], den[:], ALU.mult, [fi, den], [fi])
        tt(t0_[:], lre[:], lre[:], ALU.mult, [lre], [t0_])
        tt(t2_[:], lim[:], lim[:], ALU.mult, [lim], [t2_])
        tt(t0_[:], t0_[:], t2_[:], ALU.add, [t0_, t2_], [t0_])
        recip(t0_, t0_[:])
        tt(ire[:], lre[:], t0_[:], ALU.mult, [lre, t0_], [ire])
        tt(iim[:], lim[:], t0_[:], ALU.mult, [lim, t0_], [iim])
        ts(iim[:], iim[:], -1.0, None, ALU.mult, None, [iim], [iim])
        rev = (d == 1)
        pow_table(LPr, LPi, lre, lim, rev)
        pow_table(LMr, LMi, ire, iim, rev)
        bre = Tile(Zr.t[:, :, 0:16], "bre")
        bim = Tile(Zr.t[:, :, 16:32], "bim")
        bbr = Tile(Zr.t[:, :, 32:48], "bbr")
        bbi = Tile(Zr.t[:, :, 48:64], "bbi")
        tA = Tile(Zi.t[:, :, 0:16], "tA")
        tB = Tile(Zi.t[:, :, 16:32], "tB")
        bpadf = A([128, 16, 128], F32, "bpadf", at=Wr.off)
        assert Wi.off == Wr.off + 1024
        for (tile_, src) in ((bre, s5_b_re), (bim, s5_b_im)):
            v_ = src[l, d].rearrange("(j two) p h -> two p j h", two=2)
            for two in range(2):
                small_dma(tile_[64 * two:64 * two + 64, :, :], v_[two], tile_)
        frb = fr[:].unsqueeze(2).to_broadcast([128, 16, 16])
        fib = fi[:].unsqueeze(2).to_broadcast([128, 16, 16])
        cmul(bbr[:], bbi[:], bre[:], bim[:], frb, fib, (tA, tA[:]), (tB, tB[:]), [bre, bim, fr, fi], [bbr, bbi])
        for part, src_t in enumerate((bbr, bbi)):
            memset(bpadf, bpadf[:], 0.0)
            for half in range(2):
                for jj in range(4):
                    cp(bpadf[64 * half:64 * half + 64, jj::4, 32 * jj + 16 * half:32 * jj + 16 * half + 16],
                       src_t[64 * half:64 * half + 64, jj::4, :], [src_t], [bpadf])
            for j in range(16):
                pt, pv = next_ps()
                mm(pv[:, 0:128], bpadf[:, j, :], ident[:], True, True, [bpadf, ident], [pt])
                cp(Bpad[:, j, part, :], pv[:, 0:128], [pt], [Bpad])
        for part, src in enumerate((s5_c_re, s5_c_im)):
            for two in range(2):
                for j_ in range(16):
                    small_dma(bre[64 * two:64 * two + 64, j_, :], src[l, d, 2 * j_ + two].rearrange("h p -> p h"), bre)
            memset(bpadf, bpadf[:], 0.0)
            for half in range(2):
                for jj in range(4):
                    cp(bpadf[64 * half:64 * half + 64, jj::4, 32 * jj + 16 * half:32 * jj + 16 * half + 16],
                       bre[64 * half:64 * half + 64, jj::4, :], [bre], [bpadf])
            if part == 0:
                cp(Cpad[:, :, 0, :], bpadf[:], [bpadf], [Cpad])
            else:
                ts(Cpad[:, :, 1, :], bpadf[:], -1.0, None, ALU.mult, None, [bpadf], [Cpad])
        for al in (bre, bim, bbr, bbi):
            Zr.readers.update(al.readers); Zr.writers.update(al.writers)
        for al in (tA, tB):
            Zi.readers.update(al.readers); Zi.writers.update(al.writers)

        P.barrier()
        if stop_after == ("s_tab" + ("1" if d else ""), l):
            return True

        def gla_scan(mix, j_out, gcol):
            memset(Sst, Sst[:], 0.0)
            memset(Sbf, Sbf[:], 0.0)
            memset(nbc, nbc[:], 0.0)
            memset(vK, vK[:], 1.0)
            if mix == "h":
                qs, kFs, kKs, vs, gs = FB["hq"], FB[f"hkF{d}"], KB[f"hkK{d}"], KB["hv"], FB["hog"]
            elif mix == "r":
                qs, kFs, kKs, vs, gs = FB["rq"], FB["rkF"], KB["rkK"], KB["rv"], FB["rog"]
            else:
                qs, kFs, kKs, vs, gs = FB["mq"], FB["mk"], None, KB["mv"], FB["mz"]
            for sc in sc_order(d):
                n0 = sc * 256
                P.dma("sp", qF[:], qs[:, n0:n0 + 256].rearrange("(h p) n -> p h n", p=128), writes=[qF])
                P.dma("sp", kF[:], kFs[:, n0:n0 + 256].rearrange("(h p) n -> p h n", p=128), writes=[kF])
                vsrc = vs[n0:n0 + 256, :].rearrange("(c s) (h v) -> s c h v", s=64, v=128)
                for c in range(4):
                    P.dma("act", vK[:, c, :, 0:128], vsrc[:, c], writes=[vK])
                if kKs is not None:
                    P.dma("act", kK[:], kKs[n0:n0 + 256, :].rearrange("(c s) x -> s c x", s=64), writes=[kK])
                else:
                    for c in range(4):
                        pt, pv = next_ps()
                        for h in range(4):
                            mm(pv[0:64, h * 128:(h + 1) * 128], kF[:, h, c * 64:(c + 1) * 64], ident_b[:], True, True, [kF, ident_b], [pt])
                        act(kK[:, c, :], pv[0:64, :], AF.Identity, [pt], [kK])
                if mix == "r":
                    EP, EM, EK, ET = rEp, rEm, rEk, rEt
                    EK_ap = rEk[:]
                elif mix == "h":
                    P.dma("act", lf[:], hlogf[d][n0:n0 + 256, :].rearrange("(c s) x -> s c x", s=64), writes=[lf])
                    pe_ = PB[2]
                    for h in range(4):
                        for c in range(4):
                            mm(pe_[:, h * 256 + c * 64:h * 256 + c * 64 + 64], lf[:, c, h * 128:(h + 1) * 128], tri[d][:], True, True, [lf, tri[d]], [pe_])
                    act(EpF[:].rearrange("p h n -> p (h n)"), pe_[:], AF.Exp, [pe_], [EpF])
                    cumS = sq_t
                    act(cumS[:].rearrange("p h n -> p (h n)"), pe_[:], AF.Identity, [pe_], [cumS])
                    for cp_ in range(2):
                        pk = PB[3]
                        for c2_ in range(2):
                            c = cp_ * 2 + c2_
                            mm(pk[0:64, c2_ * 512:(c2_ + 1) * 512], stri[:], lf[:, c, :], True, True, [lf, stri], [pk])
                        act(EkK[:, cp_ * 2:cp_ * 2 + 2, :].rearrange("p c x -> p (c x)"), pk[0:64, :], AF.Exp, [pk], [EkK])
                    cp(Et[:].rearrange("p c h -> p h c"), EpF[:, :, tl::64], [EpF], [Et])
                    memset(CPt, CPt[:], 0.0)
                    c5 = cumS[:].rearrange("p h (c b t) -> p h c b t", c=4, b=4)
                    cpv = CPt[:].rearrange("p h (c b) -> p h c b", c=4)
                    if d == 0:
                        cp(cpv[:, :, :, 1:4], c5[:, :, :, 0:3, 15], [cumS], [CPt])
                    else:
                        cp(cpv[:, :, :, 0:3], c5[:, :, :, 1:4, 0], [cumS], [CPt])
                    Rt = Dtile
                    cp(Rt[:].rearrange("p h (k t) -> p h k t", t=16), CPt[:].unsqueeze(3).to_broadcast([128, 4, 16, 16]), [CPt], [Rt])
                    tt(Rt[:], cumS[:], Rt[:], ALU.subtract, [cumS, Rt], [Rt])
                    act(Rt[:], Rt[:], AF.Exp, [Rt], [Rt])
                    tt(qpb[:], qF[:], Rt[:], ALU.mult, [qF, Rt], [qpb])
                    for I in range(4):
                        tmpE = EmF
                        tt(tmpE[:].rearrange("p h (c t) -> p h c t", t=64),
                           cpv[:, :, :, I:I + 1].to_broadcast([128, 4, 4, 64]),
                           cumS[:].rearrange("p h (c t) -> p h c t", t=64), ALU.subtract, [CPt, cumS], [tmpE])
                        ts(tmpE[:], tmpE[:], 80.0, None, ALU.min, None, [tmpE], [tmpE])
                        act(tmpE[:], tmpE[:], AF.Exp, [tmpE], [tmpE])
                        tt(kpI[I][:], kF[:], tmpE[:], ALU.mult, [kF, tmpE], [kpI[I]])
                    EP, EM, EK, ET = EpF, EmF, EkK, Et
                    EK_ap = EkK[:]
                else:
                    P.dma("act", g[:], mg[n0:n0 + 256, :].rearrange("(c s) x -> s c x", s=64), writes=[g])
                    pe_ = PB[2]
                    pm_ = PB[3]
                    for h in range(4):
                        fcol = d * 8 + 4 + h
                        icol = d * 8 + h
                        for c in range(4):
                            sl = slice(h * 256 + c * 64, h * 256 + c * 64 + 64)
                            mm(pe_[:, sl], g[:, c, fcol:fcol + 1].to_broadcast([64, 128]), tri[d][:], True, True, [g, tri[d]], [pe_])
                            mm(pm_[:, sl], g[:, c, fcol:fcol + 1].to_broadcast([64, 128]), ntri[d][:], True, False, [g, ntri[d]], [pm_])
                            mm(pm_[:, sl], g[:, c, icol:icol + 1].to_broadcast([64, 128]), ident[0:64, 0:64], False, True, [g, ident], [pm_])
                    act(EpF[:].rearrange("p h n -> p (h n)"), pe_[:], AF.Exp, [pe_], [EpF])
                    act(EmF[:].rearrange("p h n -> p (h n)"), pm_[:], AF.Exp, [pm_], [EmF])
                    pt, pv = next_ps()
                    for c in range(4):
                        mm(pv[0:64, c * 16:(c + 1) * 16], tri[d][:], g[:, c, :], True, True, [g, tri[d]], [pt])
                    pvv = pv[0:64, 0:64].rearrange("p (c x) -> p c x", x=16)
                    tt(Ekm[:], g[:, :, d * 8:d * 8 + 4], pvv[:, :, d * 8 + 4:d * 8 + 8], ALU.subtract, [g, pt], [Ekm])
                    act(Ekm[:], Ekm[:], AF.Exp, [Ekm], [Ekm])
                    cp(Et[:].rearrange("p c h -> p h c"), EpF[:, :, tl::64], [EpF], [Et])
                    EP, EM, EK, ET = EpF, EmF, Ekm, Et
                    EK_ap = Ekm[:].unsqueeze(3).to_broadcast([64, 4, 4, 128])
                tt(qp[:], qF[:], EP[:], ALU.mult, [qF, EP], [qp])
                if mix != "h":
                    tt(kpF[:], kF[:], EM[:], ALU.mult, [kF, EM], [kpF])
                if mix == "m":
                    tt(kpK[:].rearrange("p c (h x) -> p c h x", x=128), kK[:].rearrange("p c (h x) -> p c h x", x=128), EK_ap, ALU.mult, [kK, EK], [kpK])
                else:
                    tt(kpK[:], kK[:], EK_ap, ALU.mult, [kK, EK], [kpK])
                pa = PB[0]
                for c in range(4):
                    for h in range(4):
                        blk = c * 4 + h
                        if mix == "h":
                            for I in range(4):
                                mm(pa[0:64, blk * 64 + 16 * I:blk * 64 + 16 * I + 16], kpI[I][:, h, c * 64:(c + 1) * 64],
                                   qpb[:, h, c * 64 + 16 * I:c * 64 + 16 * I + 16], True, True, [kpI[I], qpb], [pa])
                        else:
                            mm(pa[0:64, blk * 64:(blk + 1) * 64], kpF[:, h, c * 64:(c + 1) * 64], qp[:, h, c * 64:(c + 1) * 64], True, True, [kpF, qp], [pa])
                tt(PT[:], pa[0:64, :].rearrange("p (b t) -> p b t", t=64), tri[d][:].unsqueeze(1).to_broadcast([64, 16, 64]), ALU.mult, [pa, tri[d]], [PT])
                po = PB[1]
                pd = PB[2]
                corder = range(4) if d == 0 else range(3, -1, -1)
                for c in corder:
                    for h in range(4):
                        sl = slice(h * 256 + c * 64, h * 256 + c * 64 + 64)
                        mm(po[:, sl], vK[:, c, h, 0:128], PT[:, c * 4 + h, :], True, False, [vK, PT], [po])
                        mm(po[:, sl], Sbf[:, h, 0:128], qp[:, h, c * 64:(c + 1) * 64], False, True, [Sbf, qp], [po])
                        if mix == "m":
                            mm(pd[:, sl], ones_b[0:64, :], PT[:, c * 4 + h, :], True, False, [ones_b, PT], [pd])
                            mm(pd[:, sl], nbc[:, h, :], qp[:, h, c * 64:(c + 1) * 64], False, True, [nbc, qp], [pd])
                    pu = PB[3]
                    for h in range(4):
                        mm(pu[:, h * 256:h * 256 + 129], kpK[:, c, h * 128:(h + 1) * 128], vK[:, c, h, :], True, True, [kpK, vK], [pu])
                    puv = pu[:].rearrange("p (h x) -> p h x", x=256)[:, :, 0:129]
                    if mix == "h":
                        tt(Sst[:], Sst[:], ET[:, c, :].unsqueeze(2).to_broadcast([128, 4, 129]), ALU.mult, [Sst, ET], [Sst])
                        tt(Sst[:], Sst[:], puv, ALU.add, [Sst, pu], [Sst])
                    else:
                        tt(Sst[:], Sst[:], puv, ALU.add, [Sst, pu], [Sst])
                        tt(Sst[:], Sst[:], ET[:, c, :].unsqueeze(2).to_broadcast([128, 4, 129]), ALU.mult, [Sst, ET], [Sst])
                    act(Sbf[:], Sst[:], AF.Identity, [Sst], [Sbf])
                    if mix == "m":
                        cp(nbc[:], Sst[:, :, 128:129].to_broadcast([128, 4, 128]), [Sst], [nbc])
                Ov = Otile[:].rearrange("p h n -> p (h n)")
                if mix == "m":
                    act(Dtile[:].rearrange("p h n -> p (h n)"), pd[:], AF.Abs, [pd], [Dtile])
                    ts(Dtile[:], Dtile[:], 1.0, None, ALU.max, None, [Dtile], [Dtile])
                    recip(Dtile, Dtile[:])
                    tt(Ov, po[:], Dtile[:].rearrange("p h n -> p (h n)"), ALU.mult, [po, Dtile], [Otile])
                else:
                    act(Ov, po[:], AF.Identity, [po], [Otile])
                dst = rawf[j_out][:, n0:n0 + 256].rearrange("(h p) n -> p h n", p=128)
                if d == 0:
                    P.dma("sp", dst, Otile[:], reads=[Otile])
                else:
                    P.dma("sp", rawp[:], dst, writes=[rawp])
                    P.dma("sp", gate_t[:], gs[:, n0:n0 + 256].rearrange("(h p) n -> p h n", p=128), writes=[gate_t])
                    tt(rawp[:], rawp[:], Otile[:], ALU.add, [rawp, Otile], [rawp])
                    tt(sq_t[:], rawp[:], rawp[:], ALU.mult, [rawp], [sq_t])
                    pss = PB[0]
                    sqv = sq_t[:].rearrange("p h n -> p (h n)")
                    for hf in range(2):
                        mm(pss[:, hf * 512:(hf + 1) * 512], ones_f[:], sqv[:, hf * 512:(hf + 1) * 512], True, True, [ones_f, sq_t], [pss])
                    act(sqv, pss[:], AF.Ln, [pss], [sq_t], scale=1.0 / 128, bias=EPS)
                    act(sqv, sqv, AF.Exp, [sq_t], [sq_t], scale=-0.5)
                    tt(rawp[:], rawp[:], sq_t[:], ALU.mult, [rawp, sq_t], [rawp])
                    tt(rawp[:], rawp[:], gcol[:].unsqueeze(2).to_broadcast([128, 4, 256]), ALU.mult, [rawp, gcol], [rawp])
                    tt(o_bf[:], rawp[:], gate_t[:], ALU.mult, [rawp, gate_t], [o_bf])
                    P.dma("sp", oF[j_out * BR:(j_out + 1) * BR, n0:n0 + 256].rearrange("(h p) n -> p h n", p=128), o_bf[:], reads=[o_bf])

        def s5_scan():
            memset(xpr, xpr[:], 0.0)
            memset(xpi, xpi[:], 0.0)
            Xr, Xi = Zr, Zi
            for sc in sc_order(d):
                n0 = sc * 256
                P.dma("sp", u[:], FB["uF"][:, n0:n0 + 256].rearrange("(cc p) n -> p cc n", p=128), writes=[u])
                py = PB[1]
                corder = range(4) if d == 0 else range(3, -1, -1)
                for c in corder:
                    pbr, pbi = PB[2], PB[3]
                    for j in range(16):
                        cc = j // 4
                        mm(pbr[:, j * 64:(j + 1) * 64], Bpad[:, j, 0, :], u[:, cc, c * 64:(c + 1) * 64], True, True, [Bpad, u], [pbr])
                        mm(pbi[:, j * 64:(j + 1) * 64], Bpad[:, j, 1, :], u[:, cc, c * 64:(c + 1) * 64], True, True, [Bpad, u], [pbi])
                    bur = pbr[:].rearrange("p (j t) -> p j t", t=64)
                    bui = pbi[:].rearrange("p (j t) -> p j t", t=64)
                    cmul(Zr[:], Zi[:], bur, bui, LMr[:], LMi[:], (T1, T1[:]), (T2, T2[:]), [pbr, pbi, LMr, LMi], [Zr, Zi])
                    tin = 0 if d == 0 else 63
                    tt(Zr[:, :, tin], Zr[:, :, tin], xpr[:], ALU.add, [Zr, xpr], [Zr])
                    tt(Zi[:, :, tin], Zi[:, :, tin], xpi[:], ALU.add, [Zi, xpi], [Zi])
                    for (W_, Z_) in ((Wr, Zr), (Wi, Zi)):
                        P.op("dve", lambda e, W_=W_, Z_=Z_: e.tensor_tensor_scan(
                            W_[:].rearrange("p j t -> p (j t)"), rst[:].rearrange("p j t -> p (j t)"),
                            Z_[:].rearrange("p j t -> p (j t)"), 0.0, ALU.mult, ALU.add), reads=[rst, Z_], writes=[W_])
                        if d == 1:
                            cp(tot[:], W_[:, :, 63], [W_], [tot])
                            tt(W_[:], Z_[:], W_[:], ALU.subtract, [Z_, W_], [W_])
                            tt(W_[:], W_[:], tot[:].unsqueeze(2).to_broadcast([128, 16, 64]), ALU.add, [W_, tot], [W_])
                    cmul(Xr[:], Xi[:], Wr[:], Wi[:], LPr[:], LPi[:], (T1, T1[:]), (T2, T2[:]), [Wr, Wi, LPr, LPi], [Xr, Xi])
                    cp(xpr[:], Xr[:, :, tl], [Xr], [xpr])
                    cp(xpi[:], Xi[:, :, tl], [Xi], [xpi])
                    act(Xrb[:], Xr[:], AF.Identity, [Xr], [Xrb])
                    act(Xib[:], Xi[:], AF.Identity, [Xi], [Xib])
                    for cc in range(4):
                        sl = slice(cc * 256 + c * 64, cc * 256 + c * 64 + 64)
                        k_ = 0
                        for jj in range(4):
                            j = cc * 4 + jj
                            for part, X_ in enumerate((Xrb, Xib)):
                                mm(py[:, sl], Cpad[:, j, part, :], X_[:, j, :], k_ == 0, k_ == 7, [Cpad, X_], [py])
                                k_ += 1
                Ov = Otile[:].rearrange("p h n -> p (h n)")
                act(Ov, py[:], AF.Identity, [py], [Otile])
                dst = rawf[2][:, n0:n0 + 256].rearrange("(h p) n -> p h n", p=128)
                if d == 0:
                    P.dma("sp", dst, Otile[:], reads=[Otile])
                else:
                    P.dma("sp", rawp[:], dst, writes=[rawp])
                    tt(rawp[:], rawp[:], Otile[:], ALU.add, [rawp, Otile], [rawp])
                    cp(Dtile[:], u[:], [u], [Dtile])
                    tt(Dtile[:], Dtile[:], s5d_col[:].unsqueeze(2).to_broadcast([128, 4, 256]), ALU.mult, [Dtile, s5d_col], [Dtile])
                    tt(rawp[:], rawp[:], Dtile[:], ALU.add, [rawp, Dtile], [rawp])
                    act(rawp[:], rawp[:], AF.Gelu, [rawp], [rawp])
                    cp(yc_b[:], rawp[:], [rawp], [yc_b])
                    pg = PB[0]
                    for cco in range(4):
                        for cci in range(4):
                            mm(pg[:, cco * 256:(cco + 1) * 256], gluw[:, cci, cco * 128:(cco + 1) * 128], yc_b[:, cci, :], cci == 0, cci == 3, [gluw, yc_b], [pg])
                    for cco in range(4):
                        act(sq_t[:, cco, :], pg[:, cco * 256:(cco + 1) * 256], AF.Sigmoid, [pg, glub_col], [sq_t], bias=glub_col[:, cco:cco + 1])
                    tt(o_bf[:], rawp[:], sq_t[:], ALU.mult, [rawp, sq_t], [o_bf])
                    P.dma("sp", oF[2 * BR:3 * BR, n0:n0 + 256].rearrange("(h p) n -> p h n", p=128), o_bf[:], reads=[o_bf])

        gla_scan("h", 0, hn_col[0])
        if stop_after == ("s_h" + ("1" if d else ""), l):
            P.barrier()
            return True
        P.barrier()
        ret_tables()
        gla_scan("r", 1, hn_col[1])
        if stop_after == ("s_r" + ("1" if d else ""), l):
            P.barrier()
            return True
        s5_scan()
        if stop_after == ("s_s5" + ("1" if d else ""), l):
            P.barrier()
            return True
        gla_scan("m", 3, hn_col[2])
        P.barrier()
        return False

    def merge_phase(l):
        for j in range(4):
            def epi_cb(cb, j=j):
                def e_(pt, pv, cg, cgn, tok0, tn):
                    row = cb * 512 + cg
                    gt = sbb()
                    P.dma("act", gt[:, 0:tn], gF[j * D + row:j * D + row + 128, tok0:tok0 + tn], writes=[gt])
                    s = sf()
                    tt(s[:, 0:tn], pv[:, 0:tn], gt[:, 0:tn], ALU.mult, [pt, gt], [s])
                    dsl = yaccd[row:row + 128, tok0:tok0 + tn]
                    if j > 0:
                        s2 = sf()
                        P.dma("act", s2[:, 0:tn], dsl, writes=[s2])
                        tt(s[:, 0:tn], s[:, 0:tn], s2[:, 0:tn], ALU.add, [s, s2], [s])
                    if j < 3:
                        P.dma("sp", dsl, s[:, 0:tn], reads=[s])
                    else:
                        sb_ = sbb()
                        cp(sb_[:, 0:tn], s[:, 0:tn], [s], [sb_])
                        P.dma("sp", yF[row:row + 128, tok0:tok0 + tn], sb_[:, 0:tn], reads=[sb_])
                return e_
            jobs = [(cb * 512, 512, "F", epi_cb(cb)) for cb in range(4)]
            gemm(oF[j * BR:(j + 1) * BR, :], BR, NT, w_branch[l, j], jobs, 4352 if NT >= 4352 else NT)

    def resid_epi(gate_m, c0, ncols):
        def e_(pt, pv, tok0, tn):
            v = 1 if tok0 < CTX else 0
            gr = sf()
            P.dma("act", gr[:, 0:ncols], mod_d[v:v + 1, gate_m * D + c0:gate_m * D + c0 + ncols].partition_broadcast(128), writes=[gr])
            s = sf()
            tt(s[0:tn, 0:ncols], pv[0:tn, 0:ncols], gr[0:tn, 0:ncols], ALU.mult, [pt, gr], [s])
            s2 = sf()
            P.dma("act", s2[0:tn, 0:ncols], xres[tok0:tok0 + tn, c0:c0 + ncols], writes=[s2])
            tt(s[0:tn, 0:ncols], s[0:tn, 0:ncols], s2[0:tn, 0:ncols], ALU.add, [s, s2], [s])
            P.dma("sp", xres[tok0:tok0 + tn, c0:c0 + ncols], s[0:tn, 0:ncols], reads=[s])
        return e_

    def wout_phase(l):
        jobs = [(c0, 512, "K", resid_epi(2, c0, 512)) for c0 in range(0, D, 512)]
        TB = 2176 if NT >= 2176 else NT
        gemm(yF, D, NT, w_out[l], jobs, TB)

    def ffn_phase(l):
        def epi1(row0):
            def e_(pt, pv, cg, cgn, tok0, tn):
                s = sf()
                act(s[:, 0:tn], pv[:, 0:tn], AF.Relu, [pt], [s])
                sb_ = sbb()
                tt(sb_[:, 0:tn], s[:, 0:tn], s[:, 0:tn], ALU.mult, [s], [sb_])
                P.dma("sp", aF[row0 + cg:row0 + cg + 128, tok0:tok0 + tn], sb_[:, 0:tn], reads=[sb_])
            return e_
        jobs = [(c0, 512, "F", epi1(c0)) for c0 in range(0, DFF, 512)]
        TB = 2176 if NT >= 2176 else NT
        gemm(hF, D, NT, w_ff1[l], jobs, TB)
        for kh in range(2):
            jobs = [(c0, 256, "K", resid_epi(5, c0, 256)) for c0 in range(0, D, 256)]
            gemm(aF[kh * 4096:(kh + 1) * 4096, :], DFF // 2, NT, w_ff2[l][kh * 4096:(kh + 1) * 4096, :], jobs, 1152)

    def final_phase():
        P.arena_reset()
        fn = P.ar([128, D], F32, "fn")
        xt = [P.ar([128, D], F32, f"fxt{i}") for i in range(2)]
        xn = [P.ar([128, D], F32, f"fxn{i}") for i in range(2)]
        junk = P.ar([128, D], BF16, "fjunk")
        P.dma("sp", fn[:], final_norm.partition_broadcast(128), writes=[fn])
        for ti in range(2, NT // 128):
            x_ = xt[ti % 2]
            xn_ = xn[ti % 2]
            P.dma("sp", x_[:], xres[ti * 128:(ti + 1) * 128, :], writes=[x_])
            r = rstd_of(x_, junk)
            ts(xn_[:], x_[:], r, None, ALU.mult, None, [x_, ssq], [xn_])
            tt(xn_[:], xn_[:], fn[:], ALU.mult, [xn_, fn], [xn_])
            P.dma("sp", out[(ti - 2) * 128:(ti - 1) * 128, :], xn_[:], reads=[xn_])

    def program():
        for l in range(2):
            layer_setup(l)
            if stop_after == ("setup", l):
                return
            norm_phase(0)
            if stop_after == ("norm", l):
                return
            win_phase(l)
            if stop_after == ("win", l):
                return
            conv_phase()
            if stop_after == ("conv", l):
                return
            for d in range(2):
                if scan_phase(l, d):
                    return
                if stop_after == ("s_d0", l):
                    return
            if stop_after == ("scan", l):
                return
            merge_phase(l)
            if stop_after == ("merge", l):
                return
            wout_phase(l)
            if stop_after == ("wout", l):
                return
            norm_phase(1)
            ffn_phase(l)
            if stop_after == ("ffn", l):
                return
        final_phase()

    program()
    P.barrier()
    for name in dump:
        t, shape, dtype = SCR[name]
        o = nc.dram_tensor("dbg_" + name, list(shape), dtype, kind="ExternalOutput").ap()
        P.dma("sp", o, t)
    P.finish()
    return nc


_NC_CACHE = {}


def kernel(**inputs):
    x = np.asarray(inputs["x"], np.float32)
    B, NLAT, _ = x.shape
    ctx = np.asarray(inputs["ctx"], np.float32)
    c = np.asarray(inputs["c"], np.float32)
    c_ctx = np.asarray(inputs["c_ctx"], np.float32)
    if NLAT not in _NC_CACHE:
        _NC_CACHE[NLAT] = build(NLAT)
    nc = _NC_CACHE[NLAT]
    consts = host_consts()
    shared = {}
    for k in ("w_mod", "b_mod", "norm_mix", "norm_mlp", "w_in", "hgrn_lb_logits", "hgrn_norm", "ret_decay",
              "ret_norm", "s5_a_re", "s5_a_im", "s5_log_dt", "s5_b_re", "s5_b_im", "s5_c_re", "s5_c_im", "s5_d",
              "s5_glu_w", "s5_glu_b", "mlstm_conv_w", "mlstm_conv_b", "mlstm_norm", "w_branch", "w_out",
              "w_ff1", "w_ff2"):
        shared[k] = np.ascontiguousarray(np.asarray(inputs[k], np.float32))
    shared["ret_decay"] = np.ascontiguousarray(np.asarray(inputs["ret_decay"], np.float32).reshape(2, 8))
    shared["mlstm_gate_b"] = np.ascontiguousarray(np.asarray(inputs["mlstm_gate_b"], np.float32).reshape(2, 16))
    shared["final_norm"] = np.ascontiguousarray(np.asarray(inputs["final_norm"], np.float32).reshape(1, D))
    shared.update(consts)
    in_maps = []
    for core in range(8):
        b = core % B
        m = dict(shared)
        m["xin"] = np.ascontiguousarray(np.concatenate([ctx[b], x[b]], axis=0))
        m["c2"] = np.ascontiguousarray(np.stack([c[b], c_ctx]))
        in_maps.append(m)
    res = run_bass_kernel_spmd(nc, in_maps, core_ids=list(range(8)))
    outs = [np.asarray(res.results[b]["out"], np.float32) for b in range(B)]
    return np.stack(outs, axis=0)
```

```python
import contextlib
import math
import numpy as np
import concourse.bass as bass
import concourse.mybir as mybir
from concourse.bass_utils import run_bass_kernel_spmd

F32 = mybir.dt.float32
BF16 = mybir.dt.bfloat16
I32 = mybir.dt.int32
AF = mybir.ActivationFunctionType
ALU = mybir.AluOpType

ENGS = ("pe", "act", "dve", "pool", "sp")
KSEM = 8
NDQ = 12
SAME_ENGINE_WAITS = True

D = 2048
BR = 512
DFF = 8192
INW = 15376
CTX = 256
EPS = 1e-6


class Tile:
    def __init__(self, t, name=""):
        self.t = t
        self.name = name
        self.writers = {}
        self.readers = {}

    def __getitem__(self, k):
        return self.t[k]


class Prog:
    def __init__(self, nc):
        self.nc = nc
        self.es = contextlib.ExitStack()
        self.streams = {e: [] for e in ENGS}
        self.count = {e: 0 for e in ENGS}
        self.sems = {}
        for e in ENGS:
            self.sems[e] = [self.es.enter_context(nc.semaphore(f"s_{e}{k}")) for k in range(KSEM)]
        self.dq = {}
        for q in ("sp", "pool", "act"):
            self.dq[q] = [self.es.enter_context(nc.semaphore(f"d_{q}{k}")) for k in range(NDQ)]
        self.dq_count = {q: [0] * NDQ for q in self.dq}
        self.dq_next = {q: 0 for q in self.dq}
        self.seen = {e: {} for e in ENGS}
        self.n_sb = 0

    def sb(self, shape, dtype, name=None):
        self.n_sb += 1
        name = "sb_" + (name or f"t{self.n_sb}")
        t = self.es.enter_context(self.nc.sbuf_tensor(name, list(shape), dtype))
        return Tile(t, name)

    def arena_init(self, words):
        self.arena = self.es.enter_context(self.nc.sbuf_tensor("arena", [128, words], F32))
        self.arena_words = words
        self.arena_off = 0

    def arena_reset(self):
        self.arena_off = 0

    def ar(self, shape, dtype, name=None, at=None):
        esz = 4 if dtype in (F32, I32) else 2
        nfree = 1
        for d_ in shape[1:]:
            nfree *= d_
        words = (nfree * esz + 31) // 32 * 8
        if at is None:
            assert self.arena_off + words <= self.arena_words, (name, self.arena_off, words)
            off = self.arena_off
            self.arena_off += words
        else:
            off = at
        v = self.arena[0:shape[0], off:off + words]
        if dtype != F32:
            v = v.bitcast(dtype)
        v = v[:, 0:nfree]
        if len(shape) == 3:
            v = v.rearrange("p (a b) -> p a b", a=shape[1])
        elif len(shape) == 4:
            v = v.rearrange("p (a b c) -> p a b c", a=shape[1], b=shape[2])
        tl_ = Tile(v, name or "ar")
        tl_.off, tl_.words = off, words
        return tl_

    def ps(self, shape, dtype=F32, name=None):
        self.n_sb += 1
        name = name or f"ps{self.n_sb}"
        t = self.es.enter_context(self.nc.psum_tensor(name, list(shape), dtype))
        return Tile(t, name)

    def dram(self, name, shape, dtype):
        t = self.nc.dram_tensor(name, list(shape), dtype)
        return Tile(t.ap(), name)

    def _sem_of(self, stream, idx):
        if stream in ENGS:
            return self.sems[stream][idx % KSEM], idx // KSEM + 1
        q, k = stream
        return self.dq[q][k], 16 * (idx + 1)

    def _need(self, eng, stream, idx, is_dma=False):
        if stream == eng and (eng == "pe" or not (SAME_ENGINE_WAITS or is_dma)):
            return
        if self.seen[eng].get(stream, -1) >= idx:
            return
        self.seen[eng][stream] = idx
        sem, val = self._sem_of(stream, idx)
        self.streams[eng].append(lambda e, sem=sem, val=val: e.wait_ge(sem, val))

    def _deps(self, eng, reads, writes, is_dma=False):
        for t in reads:
            for s, i in t.writers.items():
                self._need(eng, s, i, is_dma)
        for t in writes:
            for s, i in t.writers.items():
                self._need(eng, s, i, is_dma)
            for s, i in t.readers.items():
                self._need(eng, s, i, is_dma)

    def op(self, eng, fn, reads=(), writes=(), sig=True):
        self._deps(eng, reads, writes)
        idx = self.count[eng]
        if sig:
            self.count[eng] += 1
            sem = self.sems[eng][idx % KSEM]
            self.streams[eng].append(lambda e, fn=fn, sem=sem: fn(e).then_inc(sem, 1))
        else:
            self.streams[eng].append(lambda e, fn=fn: fn(e))
        for t in reads:
            t.readers[eng] = idx
        for t in writes:
            t.writers = {eng: idx}
            t.readers = {}
        return idx

    def dma(self, q, out, in_, reads=(), writes=(), **kw):
        self._deps(q, reads, writes, True)
        k = self.dq_next[q]
        self.dq_next[q] = (k + 1) % NDQ
        j = self.dq_count[q][k]
        if j > 0:
            self._need(q, (q, k), j - 1)
        self.dq_count[q][k] += 1
        sem = self.dq[q][k]
        self.streams[q].append(
            lambda e, sem=sem, out=out, in_=in_, kw=kw: e.dma_start(out=out, in_=in_, **kw).then_inc(sem, 16))
        st = (q, k)
        for t in reads:
            t.readers[st] = j
        for t in writes:
            t.writers = {st: j}
            t.readers = {}

    def barrier(self):
        for e in ENGS:
            for s in ENGS:
                if s != e and self.count[s] > 0:
                    self._need(e, s, self.count[s] - 1)
            for q in self.dq:
                for k in range(NDQ):
                    if self.dq_count[q][k] > 0:
                        self._need(e, (q, k), self.dq_count[q][k] - 1)

    def finish(self):
        self.barrier()
        nc = self.nc
        with nc.Block() as block:
            @block.tensor
            def _(e):
                for f in self.streams["pe"]:
                    f(e)

            @block.scalar
            def _(e):
                for f in self.streams["act"]:
                    f(e)

            @block.vector
            def _(e):
                for f in self.streams["dve"]:
                    f(e)

            @block.gpsimd
            def _(e):
                for f in self.streams["pool"]:
                    f(e)

            @block.sync
            def _(e):
                for f in self.streams["sp"]:
                    f(e)
        self.es.close()


def host_consts():
    t = np.arange(64, dtype=np.float32)
    c = {}
    c["ident"] = np.eye(128, dtype=np.float32)
    triU = (t[:, None] <= t[None, :]).astype(np.float32)
    triL = (t[:, None] >= t[None, :]).astype(np.float32)
    c["tri"] = np.stack([triU, triL])
    c["iotaF"] = np.stack([np.tile(t + 1, (128, 1)), np.tile(64 - t, (128, 1))]).astype(np.float32)
    c["iotaK"] = np.stack([np.tile((t + 1)[:, None], (1, 128)), np.tile((64 - t)[:, None], (1, 128))]).astype(np.float32)
    rst = np.ones((128, 16, 64), np.float32)
    rst[:, :, 0] = 0.0
    c["rst"] = rst
    return c


def build(NLAT, dump=(), stop_after=None):
    nc = bass.Bass("TRN2", target_bir_lowering=False)
    P = Prog(nc)
    NT = CTX + NLAT
    NSC = NT // 256
    ROWS = NLAT // 64

    def din(name, shape):
        return nc.dram_tensor(name, list(shape), F32, kind="ExternalInput").ap()

    xin = din("xin", [NT, D])
    c2 = din("c2", [2, D])
    w_mod = din("w_mod", [2, D, 6 * D])
    b_mod = din("b_mod", [2, 6 * D])
    norm_mix = din("norm_mix", [2, D])
    norm_mlp = din("norm_mlp", [2, D])
    w_in = din("w_in", [2, D, INW])
    lb_logits = din("hgrn_lb_logits", [2, 2, BR])
    hgrn_norm = din("hgrn_norm", [2, BR])
    ret_decay = din("ret_decay", [2, 8])
    ret_norm = din("ret_norm", [2, BR])
    s5_a_re = din("s5_a_re", [2, 2, 32, 64])
    s5_a_im = din("s5_a_im", [2, 2, 32, 64])
    s5_log_dt = din("s5_log_dt", [2, 2, 32])
    s5_b_re = din("s5_b_re", [2, 2, 32, 64, 16])
    s5_b_im = din("s5_b_im", [2, 2, 32, 64, 16])
    s5_c_re = din("s5_c_re", [2, 2, 32, 16, 64])
    s5_c_im = din("s5_c_im", [2, 2, 32, 16, 64])
    s5_d = din("s5_d", [2, BR])
    s5_glu_w = din("s5_glu_w", [2, BR, BR])
    s5_glu_b = din("s5_glu_b", [2, BR])
    conv_w = din("mlstm_conv_w", [2, 3, 3, 2 * BR])
    conv_b = din("mlstm_conv_b", [2, 2 * BR])
    gate_b = din("mlstm_gate_b", [2, 16])
    mlstm_norm = din("mlstm_norm", [2, BR])
    w_branch = din("w_branch", [2, 4, BR, D])
    w_out = din("w_out", [2, D, D])
    w_ff1 = din("w_ff1", [2, D, DFF])
    w_ff2 = din("w_ff2", [2, DFF, D])
    final_norm = din("final_norm", [1, D])
    k_ident = din("ident", [128, 128])
    k_tri = din("tri", [2, 64, 64])
    k_iotaF = din("iotaF", [2, 128, 64])
    k_iotaK = din("iotaK", [2, 64, 128])
    k_rst = din("rst", [128, 16, 64])
    out = nc.dram_tensor("out", [NLAT, D], F32, kind="ExternalOutput").ap()

    SCR = {}

    def scratch(name, shape, dtype):
        t = nc.dram_tensor(name, list(shape), dtype).ap()
        SCR[name] = (t, shape, dtype)
        return t

    xres = scratch("xres", [NT, D], F32)
    hF = scratch("hF", [D, NT], BF16)
    scF = scratch("scF", [D, 2], BF16)
    mod_d = scratch("mod_d", [2, 6 * D], F32)
    FB = {}
    for nm in ("hq", "hkF0", "hkF1", "hog", "rq", "rkF", "rog", "uF", "mz", "mq", "mk"):
        FB[nm] = scratch(nm, [BR, NT], BF16)
    mqk = scratch("mqk", [2 * BR, NT], F32)
    gF = scratch("gF", [4 * D, NT], BF16)
    KB = {}
    for nm in ("hkK0", "hkK1", "hv", "rkK", "rv", "mv"):
        KB[nm] = scratch(nm, [NT, BR], BF16)
    hlogf = [scratch("hlogf0", [NT, BR], F32), scratch("hlogf1", [NT, BR], F32)]
    mg = scratch("mg", [NT, 16], F32)
    rawf = [scratch(f"rawf{j}", [BR, NT], F32) for j in range(4)]
    oF = scratch("oF", [4 * BR, NT], BF16)
    yF = scratch("yF", [D, NT], BF16)
    yaccd = scratch("yaccd", [D, NT], F32)
    aF = scratch("aF", [DFF, NT], BF16)

    PB = [P.ps([128, 1024], F32, name=f"pb{i}") for i in range(4)]
    slot_order = [0, 2, 4, 6, 1, 3, 5, 7]
    st = {"slot": 0, "w": 0, "sf": 0, "sb": 0}

    def next_ps():
        s = slot_order[st["slot"] % 8]
        st["slot"] += 1
        tl = PB[s // 2]
        return tl, tl.t[:, (s % 2) * 512:(s % 2) * 512 + 512]

    stg_f = [P.sb([128, 512], F32, f"stgf{i}") for i in range(4)]
    stg_b = [P.sb([128, 512], BF16, f"stgb{i}") for i in range(4)]

    def sf():
        st["sf"] += 1
        return stg_f[st["sf"] % 4]

    def sbb():
        st["sb"] += 1
        return stg_b[st["sb"] % 4]

    ident = P.sb([128, 128], F32, "ident")
    ident_b = P.sb([128, 128], BF16, "identb")
    ones_f = P.sb([128, 128], F32, "onesf")
    ones_b = P.sb([128, 128], BF16, "onesb")
    tri = [P.sb([64, 64], F32, f"tri{d}") for d in range(2)]
    ntri = [P.sb([64, 64], F32, f"ntri{d}") for d in range(2)]
    iotaF = [P.sb([128, 64], F32, f"iotaF{d}") for d in range(2)]
    iotaK = [P.sb([64, 128], F32, f"iotaK{d}") for d in range(2)]
    rst = P.sb([128, 16, 64], F32, "rst")
    modcol = P.sb([128, 2, 6, 16], F32, "modcol")
    gcolA = P.sb([128, 2, 16], F32, "gcolA")
    ngc = [P.sb([128, 16], F32, f"ngc{i}") for i in range(2)]
    lbrow = [P.sb([128, BR], F32, f"lbrow{d}") for d in range(2)]
    omlrow = [P.sb([128, BR], F32, f"omlrow{d}") for d in range(2)]
    omlcol = [P.sb([128, 4], F32, f"omlcol{d}") for d in range(2)]
    hn_col = [P.sb([128, 4], F32, f"hncol{j}") for j in range(3)]
    gb_row = P.sb([128, 16], F32, "gbrow")
    lgB = P.sb([128, 8], F32, "lgB")
    nlgB = P.sb([128, 8], F32, "nlgB")
    cw = P.sb([128, 8, 9], F32, "cw")
    cbias = P.sb([128, 8], F32, "cbias")
    s5d_col = P.sb([128, 4], F32, "s5dcol")
    glub_col = P.sb([128, 4], F32, "glubcol")
    gluw = P.sb([128, 4, BR], BF16, "gluw")
    ssq = P.sb([128, 4], F32, "ssq")
    P.arena_init(44500)

    def act(out_ap, in_ap, func, reads, writes, **kw):
        P.op("act", lambda e: e.activation(out_ap, in_ap, func, **kw), reads=reads, writes=writes)

    def tt(out_ap, a, b, op, reads, writes, eng="dve"):
        P.op(eng, lambda e: e.tensor_tensor(out_ap, a, b, op), reads=reads, writes=writes)

    def ts(out_ap, a, s1, s2, op0, op1, reads, writes):
        if op1 is None:
            P.op("dve", lambda e: e.tensor_scalar(out_ap, a, s1, None, op0), reads=reads, writes=writes)
        else:
            P.op("dve", lambda e: e.tensor_scalar(out_ap, a, s1, s2, op0, op1), reads=reads, writes=writes)

    def cp(out_ap, in_ap, reads, writes, eng="dve"):
        P.op(eng, lambda e: e.tensor_copy(out_ap, in_ap), reads=reads, writes=writes)

    def mm(out_ap, lhsT, rhs, start, stop, reads, writes):
        P.op("pe", lambda e: e.matmul(out_ap, lhsT=lhsT, rhs=rhs, start=start, stop=stop), reads=reads, writes=writes, sig=bool(stop))

    def memset(tile_, ap, val, eng="dve"):
        P.op(eng, lambda e: e.memset(ap, val), writes=[tile_])

    def recip(tile_, ap):
        P.op("dve", lambda e: e.reciprocal(ap, ap), reads=[tile_], writes=[tile_])

    def small_dma(out_ap, in_ap, tile_, q="sp"):
        P.dma(q, out_ap, in_ap, writes=[tile_], allow_slow_non_contiguous=True)

    P.dma("sp", ident[:], k_ident, writes=[ident])
    cp(ident_b[:], ident[:], [ident], [ident_b])
    memset(ones_f, ones_f[:], 1.0)
    memset(ones_b, ones_b[:], 1.0)
    for d in range(2):
        P.dma("sp", tri[d][:], k_tri[d], writes=[tri[d]])
        ts(ntri[d][:], tri[d][:], -1.0, None, ALU.mult, None, [tri[d]], [ntri[d]])
        P.dma("sp", iotaF[d][:], k_iotaF[d], writes=[iotaF[d]])
        P.dma("sp", iotaK[d][:], k_iotaK[d], writes=[iotaK[d]])
    P.dma("sp", rst[:], k_rst, writes=[rst])
    P.dma("sp", xres, xin)
    P.barrier()

    def gemm(A, Kd, ntok, Wap, jobs, TBLK):
        P.arena_reset()
        ablk = P.ar([128, 36864], BF16, "ablk")
        wbuf = [P.ar([128, 8192], BF16, f"wbuf{i}") for i in range(2)]
        KC = Kd // 128
        for tb0 in range(0, ntok, TBLK):
            tbn = min(TBLK, ntok - tb0)
            av = ablk.t[:, 0:KC * tbn].rearrange("p (k n) -> p k n", k=KC)
            P.dma("sp", av, A[:, tb0:tb0 + tbn].rearrange("(k p) n -> p k n", p=128), writes=[ablk])
            for (col0, ncols, mode, epi) in jobs:
                wt = wbuf[st["w"] % 2]
                st["w"] += 1
                wv = wt.t[:, 0:KC * ncols].rearrange("p (k n) -> p k n", k=KC)
                P.dma("pool", wv, Wap[:, col0:col0 + ncols].rearrange("(k p) n -> p k n", p=128), writes=[wt])
                if mode == "F":
                    for cg in range(0, ncols, 128):
                        cgn = min(128, ncols - cg)
                        for t0 in range(0, tbn, 512):
                            tn = min(512, tbn - t0)
                            pt, pv = next_ps()
                            for k in range(KC):
                                mm(pv[0:cgn, 0:tn], wv[:, k, cg:cg + cgn], av[:, k, t0:t0 + tn], k == 0, k == KC - 1,
                                   [wt, ablk], [pt])
                            epi(pt, pv, cg, cgn, tb0 + t0, tn)
                else:
                    for t0 in range(0, tbn, 128):
                        tn = min(128, tbn - t0)
                        pt, pv = next_ps()
                        for k in range(KC):
                            mm(pv[0:tn, 0:ncols], av[:, k, t0:t0 + tn], wv[:, k, 0:ncols], k == 0, k == KC - 1,
                               [wt, ablk], [pt])
                        epi(pt, pv, tb0 + t0, tn)
        P.barrier()

    def epiF_simple(dst, row0, func, scale=1.0, dtype=BF16):
        def epi(pt, pv, cg, cgn, tok0, tn):
            s = sbb() if dtype == BF16 else sf()
            act(s[0:cgn, 0:tn], pv[0:cgn, 0:tn], func, [pt], [s], scale=scale)
            P.dma("sp", dst[row0 + cg:row0 + cg + cgn, tok0:tok0 + tn], s[0:cgn, 0:tn], reads=[s])
        return epi

    def epiK_simple(dst, func, scale=1.0, dtype=BF16):
        def epi(pt, pv, tok0, tn):
            s = sbb() if dtype == BF16 else sf()
            nco = dst.shape[1]
            act(s[0:tn, 0:nco], pv[0:tn, 0:nco], func, [pt], [s], scale=scale)
            P.dma("sp", dst[tok0:tok0 + tn, :], s[0:tn, 0:nco], reads=[s])
        return epi

    def layer_setup(l):
        cc = sf()
        ccv = cc[:, 0:32].rearrange("p (j v) -> p j v", v=2)
        for v in range(2):
            small_dma(ccv[:, :, v], c2[v].rearrange("(j p) -> p j", p=128), cc)
        cb = sbb()
        cbv = cb[:, 0:32].rearrange("p (j v) -> p j v", v=2)
        act(cbv, ccv, AF.Silu, [cc], [cb])
        P.dma("sp", scF.rearrange("(j p) v -> p j v", p=128), cbv, reads=[cb], allow_slow_non_contiguous=True)
        P.barrier()

        def epi_mod_factory(col0):
            def epi(pt, pv, tok0, tn):
                bm = sf()
                P.dma("act", bm[0:2, :], b_mod[l:l + 1, col0:col0 + 512].partition_broadcast(2), writes=[bm])
                s = sf()
                tt(s[0:2, 0:512], pv[0:2, 0:512], bm[0:2, :], ALU.add, [pt, bm], [s])
                P.dma("sp", mod_d[:, col0:col0 + 512], s[0:2, 0:512], reads=[s])
            return epi
        jobs = [(c0, 512, "K", epi_mod_factory(c0)) for c0 in range(0, 6 * D, 512)]
        gemm(scF, D, 2, w_mod[l], jobs, 128)
        for v in range(2):
            for m_ in range(6):
                small_dma(modcol[:, v, m_, :], mod_d[v, m_ * D:(m_ + 1) * D].rearrange("(j p) -> p j", p=128), modcol)
        small_dma(ngc[0][:], norm_mix[l].rearrange("(j p) -> p j", p=128), ngc[0])
        small_dma(ngc[1][:], norm_mlp[l].rearrange("(j p) -> p j", p=128), ngc[1])
        for d in range(2):
            if l == 0:
                memset(lbrow[d], lbrow[d][:], 0.0)
                memset(omlrow[d], omlrow[d][:], 1.0)
                memset(omlcol[d], omlcol[d][:], 1.0)
            else:
                a0 = sf()
                a1 = sf()
                P.dma("sp", a0[:], lb_logits[0, d:d + 1, :].partition_broadcast(128), writes=[a0])
                P.dma("sp", a1[:], lb_logits[1, d:d + 1, :].partition_broadcast(128), writes=[a1])
                tt(a1[:], a1[:], a0[:], ALU.subtract, [a0, a1], [a1])
                act(lbrow[d][:], a1[:], AF.Sigmoid, [a1], [lbrow[d]])
                ts(omlrow[d][:], lbrow[d][:], -1.0, 1.0, ALU.mult, ALU.add, [lbrow[d]], [omlrow[d]])
                c0_ = sf()
                small_dma(c0_[:, 0:4], lb_logits[0, d].rearrange("(h p) -> p h", p=128), c0_)
                small_dma(c0_[:, 4:8], lb_logits[1, d].rearrange("(h p) -> p h", p=128), c0_)
                tt(c0_[:, 8:12], c0_[:, 0:4], c0_[:, 4:8], ALU.subtract, [c0_], [c0_])
                act(omlcol[d][:], c0_[:, 8:12], AF.Sigmoid, [c0_], [omlcol[d]])
        for j, g in enumerate((hgrn_norm, ret_norm, mlstm_norm)):
            small_dma(hn_col[j][:], g[l].rearrange("(h p) -> p h", p=128), hn_col[j])
        P.dma("sp", gb_row[:], gate_b[l:l + 1, :].partition_broadcast(128), writes=[gb_row])
        P.dma("sp", lgB[:], ret_decay[l:l + 1, :].partition_broadcast(128), writes=[lgB])
        act(lgB[:], lgB[:], AF.Exp, [lgB], [lgB])
        ts(lgB[:], lgB[:], -1.0, 1.0, ALU.mult, ALU.add, [lgB], [lgB])
        act(lgB[:], lgB[:], AF.Ln, [lgB], [lgB])
        ts(nlgB[:], lgB[:], -1.0, None, ALU.mult, None, [lgB], [nlgB])
        for cc_ in range(8):
            small_dma(cw[:, cc_, :], conv_w[l][:, :, cc_ * 128:(cc_ + 1) * 128].rearrange("a b p -> p (a b)"), cw)
        small_dma(cbias[:], conv_b[l].rearrange("(cc p) -> p cc", p=128), cbias)
        small_dma(s5d_col[:], s5_d[l].rearrange("(cc p) -> p cc", p=128), s5d_col)
        small_dma(glub_col[:], s5_glu_b[l].rearrange("(cc p) -> p cc", p=128), glub_col)
        P.dma("pool", gluw[:], s5_glu_w[l].rearrange("(k p) n -> p k n", p=128), writes=[gluw])
        P.barrier()

    def rstd_of(tile_x, junk):
        memset(ssq, ssq[:, 0:1], 0.0)
        act(junk[:], tile_x[:], AF.Square, [tile_x, ssq], [junk, ssq], accum_out=ssq[:, 0:1])
        act(ssq[:, 1:2], ssq[:, 0:1], AF.Ln, [ssq], [ssq], scale=1.0 / D, bias=EPS)
        act(ssq[:, 2:3], ssq[:, 1:2], AF.Exp, [ssq], [ssq], scale=-0.5)
        return ssq[:, 2:3]

    def norm_phase(which):
        P.arena_reset()
        xt = [P.ar([128, D], F32, f"xt{i}") for i in range(2)]
        xn = P.ar([128, D], F32, "xn")
        hT = [P.ar([128, 16, 128], BF16, f"hT{i}") for i in range(2)]
        junk = P.ar([128, D], BF16, "junk")
        m_shift, m_scale = (0, 1) if which == 0 else (3, 4)
        for v in range(2):
            ts(gcolA[:, v, :], modcol[:, v, m_scale, :], 1.0, None, ALU.add, None, [modcol], [gcolA])
            tt(gcolA[:, v, :], gcolA[:, v, :], ngc[which][:], ALU.mult, [gcolA, ngc[which]], [gcolA])
        for ti in range(NT // 128):
            v = 1 if ti < 2 else 0
            x_ = xt[ti % 2]
            P.dma("sp", x_[:], xres[ti * 128:(ti + 1) * 128, :], writes=[x_])
            r = rstd_of(x_, junk)
            ts(xn[:], x_[:], r, None, ALU.mult, None, [x_, ssq], [xn])
            h_ = hT[ti % 2]
            for j in range(16):
                pt, pv = next_ps()
                mm(pv[:, 0:128], xn[:, j * 128:(j + 1) * 128], ident[:], True, True, [xn, ident], [pt])
                act(h_[:, j, :], pv[:, 0:128], AF.Identity, [pt, gcolA, modcol], [h_],
                    scale=gcolA[:, v, j:j + 1], bias=modcol[:, v, m_shift, j:j + 1])
            P.dma("sp", hF[:, ti * 128:(ti + 1) * 128].rearrange("(j p) n -> p j n", p=128), h_[:], reads=[h_])
        P.barrier()

    def win_phase(l):
        jobs = []
        jobs.append((0, 512, "F", epiF_simple(FB["hq"], 0, AF.Silu)))

        def epi_hk(d):
            def epi(pt, pv, cg, cgn, tok0, tn):
                s = sf()
                act(s[:, 0:tn], pv[:, 0:tn], AF.Sigmoid, [pt], [s], scale=-1.0)
                sb_ = sbb()
                h = cg // 128
                ts(sb_[:, 0:tn], s[:, 0:tn], omlcol[d][:, h:h + 1], None, ALU.mult, None, [s, omlcol[d]], [sb_])
                P.dma("sp", FB[f"hkF{d}"][cg:cg + 128, tok0:tok0 + tn], sb_[:, 0:tn], reads=[sb_])
            return epi
        jobs.append((512, 512, "F", epi_hk(0)))
        jobs.append((1024, 512, "F", epi_hk(1)))
        jobs.append((2048, 512, "F", epiF_simple(FB["hog"], 0, AF.Silu)))
        jobs.append((2560, 512, "F", epiF_simple(FB["rq"], 0, AF.Identity)))
        jobs.append((3072, 512, "F", epiF_simple(FB["rkF"], 0, AF.Identity, scale=128 ** -0.5)))
        jobs.append((4096, 512, "F", epiF_simple(FB["rog"], 0, AF.Silu)))
        jobs.append((4608, 512, "F", epiF_simple(FB["uF"], 0, AF.Identity)))
        jobs.append((5120, 512, "F", epiF_simple(mqk, 0, AF.Identity, dtype=F32)))
        jobs.append((5632, 512, "F", epiF_simple(mqk, 512, AF.Identity, dtype=F32)))
        jobs.append((6656, 512, "F", epiF_simple(FB["mz"], 0, AF.Silu)))
        for gblk in range(16):
            jobs.append((7184 + gblk * 512, 512, "F", epiF_simple(gF, gblk * 512, AF.Sigmoid)))

        def epi_hf(d):
            def epi(pt, pv, tok0, tn):
                s = sf()
                act(s[0:tn, :], pv[0:tn, :], AF.Sigmoid, [pt], [s])
                tt(s[0:tn, :], s[0:tn, :], omlrow[d][0:tn, :], ALU.mult, [s, omlrow[d]], [s])
                tt(s[0:tn, :], s[0:tn, :], lbrow[d][0:tn, :], ALU.add, [s, lbrow[d]], [s])
                kb = sbb()
                ts(kb[0:tn, :], s[0:tn, :], -1.0, 1.0, ALU.mult, ALU.add, [s], [kb])
                P.dma("sp", KB[f"hkK{d}"][tok0:tok0 + tn, :], kb[0:tn, :], reads=[kb])
                ts(s[0:tn, :], s[0:tn, :], 1e-30, None, ALU.max, None, [s], [s])
                s2 = sf()
                act(s2[0:tn, :], s[0:tn, :], AF.Ln, [s], [s2])
                P.dma("sp", hlogf[d][tok0:tok0 + tn, :], s2[0:tn, :], reads=[s2])
            return epi
        jobs.append((512, 512, "K", epi_hf(0)))
        jobs.append((1024, 512, "K", epi_hf(1)))
        jobs.append((1536, 512, "K", epiK_simple(KB["hv"], AF.Identity)))
        jobs.append((3072, 512, "K", epiK_simple(KB["rkK"], AF.Identity, scale=128 ** -0.5)))
        jobs.append((3584, 512, "K", epiK_simple(KB["rv"], AF.Identity)))
        jobs.append((6144, 512, "K", epiK_simple(KB["mv"], AF.Identity)))

        def epi_mg(pt, pv, tok0, tn):
            s = sf()
            tt(s[0:tn, 0:16], pv[0:tn, 0:16], gb_row[0:tn, :], ALU.add, [pt, gb_row], [s])
            sv = s[0:tn, 0:16].rearrange("p (d i h) -> p d i h", d=2, i=2)
            s2 = sf()
            s2v = s2[0:tn, 0:16].rearrange("p (d i h) -> p d i h", d=2, i=2)
            act(s2v[:, :, 1, :], sv[:, :, 1, :], AF.Sigmoid, [s], [s2])
            act(sv[:, :, 1, :], s2v[:, :, 1, :], AF.Ln, [s2], [s])
            P.dma("sp", mg[tok0:tok0 + tn, :], s[0:tn, 0:16], reads=[s])
        jobs.append((7168, 16, "K", epi_mg))
        TB = 2176 if NT >= 2176 else NT
        gemm(hF, D, NT, w_in[l], jobs, TB)

    def conv_phase():
        P.arena_reset()
        cin = P.ar([128, NT], F32, "cin")
        cout = P.ar([128, NT], F32, "cout")
        cob = P.ar([128, NT], BF16, "cob")
        for cc in range(8):
            P.dma("sp", cin[:], mqk[cc * 128:(cc + 1) * 128, :], writes=[cin])

            def w(a, b, cc=cc):
                return cw[:, cc, a * 3 + b:a * 3 + b + 1]
            ts(cout[:], cin[:], w(1, 1), cbias[:, cc:cc + 1], ALU.mult, ALU.add, [cin, cw, cbias], [cout])

            def acc(o_ap, i_ap, wap):
                P.op("dve", lambda e: e.scalar_tensor_tensor(o_ap, i_ap, wap, o_ap, ALU.mult, ALU.add),
                     reads=[cin, cout, cw], writes=[cout])
            acc(cout[:, 1:CTX], cin[:, 0:CTX - 1], w(1, 0))
            acc(cout[:, 0:CTX - 1], cin[:, 1:CTX], w(1, 2))
            ci = cin[:, CTX:NT].rearrange("p (r c) -> p r c", c=64)
            co = cout[:, CTX:NT].rearrange("p (r c) -> p r c", c=64)
            for dr in (-1, 0, 1):
                for dc in (-1, 0, 1):
                    if dr == 0 and dc == 0:
                        continue
                    ro = slice(max(0, -dr), ROWS - max(0, dr))
                    ri = slice(max(0, dr), ROWS - max(0, -dr))
                    co_ = slice(max(0, -dc), 64 - max(0, dc))
                    ci_ = slice(max(0, dc), 64 - max(0, -dc))
                    acc(co[:, ro, co_], ci[:, ri, ci_], w(dr + 1, dc + 1))
            act(cob[:], cout[:], AF.Silu, [cout], [cob])
            if cc < 4:
                P.dma("sp", FB["mq"][cc * 128:(cc + 1) * 128, :], cob[:], reads=[cob])
            else:
                ts(cob[:], cob[:], 128 ** -0.5, None, ALU.mult, None, [cob], [cob])
                P.dma("sp", FB["mk"][(cc - 4) * 128:(cc - 3) * 128, :], cob[:], reads=[cob])
        P.barrier()

    def sc_order(d):
        if d == 0:
            return list(range(NSC))
        return [0] + list(range(NSC - 1, 0, -1))

    def cmul(or_, oi_, ar, ai, br, bi, t1, t2, reads, writes):
        wr, wi = writes[0:1], writes[1:2]
        if writes[0] is writes[1]:
            tt(t1[1], ar, br, ALU.mult, reads, [t1[0]])
            tt(t2[1], ai, bi, ALU.mult, reads, [t2[0]])
            tt(or_, t1[1], t2[1], ALU.subtract, [t1[0], t2[0]], wr)
            tt(t1[1], ar, bi, ALU.mult, reads, [t1[0]])
            tt(t2[1], ai, br, ALU.mult, reads, [t2[0]])
            tt(oi_, t1[1], t2[1], ALU.add, [t1[0], t2[0]], wi)
            return
        tt(or_, ar, br, ALU.mult, reads, wr)
        tt(t1[1], ai, bi, ALU.mult, reads, [t1[0]])
        tt(oi_, ar, bi, ALU.mult, reads, wi)
        tt(t2[1], ai, br, ALU.mult, reads, [t2[0]])
        tt(or_, or_, t1[1], ALU.subtract, [writes[0], t1[0]], wr)
        tt(oi_, oi_, t2[1], ALU.add, [writes[1], t2[0]], wi)

    def scan_phase(l, d):
        P.arena_reset()
        tl = 63 if d == 0 else 0
        A = P.ar
        qF = A([128, 4, 256], BF16, "qF")
        kF = A([128, 4, 256], BF16, "kF")
        kK = A([64, 4, 512], BF16, "kK")
        vK = A([64, 4, 4, 129], BF16, "vK")
        lf = A([64, 4, 512], F32, "lf")
        g = A([64, 4, 16], F32, "gK")
        EpF = A([128, 4, 256], F32, "EpF")
        EmF = A([128, 4, 256], F32, "EmF")
        EkK = A([64, 4, 512], F32, "EkK")
        Ekm = A([64, 4, 4], F32, "Ekm")
        Et = A([128, 4, 4], F32, "Et")
        qp = A([128, 4, 256], BF16, "qp")
        kpF = A([128, 4, 256], BF16, "kpF")
        kpK = A([64, 4, 512], BF16, "kpK")
        PT = A([64, 16, 64], BF16, "PT")
        Sst = A([128, 4, 129], F32, "Sst")
        Sbf = A([128, 4, 129], BF16, "Sbf")
        nbc = A([128, 4, 128], BF16, "nbc")
        Otile = A([128, 4, 256], F32, "Otile")
        Dtile = A([128, 4, 256], F32, "Dtile")
        rawp = A([128, 4, 256], F32, "rawp")
        sq_t = A([128, 4, 256], F32, "sq_t")
        gate_t = A([128, 4, 256], BF16, "gate_t")
        o_bf = A([128, 4, 256], BF16, "o_bf")
        CPt = A([128, 4, 16], F32, "CPt")
        stri = A([64, 64], F32, "stri")
        rEp = A([128, 4, 256], F32, "rEp")
        rEm = A([128, 4, 256], F32, "rEm")
        rEk = A([64, 4, 512], F32, "rEk")
        rEt = A([128, 4, 4], F32, "rEt")
        qpb = A([128, 4, 256], BF16, "qpb", at=kpF.off)
        kpI = [A([128, 4, 256], BF16, "kpI0", at=rEp.off), A([128, 4, 256], BF16, "kpI1", at=rEp.off + 512),
               A([128, 4, 256], BF16, "kpI2", at=rEm.off), A([128, 4, 256], BF16, "kpI3", at=rEm.off + 512)]
        LPr = A([128, 16, 64], F32, "LPr")
        LPi = A([128, 16, 64], F32, "LPi")
        LMr = A([128, 16, 64], F32, "LMr")
        LMi = A([128, 16, 64], F32, "LMi")
        Bpad = A([128, 16, 2, 128], BF16, "Bpad")
        Cpad = A([128, 16, 2, 128], BF16, "Cpad")
        Zr = A([128, 16, 64], F32, "Zr")
        Zi = A([128, 16, 64], F32, "Zi")
        Wr = A([128, 16, 64], F32, "Wr")
        Wi = A([128, 16, 64], F32, "Wi")
        T1 = A([128, 16, 64], F32, "T1")
        T2 = A([128, 16, 64], F32, "T2")
        Xrb = A([128, 16, 64], BF16, "Xrb")
        Xib = A([128, 16, 64], BF16, "Xib")
        xpr = A([128, 16], F32, "xpr")
        xpi = A([128, 16], F32, "xpi")
        tot = A([128, 16], F32, "tot")
        u = A([128, 4, 256], BF16, "u")
        yc_b = A([128, 4, 256], BF16, "yc_b")
        sm = [A([128, 16], F32, f"sm{i}") for i in range(14)]
        smi = A([128, 16], I32, "smi")

        def ret_tables():
            for h in range(4):
                c_ = d * 4 + h
                for c in range(4):
                    act(rEp[:, h, c * 64:(c + 1) * 64], iotaF[d][:], AF.Exp, [iotaF[d], lgB], [rEp], scale=lgB[:, c_:c_ + 1])
                    act(rEm[:, h, c * 64:(c + 1) * 64], iotaF[d][:], AF.Exp, [iotaF[d], nlgB], [rEm], scale=nlgB[:, c_:c_ + 1])
                    act(rEk[:, c, h * 128:(h + 1) * 128], iotaK[d][:], AF.Exp, [iotaK[d], nlgB], [rEk], scale=nlgB[0:64, c_:c_ + 1])
                    cp(rEt[:, c, h:h + 1], rEp[:, h, tl:tl + 1], [rEp], [rEt])

        tt(stri[:], tri[1 - d][:], ident[0:64, 0:64], ALU.subtract, [tri[1 - d], ident], [stri])

        def sin_of(dst_t, theta_t, tmp):
            ts(tmp[:], theta_t[:], 1.0 / (2 * math.pi), None, ALU.mult, None, [theta_t], [tmp])
            cp(smi[:], tmp[:], [tmp], [smi])
            cp(tmp[:], smi[:], [smi], [tmp])
            P.op("dve", lambda e: e.scalar_tensor_tensor(tmp[:], tmp[:], -2 * math.pi, theta_t[:], ALU.mult, ALU.add),
                 reads=[tmp, theta_t], writes=[tmp])
            act(dst_t[:], tmp[:], AF.Sin, [tmp], [dst_t])

        def pow_table(Tr, Ti, br, bi, rev):
            c0 = 63 if rev else 0
            cp(Tr[:, :, c0], br[:], [br], [Tr])
            cp(Ti[:, :, c0], bi[:], [bi], [Ti])
            n = 1
            while n < 64:
                if not rev:
                    src, dst, top = slice(0, n), slice(n, 2 * n), n - 1
                else:
                    src, dst, top = slice(64 - n, 64), slice(64 - 2 * n, 64 - n), 64 - n
                fr_ = Tr[:, :, top:top + 1].to_broadcast([128, 16, n])
                fi_ = Ti[:, :, top:top + 1].to_broadcast([128, 16, n])
                cmul(Tr[:, :, dst], Ti[:, :, dst], Tr[:, :, src], Ti[:, :, src], fr_, fi_,
                     (T1, T1[:, :, 0:n]), (T2, T2[:, :, 0:n]), [Tr, Ti], [Tr, Ti])
                n *= 2

        are, aim, dt_, t0_, t1_, lre, lim, den, fr, fi, ire, iim, t2_, t3_ = sm
        for (tile_, src) in ((are, s5_a_re), (aim, s5_a_im)):
            v_ = src[l, d].rearrange("(j two) p -> two p j", two=2)
            for two in range(2):
                small_dma(tile_[64 * two:64 * two + 64, :], v_[two], tile_)
        ldt = s5_log_dt[l, d].rearrange("(j two) -> two j", two=2)
        for two in range(2):
            small_dma(dt_[64 * two:64 * two + 64, :], ldt[two:two + 1, :].partition_broadcast(64), dt_)
        act(dt_[:], dt_[:], AF.Exp, [dt_], [dt_])
        tt(t0_[:], are[:], dt_[:], ALU.mult, [are, dt_], [t0_])
        act(t0_[:], t0_[:], AF.Exp, [t0_], [t0_])
        tt(t1_[:], aim[:], dt_[:], ALU.mult, [aim, dt_], [t1_])
        sin_of(lim, t1_, t2_)
        ts(t3_[:], t1_[:], math.pi / 2, None, ALU.add, None, [t1_], [t3_])
        sin_of(lre, t3_, t2_)
        tt(lre[:], lre[:], t0_[:], ALU.mult, [lre, t0_], [lre])
        tt(lim[:], lim[:], t0_[:], ALU.mult, [lim, t0_], [lim])
        tt(den[:], are[:], are[:], ALU.mult, [are], [den])
        tt(t2_[:], aim[:], aim[:], ALU.mult, [aim], [t2_])
        tt(den[:], den[:], t2_[:], ALU.add, [den, t2_], [den])
        recip(den, den[:])
        ts(t0_[:], lre[:], -1.0, None, ALU.add, None, [lre], [t0_])
        tt(fr[:], t0_[:], are[:], ALU.mult, [t0_, are], [fr])
        tt(t2_[:], lim[:], aim[:], ALU.mult, [lim, aim], [t2_])
        tt(fr[:], fr[:], t2_[:], ALU.add, [fr, t2_], [fr])
        tt(fr[:], fr[:], den[:], ALU.mult, [fr, den], [fr])
        tt(fi[:], lim[:], are[:], ALU.mult, [lim, are], [fi])
        tt(t2_[:], t0_[:], aim[:], ALU.mult, [t0_, aim], [t2_])
        tt(fi[:], fi[:], t2_[:], ALU.subtract, [fi, t2_], [fi])
        tt(fi[:], fi[:], den[:], ALU.mult, [fi, den], [fi])
        tt(t0_[:], lre[:], lre[:], ALU.mult, [lre], [t0_])
        tt(t2_[:], lim[:], lim[:], ALU.mult, [lim], [t2_])
        tt(t0_[:], t0_[:], t2_[:], ALU.add, [t0_, t2_], [t0_])
        recip(t0_, t0_[:])
        tt(ire[:], lre[:], t0_[:], ALU.mult, [lre, t0_], [ire])
        tt(iim[:], lim[:], t0_[:], ALU.mult, [lim, t0_], [iim])
        ts(iim[:], iim[:], -1.0, None, ALU.mult, None, [iim], [iim])
        rev = (d == 1)
        pow_table(LPr, LPi, lre, lim, rev)
        pow_table(LMr, LMi, ire, iim, rev)
        bre = Tile(Zr.t[:, :, 0:16], "bre")
        bim = Tile(Zr.t[:, :, 16:32], "bim")
        bbr = Tile(Zr.t[:, :, 32:48], "bbr")
        bbi = Tile(Zr.t[:, :, 48:64], "bbi")
        tA = Tile(Zi.t[:, :, 0:16], "tA")
        tB = Tile(Zi.t[:, :, 16:32], "tB")
        bpadf = A([128, 16, 128], F32, "bpadf", at=Wr.off)
        assert Wi.off == Wr.off + 1024
        for (tile_, src) in ((bre, s5_b_re), (bim, s5_b_im)):
            v_ = src[l, d].rearrange("(j two) p h -> two p j h", two=2)
            for two in range(2):
                small_dma(tile_[64 * two:64 * two + 64, :, :], v_[two], tile_)
        frb = fr[:].unsqueeze(2).to_broadcast([128, 16, 16])
        fib = fi[:].unsqueeze(2).to_broadcast([128, 16, 16])
        cmul(bbr[:], bbi[:], bre[:], bim[:], frb, fib, (tA, tA[:]), (tB, tB[:]), [bre, bim, fr, fi], [bbr, bbi])
        for part, src_t in enumerate((bbr, bbi)):
            memset(bpadf, bpadf[:], 0.0)
            for half in range(2):
                for jj in range(4):
                    cp(bpadf[64 * half:64 * half + 64, jj::4, 32 * jj + 16 * half:32 * jj + 16 * half + 16],
                       src_t[64 * half:64 * half + 64, jj::4, :], [src_t], [bpadf])
            for j in range(16):
                pt, pv = next_ps()
                mm(pv[:, 0:128], bpadf[:, j, :], ident[:], True, True, [bpadf, ident], [pt])
                cp(Bpad[:, j, part, :], pv[:, 0:128], [pt], [Bpad])
        for part, src in enumerate((s5_c_re, s5_c_im)):
            for two in range(2):
                for j_ in range(16):
                    small_dma(bre[64 * two:64 * two + 64, j_, :], src[l, d, 2 * j_ + two].rearrange("h p -> p h"), bre)
            memset(bpadf, bpadf[:], 0.0)
            for half in range(2):
                for jj in range(4):
                    cp(bpadf[64 * half:64 * half + 64, jj::4, 32 * jj + 16 * half:32 * jj + 16 * half + 16],
                       bre[64 * half:64 * half + 64, jj::4, :], [bre], [bpadf])
            if part == 0:
                cp(Cpad[:, :, 0, :], bpadf[:], [bpadf], [Cpad])
            else:
                ts(Cpad[:, :, 1, :], bpadf[:], -1.0, None, ALU.mult, None, [bpadf], [Cpad])
        for al in (bre, bim, bbr, bbi):
            Zr.readers.update(al.readers); Zr.writers.update(al.writers)
        for al in (tA, tB):
            Zi.readers.update(al.readers); Zi.writers.update(al.writers)

        P.barrier()
        if stop_after == ("s_tab" + ("1" if d else ""), l):
            return True

        def gla_scan(mix, j_out, gcol):
            memset(Sst, Sst[:], 0.0)
            memset(Sbf, Sbf[:], 0.0)
            memset(nbc, nbc[:], 0.0)
            memset(vK, vK[:], 1.0)
            if mix == "h":
                qs, kFs, kKs, vs, gs = FB["hq"], FB[f"hkF{d}"], KB[f"hkK{d}"], KB["hv"], FB["hog"]
            elif mix == "r":
                qs, kFs, kKs, vs, gs = FB["rq"], FB["rkF"], KB["rkK"], KB["rv"], FB["rog"]
            else:
                qs, kFs, kKs, vs, gs = FB["mq"], FB["mk"], None, KB["mv"], FB["mz"]
            for sc in sc_order(d):
                n0 = sc * 256
                P.dma("sp", qF[:], qs[:, n0:n0 + 256].rearrange("(h p) n -> p h n", p=128), writes=[qF])
                P.dma("sp", kF[:], kFs[:, n0:n0 + 256].rearrange("(h p) n -> p h n", p=128), writes=[kF])
                vsrc = vs[n0:n0 + 256, :].rearrange("(c s) (h v) -> s c h v", s=64, v=128)
                for c in range(4):
                    P.dma("act", vK[:, c, :, 0:128], vsrc[:, c], writes=[vK])
                if kKs is not None:
                    P.dma("act", kK[:], kKs[n0:n0 + 256, :].rearrange("(c s) x -> s c x", s=64), writes=[kK])
                else:
                    for c in range(4):
                        pt, pv = next_ps()
                        for h in range(4):
                            mm(pv[0:64, h * 128:(h + 1) * 128], kF[:, h, c * 64:(c + 1) * 64], ident_b[:], True, True, [kF, ident_b], [pt])
                        act(kK[:, c, :], pv[0:64, :], AF.Identity, [pt], [kK])
                if mix == "r":
                    EP, EM, EK, ET = rEp, rEm, rEk, rEt
                    EK_ap = rEk[:]
                elif mix == "h":
                    P.dma("act", lf[:], hlogf[d][n0:n0 + 256, :].rearrange("(c s) x -> s c x", s=64), writes=[lf])
                    pe_ = PB[2]
                    for h in range(4):
                        for c in range(4):
                            mm(pe_[:, h * 256 + c * 64:h * 256 + c * 64 + 64], lf[:, c, h * 128:(h + 1) * 128], tri[d][:], True, True, [lf, tri[d]], [pe_])
                    act(EpF[:].rearrange("p h n -> p (h n)"), pe_[:], AF.Exp, [pe_], [EpF])
                    cumS = sq_t
                    act(cumS[:].rearrange("p h n -> p (h n)"), pe_[:], AF.Identity, [pe_], [cumS])
                    for cp_ in range(2):
                        pk = PB[3]
                        for c2_ in range(2):
                            c = cp_ * 2 + c2_
                            mm(pk[0:64, c2_ * 512:(c2_ + 1) * 512], stri[:], lf[:, c, :], True, True, [lf, stri], [pk])
                        act(EkK[:, cp_ * 2:cp_ * 2 + 2, :].rearrange("p c x -> p (c x)"), pk[0:64, :], AF.Exp, [pk], [EkK])
                    cp(Et[:].rearrange("p c h -> p h c"), EpF[:, :, tl::64], [EpF], [Et])
                    memset(CPt, CPt[:], 0.0)
                    c5 = cumS[:].rearrange("p h (c b t) -> p h c b t", c=4, b=4)
                    cpv = CPt[:].rearrange("p h (c b) -> p h c b", c=4)
                    if d == 0:
                        cp(cpv[:, :, :, 1:4], c5[:, :, :, 0:3, 15], [cumS], [CPt])
                    else:
                        cp(cpv[:, :, :, 0:3], c5[:, :, :, 1:4, 0], [cumS], [CPt])
                    Rt = Dtile
                    cp(Rt[:].rearrange("p h (k t) -> p h k t", t=16), CPt[:].unsqueeze(3).to_broadcast([128, 4, 16, 16]), [CPt], [Rt])
                    tt(Rt[:], cumS[:], Rt[:], ALU.subtract, [cumS, Rt], [Rt])
                    act(Rt[:], Rt[:], AF.Exp, [Rt], [Rt])
                    tt(qpb[:], qF[:], Rt[:], ALU.mult, [qF, Rt], [qpb])
                    for I in range(4):
                        tmpE = EmF
                        tt(tmpE[:].rearrange("p h (c t) -> p h c t", t=64),
                           cpv[:, :, :, I:I + 1].to_broadcast([128, 4, 4, 64]),
                           cumS[:].rearrange("p h (c t) -> p h c t", t=64), ALU.subtract, [CPt, cumS], [tmpE])
                        ts(tmpE[:], tmpE[:], 80.0, None, ALU.min, None, [tmpE], [tmpE])
                        act(tmpE[:], tmpE[:], AF.Exp, [tmpE], [tmpE])
                        tt(kpI[I][:], kF[:], tmpE[:], ALU.mult, [kF, tmpE], [kpI[I]])
                    EP, EM, EK, ET = EpF, EmF, EkK, Et
                    EK_ap = EkK[:]
                else:
                    P.dma("act", g[:], mg[n0:n0 + 256, :].rearrange("(c s) x -> s c x", s=64), writes=[g])
                    pe_ = PB[2]
                    pm_ = PB[3]
                    for h in range(4):
                        fcol = d * 8 + 4 + h
                        icol = d * 8 + h
                        for c in range(4):
                            sl = slice(h * 256 + c * 64, h * 256 + c * 64 + 64)
                            mm(pe_[:, sl], g[:, c, fcol:fcol + 1].to_broadcast([64, 128]), tri[d][:], True, True, [g, tri[d]], [pe_])
                            mm(pm_[:, sl], g[:, c, fcol:fcol + 1].to_broadcast([64, 128]), ntri[d][:], True, False, [g, ntri[d]], [pm_])
                            mm(pm_[:, sl], g[:, c, icol:icol + 1].to_broadcast([64, 128]), ident[0:64, 0:64], False, True, [g, ident], [pm_])
                    act(EpF[:].rearrange("p h n -> p (h n)"), pe_[:], AF.Exp, [pe_], [EpF])
                    act(EmF[:].rearrange("p h n -> p (h n)"), pm_[:], AF.Exp, [pm_], [EmF])
                    pt, pv = next_ps()
                    for c in range(4):
                        mm(pv[0:64, c * 16:(c + 1) * 16], tri[d][:], g[:, c, :], True, True, [g, tri[d]], [pt])
                    pvv = pv[0:64, 0:64].rearrange("p (c x) -> p c x", x=16)
                    tt(Ekm[:], g[:, :, d * 8:d * 8 + 4], pvv[:, :, d * 8 + 4:d * 8 + 8], ALU.subtract, [g, pt], [Ekm])
                    act(Ekm[:], Ekm[:], AF.Exp, [Ekm], [Ekm])
                    cp(Et[:].rearrange("p c h -> p h c"), EpF[:, :, tl::64], [EpF], [Et])
                    EP, EM, EK, ET = EpF, EmF, Ekm, Et
                    EK_ap = Ekm[:].unsqueeze(3).to_broadcast([64, 4, 4, 128])
                tt(qp[:], qF[:], EP[:], ALU.mult, [qF, EP], [qp])
                if mix != "h":
                    tt(kpF[:], kF[:], EM[:], ALU.mult, [kF, EM], [kpF])
                if mix == "m":
                    tt(kpK[:].rearrange("p c (h x) -> p c h x", x=128), kK[:].rearrange("p c (h x) -> p c h x", x=128), EK_ap, ALU.mult, [kK, EK], [kpK])
                else:
                    tt(kpK[:], kK[:], EK_ap, ALU.mult, [kK, EK], [kpK])
                pa = PB[0]
                for c in range(4):
                    for h in range(4):
                        blk = c * 4 + h
                        if mix == "h":
                            for I in range(4):
                                mm(pa[0:64, blk * 64 + 16 * I:blk * 64 + 16 * I + 16], kpI[I][:, h, c * 64:(c + 1) * 64],
                                   qpb[:, h, c * 64 + 16 * I:c * 64 + 16 * I + 16], True, True, [kpI[I], qpb], [pa])
                        else:
                            mm(pa[0:64, blk * 64:(blk + 1) * 64], kpF[:, h, c * 64:(c + 1) * 64], qp[:, h, c * 64:(c + 1) * 64], True, True, [kpF, qp], [pa])
                tt(PT[:], pa[0:64, :].rearrange("p (b t) -> p b t", t=64), tri[d][:].unsqueeze(1).to_broadcast([64, 16, 64]), ALU.mult, [pa, tri[d]], [PT])
                po = PB[1]
                pd = PB[2]
                corder = range(4) if d == 0 else range(3, -1, -1)
                for c in corder:
                    for h in range(4):
                        sl = slice(h * 256 + c * 64, h * 256 + c * 64 + 64)
                        mm(po[:, sl], vK[:, c, h, 0:128], PT[:, c * 4 + h, :], True, False, [vK, PT], [po])
                        mm(po[:, sl], Sbf[:, h, 0:128], qp[:, h, c * 64:(c + 1) * 64], False, True, [Sbf, qp], [po])
                        if mix == "m":
                            mm(pd[:, sl], ones_b[0:64, :], PT[:, c * 4 + h, :], True, False, [ones_b, PT], [pd])
                            mm(pd[:, sl], nbc[:, h, :], qp[:, h, c * 64:(c + 1) * 64], False, True, [nbc, qp], [pd])
                    pu = PB[3]
                    for h in range(4):
                        mm(pu[:, h * 256:h * 256 + 129], kpK[:, c, h * 128:(h + 1) * 128], vK[:, c, h, :], True, True, [kpK, vK], [pu])
                    puv = pu[:].rearrange("p (h x) -> p h x", x=256)[:, :, 0:129]
                    if mix == "h":
                        tt(Sst[:], Sst[:], ET[:, c, :].unsqueeze(2).to_broadcast([128, 4, 129]), ALU.mult, [Sst, ET], [Sst])
                        tt(Sst[:], Sst[:], puv, ALU.add, [Sst, pu], [Sst])
                    else:
                        tt(Sst[:], Sst[:], puv, ALU.add, [Sst, pu], [Sst])
                        tt(Sst[:], Sst[:], ET[:, c, :].unsqueeze(2).to_broadcast([128, 4, 129]), ALU.mult, [Sst, ET], [Sst])
                    act(Sbf[:], Sst[:], AF.Identity, [Sst], [Sbf])
                    if mix == "m":
                        cp(nbc[:], Sst[:, :, 128:129].to_broadcast([128, 4, 128]), [Sst], [nbc])
                Ov = Otile[:].rearrange("p h n -> p (h n)")
                if mix == "m":
                    act(Dtile[:].rearrange("p h n -> p (h n)"), pd[:], AF.Abs, [pd], [Dtile])
                    ts(Dtile[:], Dtile[:], 1.0, None, ALU.max, None, [Dtile], [Dtile])
                    recip(Dtile, Dtile[:])
                    tt(Ov, po[:], Dtile[:].rearrange("p h n -> p (h n)"), ALU.mult, [po, Dtile], [Otile])
                else:
                    act(Ov, po[:], AF.Identity, [po], [Otile])
                dst = rawf[j_out][:, n0:n0 + 256].rearrange("(h p) n -> p h n", p=128)
                if d == 0:
                    P.dma("sp", dst, Otile[:], reads=[Otile])
                else:
                    P.dma("sp", rawp[:], dst, writes=[rawp])
                    P.dma("sp", gate_t[:], gs[:, n0:n0 + 256].rearrange("(h p) n -> p h n", p=128), writes=[gate_t])
                    tt(rawp[:], rawp[:], Otile[:], ALU.add, [rawp, Otile], [rawp])
                    tt(sq_t[:], rawp[:], rawp[:], ALU.mult, [rawp], [sq_t])
                    pss = PB[0]
                    sqv = sq_t[:].rearrange("p h n -> p (h n)")
                    for hf in range(2):
                        mm(pss[:, hf * 512:(hf + 1) * 512], ones_f[:], sqv[:, hf * 512:(hf + 1) * 512], True, True, [ones_f, sq_t], [pss])
                    act(sqv, pss[:], AF.Ln, [pss], [sq_t], scale=1.0 / 128, bias=EPS)
                    act(sqv, sqv, AF.Exp, [sq_t], [sq_t], scale=-0.5)
                    tt(rawp[:], rawp[:], sq_t[:], ALU.mult, [rawp, sq_t], [rawp])
                    tt(rawp[:], rawp[:], gcol[:].unsqueeze(2).to_broadcast([128, 4, 256]), ALU.mult, [rawp, gcol], [rawp])
                    tt(o_bf[:], rawp[:], gate_t[:], ALU.mult, [rawp, gate_t], [o_bf])
                    P.dma("sp", oF[j_out * BR:(j_out + 1) * BR, n0:n0 + 256].rearrange("(h p) n -> p h n", p=128), o_bf[:], reads=[o_bf])

        def s5_scan():
            memset(xpr, xpr[:], 0.0)
            memset(xpi, xpi[:], 0.0)
            Xr, Xi = Zr, Zi
            for sc in sc_order(d):
                n0 = sc * 256
                P.dma("sp", u[:], FB["uF"][:, n0:n0 + 256].rearrange("(cc p) n -> p cc n", p=128), writes=[u])
                py = PB[1]
                corder = range(4) if d == 0 else range(3, -1, -1)
                for c in corder:
                    pbr, pbi = PB[2], PB[3]
                    for j in range(16):
                        cc = j // 4
                        mm(pbr[:, j * 64:(j + 1) * 64], Bpad[:, j, 0, :], u[:, cc, c * 64:(c + 1) * 64], True, True, [Bpad, u], [pbr])
                        mm(pbi[:, j * 64:(j + 1) * 64], Bpad[:, j, 1, :], u[:, cc, c * 64:(c + 1) * 64], True, True, [Bpad, u], [pbi])
                    bur = pbr[:].rearrange("p (j t) -> p j t", t=64)
                    bui = pbi[:].rearrange("p (j t) -> p j t", t=64)
                    cmul(Zr[:], Zi[:], bur, bui, LMr[:], LMi[:], (T1, T1[:]), (T2, T2[:]), [pbr, pbi, LMr, LMi], [Zr, Zi])
                    tin = 0 if d == 0 else 63
                    tt(Zr[:, :, tin], Zr[:, :, tin], xpr[:], ALU.add, [Zr, xpr], [Zr])
                    tt(Zi[:, :, tin], Zi[:, :, tin], xpi[:], ALU.add, [Zi, xpi], [Zi])
                    for (W_, Z_) in ((Wr, Zr), (Wi, Zi)):
                        P.op("dve", lambda e, W_=W_, Z_=Z_: e.tensor_tensor_scan(
                            W_[:].rearrange("p j t -> p (j t)"), rst[:].rearrange("p j t -> p (j t)"),
                            Z_[:].rearrange("p j t -> p (j t)"), 0.0, ALU.mult, ALU.add), reads=[rst, Z_], writes=[W_])
                        if d == 1:
                            cp(tot[:], W_[:, :, 63], [W_], [tot])
                            tt(W_[:], Z_[:], W_[:], ALU.subtract, [Z_, W_], [W_])
                            tt(W_[:], W_[:], tot[:].unsqueeze(2).to_broadcast([128, 16, 64]), ALU.add, [W_, tot], [W_])
                    cmul(Xr[:], Xi[:], Wr[:], Wi[:], LPr[:], LPi[:], (T1, T1[:]), (T2, T2[:]), [Wr, Wi, LPr, LPi], [Xr, Xi])
                    cp(xpr[:], Xr[:, :, tl], [Xr], [xpr])
                    cp(xpi[:], Xi[:, :, tl], [Xi], [xpi])
                    act(Xrb[:], Xr[:], AF.Identity, [Xr], [Xrb])
                    act(Xib[:], Xi[:], AF.Identity, [Xi], [Xib])
                    for cc in range(4):
                        sl = slice(cc * 256 + c * 64, cc * 256 + c * 64 + 64)
                        k_ = 0
                        for jj in range(4):
                            j = cc * 4 + jj
                            for part, X_ in enumerate((Xrb, Xib)):
                                mm(py[:, sl], Cpad[:, j, part, :], X_[:, j, :], k_ == 0, k_ == 7, [Cpad, X_], [py])
                                k_ += 1
                Ov = Otile[:].rearrange("p h n -> p (h n)")
                act(Ov, py[:], AF.Identity, [py], [Otile])
                dst = rawf[2][:, n0:n0 + 256].rearrange("(h p) n -> p h n", p=128)
                if d == 0:
                    P.dma("sp", dst, Otile[:], reads=[Otile])
                else:
                    P.dma("sp", rawp[:], dst, writes=[rawp])
                    tt(rawp[:], rawp[:], Otile[:], ALU.add, [rawp, Otile], [rawp])
                    cp(Dtile[:], u[:], [u], [Dtile])
                    tt(Dtile[:], Dtile[:], s5d_col[:].unsqueeze(2).to_broadcast([128, 4, 256]), ALU.mult, [Dtile, s5d_col], [Dtile])
                    tt(rawp[:], rawp[:], Dtile[:], ALU.add, [rawp, Dtile], [rawp])
                    act(rawp[:], rawp[:], AF.Gelu, [rawp], [rawp])
                    cp(yc_b[:], rawp[:], [rawp], [yc_b])
                    pg = PB[0]
                    for cco in range(4):
                        for cci in range(4):
                            mm(pg[:, cco * 256:(cco + 1) * 256], gluw[:, cci, cco * 128:(cco + 1) * 128], yc_b[:, cci, :], cci == 0, cci == 3, [gluw, yc_b], [pg])
                    for cco in range(4):
                        act(sq_t[:, cco, :], pg[:, cco * 256:(cco + 1) * 256], AF.Sigmoid, [pg, glub_col], [sq_t], bias=glub_col[:, cco:cco + 1])
                    tt(o_bf[:], rawp[:], sq_t[:], ALU.mult, [rawp, sq_t], [o_bf])
                    P.dma("sp", oF[2 * BR:3 * BR, n0:n0 + 256].rearrange("(h p) n -> p h n", p=128), o_bf[:], reads=[o_bf])

        gla_scan("h", 0, hn_col[0])
        if stop_after == ("s_h" + ("1" if d else ""), l):
            P.barrier()
            return True
        P.barrier()
        ret_tables()
        gla_scan("r", 1, hn_col[1])
        if stop_after == ("s_r" + ("1" if d else ""), l):
            P.barrier()
            return True
        s5_scan()
        if stop_after == ("s_s5" + ("1" if d else ""), l):
            P.barrier()
            return True
        gla_scan("m", 3, hn_col[2])
        P.barrier()
        return False

    def merge_phase(l):
        for j in range(4):
            def epi_cb(cb, j=j):
                def e_(pt, pv, cg, cgn, tok0, tn):
                    row = cb * 512 + cg
                    gt = sbb()
                    P.dma("act", gt[:, 0:tn], gF[j * D + row:j * D + row + 128, tok0:tok0 + tn], writes=[gt])
                    s = sf()
                    tt(s[:, 0:tn], pv[:, 0:tn], gt[:, 0:tn], ALU.mult, [pt, gt], [s])
                    dsl = yaccd[row:row + 128, tok0:tok0 + tn]
                    if j > 0:
                        s2 = sf()
                        P.dma("act", s2[:, 0:tn], dsl, writes=[s2])
                        tt(s[:, 0:tn], s[:, 0:tn], s2[:, 0:tn], ALU.add, [s, s2], [s])
                    if j < 3:
                        P.dma("sp", dsl, s[:, 0:tn], reads=[s])
                    else:
                        sb_ = sbb()
                        cp(sb_[:, 0:tn], s[:, 0:tn], [s], [sb_])
                        P.dma("sp", yF[row:row + 128, tok0:tok0 + tn], sb_[:, 0:tn], reads=[sb_])
                return e_
            jobs = [(cb * 512, 512, "F", epi_cb(cb)) for cb in range(4)]
            gemm(oF[j * BR:(j + 1) * BR, :], BR, NT, w_branch[l, j], jobs, 4352 if NT >= 4352 else NT)

    def resid_epi(gate_m, c0, ncols):
        def e_(pt, pv, tok0, tn):
            v = 1 if tok0 < CTX else 0
            gr = sf()
            P.dma("act", gr[:, 0:ncols], mod_d[v:v + 1, gate_m * D + c0:gate_m * D + c0 + ncols].partition_broadcast(128), writes=[gr])
            s = sf()
            tt(s[0:tn, 0:ncols], pv[0:tn, 0:ncols], gr[0:tn, 0:ncols], ALU.mult, [pt, gr], [s])
            s2 = sf()
            P.dma("act", s2[0:tn, 0:ncols], xres[tok0:tok0 + tn, c0:c0 + ncols], writes=[s2])
            tt(s[0:tn, 0:ncols], s[0:tn, 0:ncols], s2[0:tn, 0:ncols], ALU.add, [s, s2], [s])
            P.dma("sp", xres[tok0:tok0 + tn, c0:c0 + ncols], s[0:tn, 0:ncols], reads=[s])
        return e_

    def wout_phase(l):
        jobs = [(c0, 512, "K", resid_epi(2, c0, 512)) for c0 in range(0, D, 512)]
        TB = 2176 if NT >= 2176 else NT
        gemm(yF, D, NT, w_out[l], jobs, TB)

    def ffn_phase(l):
        def epi1(row0):
            def e_(pt, pv, cg, cgn, tok0, tn):
                s = sf()
                act(s[:, 0:tn], pv[:, 0:tn], AF.Relu, [pt], [s])
                sb_ = sbb()
                tt(sb_[:, 0:tn], s[:, 0:tn], s[:, 0:tn], ALU.mult, [s], [sb_])
                P.dma("sp", aF[row0 + cg:row0 + cg + 128, tok0:tok0 + tn], sb_[:, 0:tn], reads=[sb_])
            return e_
        jobs = [(c0, 512, "F", epi1(c0)) for c0 in range(0, DFF, 512)]
        TB = 2176 if NT >= 2176 else NT
        gemm(hF, D, NT, w_ff1[l], jobs, TB)
        for kh in range(2):
            jobs = [(c0, 256, "K", resid_epi(5, c0, 256)) for c0 in range(0, D, 256)]
            gemm(aF[kh * 4096:(kh + 1) * 4096, :], DFF // 2, NT, w_ff2[l][kh * 4096:(kh + 1) * 4096, :], jobs, 1152)

    def final_phase():
        P.arena_reset()
        fn = P.ar([128, D], F32, "fn")
        xt = [P.ar([128, D], F32, f"fxt{i}") for i in range(2)]
        xn = [P.ar([128, D], F32, f"fxn{i}") for i in range(2)]
        junk = P.ar([128, D], BF16, "fjunk")
        P.dma("sp", fn[:], final_norm.partition_broadcast(128), writes=[fn])
        for ti in range(2, NT // 128):
            x_ = xt[ti % 2]
            xn_ = xn[ti % 2]
            P.dma("sp", x_[:], xres[ti * 128:(ti + 1) * 128, :], writes=[x_])
            r = rstd_of(x_, junk)
            ts(xn_[:], x_[:], r, None, ALU.mult, None, [x_, ssq], [xn_])
            tt(xn_[:], xn_[:], fn[:], ALU.mult, [xn_, fn], [xn_])
            P.dma("sp", out[(ti - 2) * 128:(ti - 1) * 128, :], xn_[:], reads=[xn_])

    def program():
        for l in range(2):
            layer_setup(l)
            if stop_after == ("setup", l):
                return
            norm_phase(0)
            if stop_after == ("norm", l):
                return
            win_phase(l)
            if stop_after == ("win", l):
                return
            conv_phase()
            if stop_after == ("conv", l):
                return
            for d in range(2):
                if scan_phase(l, d):
                    return
                if stop_after == ("s_d0", l):
                    return
            if stop_after == ("scan", l):
                return
            merge_phase(l)
            if stop_after == ("merge", l):
                return
            wout_phase(l)
            if stop_after == ("wout", l):
                return
            norm_phase(1)
            ffn_phase(l)
            if stop_after == ("ffn", l):
                return
        final_phase()

    program()
    P.barrier()
    for name in dump:
        t, shape, dtype = SCR[name]
        o = nc.dram_tensor("dbg_" + name, list(shape), dtype, kind="ExternalOutput").ap()
        P.dma("sp", o, t)
    P.finish()
    return nc


_NC_CACHE = {}


def kernel(**inputs):
    x = np.asarray(inputs["x"], np.float32)
    B, NLAT, _ = x.shape
    ctx = np.asarray(inputs["ctx"], np.float32)
    c = np.asarray(inputs["c"], np.float32)
    c_ctx = np.asarray(inputs["c_ctx"], np.float32)
    if NLAT not in _NC_CACHE:
        _NC_CACHE[NLAT] = build(NLAT)
    nc = _NC_CACHE[NLAT]
    consts = host_consts()
    shared = {}
    for k in ("w_mod", "b_mod", "norm_mix", "norm_mlp", "w_in", "hgrn_lb_logits", "hgrn_norm", "ret_decay",
              "ret_norm", "s5_a_re", "s5_a_im", "s5_log_dt", "s5_b_re", "s5_b_im", "s5_c_re", "s5_c_im", "s5_d",
              "s5_glu_w", "s5_glu_b", "mlstm_conv_w", "mlstm_conv_b", "mlstm_norm", "w_branch", "w_out",
              "w_ff1", "w_ff2"):
        shared[k] = np.ascontiguousarray(np.asarray(inputs[k], np.float32))
    shared["ret_decay"] = np.ascontiguousarray(np.asarray(inputs["ret_decay"], np.float32).reshape(2, 8))
    shared["mlstm_gate_b"] = np.ascontiguousarray(np.asarray(inputs["mlstm_gate_b"], np.float32).reshape(2, 16))
    shared["final_norm"] = np.ascontiguousarray(np.asarray(inputs["final_norm"], np.float32).reshape(1, D))
    shared.update(consts)
    in_maps = []
    for core in range(8):
        b = core % B
        m = dict(shared)
        m["xin"] = np.ascontiguousarray(np.concatenate([ctx[b], x[b]], axis=0))
        m["c2"] = np.ascontiguousarray(np.stack([c[b], c_ctx]))
        in_maps.append(m)
    res = run_bass_kernel_spmd(nc, in_maps, core_ids=list(range(8)))
    outs = [np.asarray(res.results[b]["out"], np.float32) for b in range(B)]
    return np.stack(outs, axis=0)
```

```python
import contextlib
import math
import numpy as np
import concourse.bass as bass
import concourse.mybir as mybir
from concourse.bass_utils import run_bass_kernel_spmd

F32 = mybir.dt.float32
BF16 = mybir.dt.bfloat16
I32 = mybir.dt.int32
AF = mybir.ActivationFunctionType
ALU = mybir.AluOpType

ENGS = ("pe", "act", "dve", "pool", "sp")
KSEM = 8
NDQ = 12
SAME_ENGINE_WAITS = True

D = 2048
BR = 512
DFF = 8192
INW = 15376
CTX = 256
EPS = 1e-6


class Tile:
    def __init__(self, t, name=""):
        self.t = t
        self.name = name
        self.writers = {}
        self.readers = {}

    def __getitem__(self, k):
        return self.t[k]


class Prog:
    def __init__(self, nc):
        self.nc = nc
        self.es = contextlib.ExitStack()
        self.streams = {e: [] for e in ENGS}
        self.count = {e: 0 for e in ENGS}
        self.sems = {}
        for e in ENGS:
            self.sems[e] = [self.es.enter_context(nc.semaphore(f"s_{e}{k}")) for k in range(KSEM)]
        self.dq = {}
        for q in ("sp", "pool", "act"):
            self.dq[q] = [self.es.enter_context(nc.semaphore(f"d_{q}{k}")) for k in range(NDQ)]
        self.dq_count = {q: [0] * NDQ for q in self.dq}
        self.dq_next = {q: 0 for q in self.dq}
        self.seen = {e: {} for e in ENGS}
        self.n_sb = 0

    def sb(self, shape, dtype, name=None):
        self.n_sb += 1
        name = "sb_" + (name or f"t{self.n_sb}")
        t = self.es.enter_context(self.nc.sbuf_tensor(name, list(shape), dtype))
        return Tile(t, name)

    def arena_init(self, words):
        self.arena = self.es.enter_context(self.nc.sbuf_tensor("arena", [128, words], F32))
        self.arena_words = words
        self.arena_off = 0

    def arena_reset(self):
        self.arena_off = 0

    def ar(self, shape, dtype, name=None, at=None):
        esz = 4 if dtype in (F32, I32) else 2
        nfree = 1
        for d_ in shape[1:]:
            nfree *= d_
        words = (nfree * esz + 31) // 32 * 8
        if at is None:
            assert self.arena_off + words <= self.arena_words, (name, self.arena_off, words)
            off = self.arena_off
            self.arena_off += words
        else:
            off = at
        v = self.arena[0:shape[0], off:off + words]
        if dtype != F32:
            v = v.bitcast(dtype)
        v = v[:, 0:nfree]
        if len(shape) == 3:
            v = v.rearrange("p (a b) -> p a b", a=shape[1])
        elif len(shape) == 4:
            v = v.rearrange("p (a b c) -> p a b c", a=shape[1], b=shape[2])
        tl_ = Tile(v, name or "ar")
        tl_.off, tl_.words = off, words
        return tl_

    def ps(self, shape, dtype=F32, name=None):
        self.n_sb += 1
        name = name or f"ps{self.n_sb}"
        t = self.es.enter_context(self.nc.psum_tensor(name, list(shape), dtype))
        return Tile(t, name)

    def dram(self, name, shape, dtype):
        t = self.nc.dram_tensor(name, list(shape), dtype)
        return Tile(t.ap(), name)

    def _sem_of(self, stream, idx):
        if stream in ENGS:
            return self.sems[stream][idx % KSEM], idx // KSEM + 1
        q, k = stream
        return self.dq[q][k], 16 * (idx + 1)

    def _need(self, eng, stream, idx, is_dma=False):
        if stream == eng and (eng == "pe" or not (SAME_ENGINE_WAITS or is_dma)):
            return
        if self.seen[eng].get(stream, -1) >= idx:
            return
        self.seen[eng][stream] = idx
        sem, val = self._sem_of(stream, idx)
        self.streams[eng].append(lambda e, sem=sem, val=val: e.wait_ge(sem, val))

    def _deps(self, eng, reads, writes, is_dma=False):
        for t in reads:
            for s, i in t.writers.items():
                self._need(eng, s, i, is_dma)
        for t in writes:
            for s, i in t.writers.items():
                self._need(eng, s, i, is_dma)
            for s, i in t.readers.items():
                self._need(eng, s, i, is_dma)

    def op(self, eng, fn, reads=(), writes=(), sig=True):
        self._deps(eng, reads, writes)
        idx = self.count[eng]
        if sig:
            self.count[eng] += 1
            sem = self.sems[eng][idx % KSEM]
            self.streams[eng].append(lambda e, fn=fn, sem=sem: fn(e).then_inc(sem, 1))
        else:
            self.streams[eng].append(lambda e, fn=fn: fn(e))
        for t in reads:
            t.readers[eng] = idx
        for t in writes:
            t.writers = {eng: idx}
            t.readers = {}
        return idx

    def dma(self, q, out, in_, reads=(), writes=(), **kw):
        self._deps(q, reads, writes, True)
        k = self.dq_next[q]
        self.dq_next[q] = (k + 1) % NDQ
        j = self.dq_count[q][k]
        if j > 0:
            self._need(q, (q, k), j - 1)
        self.dq_count[q][k] += 1
        sem = self.dq[q][k]
        self.streams[q].append(
            lambda e, sem=sem, out=out, in_=in_, kw=kw: e.dma_start(out=out, in_=in_, **kw).then_inc(sem, 16))
        st = (q, k)
        for t in reads:
            t.readers[st] = j
        for t in writes:
            t.writers = {st: j}
            t.readers = {}

    def barrier(self):
        for e in ENGS:
            for s in ENGS:
                if s != e and self.count[s] > 0:
                    self._need(e, s, self.count[s] - 1)
            for q in self.dq:
                for k in range(NDQ):
                    if self.dq_count[q][k] > 0:
                        self._need(e, (q, k), self.dq_count[q][k] - 1)

    def finish(self):
        self.barrier()
        nc = self.nc
        with nc.Block() as block:
            @block.tensor
            def _(e):
                for f in self.streams["pe"]:
                    f(e)

            @block.scalar
            def _(e):
                for f in self.streams["act"]:
                    f(e)

            @block.vector
            def _(e):
                for f in self.streams["dve"]:
                    f(e)

            @block.gpsimd
            def _(e):
                for f in self.streams["pool"]:
                    f(e)

            @block.sync
            def _(e):
                for f in self.streams["sp"]:
                    f(e)
        self.es.close()


def host_consts():
    t = np.arange(64, dtype=np.float32)
    c = {}
    c["ident"] = np.eye(128, dtype=np.float32)
    triU = (t[:, None] <= t[None, :]).astype(np.float32)
    triL = (t[:, None] >= t[None, :]).astype(np.float32)
    c["tri"] = np.stack([triU, triL])
    c["iotaF"] = np.stack([np.tile(t + 1, (128, 1)), np.tile(64 - t, (128, 1))]).astype(np.float32)
    c["iotaK"] = np.stack([np.tile((t + 1)[:, None], (1, 128)), np.tile((64 - t)[:, None], (1, 128))]).astype(np.float32)
    rst = np.ones((128, 16, 64), np.float32)
    rst[:, :, 0] = 0.0
    c["rst"] = rst
    return c


def build(NLAT, dump=(), stop_after=None):
    nc = bass.Bass("TRN2", target_bir_lowering=False)
    P = Prog(nc)
    NT = CTX + NLAT
    NSC = NT // 256
    ROWS = NLAT // 64

    def din(name, shape):
        return nc.dram_tensor(name, list(shape), F32, kind="ExternalInput").ap()

    xin = din("xin", [NT, D])
    c2 = din("c2", [2, D])
    w_mod = din("w_mod", [2, D, 6 * D])
    b_mod = din("b_mod", [2, 6 * D])
    norm_mix = din("norm_mix", [2, D])
    norm_mlp = din("norm_mlp", [2, D])
    w_in = din("w_in", [2, D, INW])
    lb_logits = din("hgrn_lb_logits", [2, 2, BR])
    hgrn_norm = din("hgrn_norm", [2, BR])
    ret_decay = din("ret_decay", [2, 8])
    ret_norm = din("ret_norm", [2, BR])
    s5_a_re = din("s5_a_re", [2, 2, 32, 64])
    s5_a_im = din("s5_a_im", [2, 2, 32, 64])
    s5_log_dt = din("s5_log_dt", [2, 2, 32])
    s5_b_re = din("s5_b_re", [2, 2, 32, 64, 16])
    s5_b_im = din("s5_b_im", [2, 2, 32, 64, 16])
    s5_c_re = din("s5_c_re", [2, 2, 32, 16, 64])
    s5_c_im = din("s5_c_im", [2, 2, 32, 16, 64])
    s5_d = din("s5_d", [2, BR])
    s5_glu_w = din("s5_glu_w", [2, BR, BR])
    s5_glu_b = din("s5_glu_b", [2, BR])
    conv_w = din("mlstm_conv_w", [2, 3, 3, 2 * BR])
    conv_b = din("mlstm_conv_b", [2, 2 * BR])
    gate_b = din("mlstm_gate_b", [2, 16])
    mlstm_norm = din("mlstm_norm", [2, BR])
    w_branch = din("w_branch", [2, 4, BR, D])
    w_out = din("w_out", [2, D, D])
    w_ff1 = din("w_ff1", [2, D, DFF])
    w_ff2 = din("w_ff2", [2, DFF, D])
    final_norm = din("final_norm", [1, D])
    k_ident = din("ident", [128, 128])
    k_tri = din("tri", [2, 64, 64])
    k_iotaF = din("iotaF", [2, 128, 64])
    k_iotaK = din("iotaK", [2, 64, 128])
    k_rst = din("rst", [128, 16, 64])
    out = nc.dram_tensor("out", [NLAT, D], F32, kind="ExternalOutput").ap()

    SCR = {}

    def scratch(name, shape, dtype):
        t = nc.dram_tensor(name, list(shape), dtype).ap()
        SCR[name] = (t, shape, dtype)
        return t

    xres = scratch("xres", [NT, D], F32)
    hF = scratch("hF", [D, NT], BF16)
    scF = scratch("scF", [D, 2], BF16)
    mod_d = scratch("mod_d", [2, 6 * D], F32)
    FB = {}
    for nm in ("hq", "hkF0", "hkF1", "hog", "rq", "rkF", "rog", "uF", "mz", "mq", "mk"):
        FB[nm] = scratch(nm, [BR, NT], BF16)
    mqk = scratch("mqk", [2 * BR, NT], F32)
    gF = scratch("gF", [4 * D, NT], BF16)
    KB = {}
    for nm in ("hkK0", "hkK1", "hv", "rkK", "rv", "mv"):
        KB[nm] = scratch(nm, [NT, BR], BF16)
    hlogf = [scratch("hlogf0", [NT, BR], F32), scratch("hlogf1", [NT, BR], F32)]
    mg = scratch("mg", [NT, 16], F32)
    rawf = [scratch(f"rawf{j}", [BR, NT], F32) for j in range(4)]
    oF = scratch("oF", [4 * BR, NT], BF16)
    yF = scratch("yF", [D, NT], BF16)
    yaccd = scratch("yaccd", [D, NT], F32)
    aF = scratch("aF", [DFF, NT], BF16)

    PB = [P.ps([128, 1024], F32, name=f"pb{i}") for i in range(4)]
    slot_order = [0, 2, 4, 6, 1, 3, 5, 7]
    st = {"slot": 0, "w": 0, "sf": 0, "sb": 0}

    def next_ps():
        s = slot_order[st["slot"] % 8]
        st["slot"] += 1
        tl = PB[s // 2]
        return tl, tl.t[:, (s % 2) * 512:(s % 2) * 512 + 512]

    stg_f = [P.sb([128, 512], F32, f"stgf{i}") for i in range(4)]
    stg_b = [P.sb([128, 512], BF16, f"stgb{i}") for i in range(4)]

    def sf():
        st["sf"] += 1
        return stg_f[st["sf"] % 4]

    def sbb():
        st["sb"] += 1
        return stg_b[st["sb"] % 4]

    ident = P.sb([128, 128], F32, "ident")
    ident_b = P.sb([128, 128], BF16, "identb")
    ones_f = P.sb([128, 128], F32, "onesf")
    ones_b = P.sb([128, 128], BF16, "onesb")
    tri = [P.sb([64, 64], F32, f"tri{d}") for d in range(2)]
    ntri = [P.sb([64, 64], F32, f"ntri{d}") for d in range(2)]
    iotaF = [P.sb([128, 64], F32, f"iotaF{d}") for d in range(2)]
    iotaK = [P.sb([64, 128], F32, f"iotaK{d}") for d in range(2)]
    rst = P.sb([128, 16, 64], F32, "rst")
    modcol = P.sb([128, 2, 6, 16], F32, "modcol")
    gcolA = P.sb([128, 2, 16], F32, "gcolA")
    ngc = [P.sb([128, 16], F32, f"ngc{i}") for i in range(2)]
    lbrow = [P.sb([128, BR], F32, f"lbrow{d}") for d in range(2)]
    omlrow = [P.sb([128, BR], F32, f"omlrow{d}") for d in range(2)]
    omlcol = [P.sb([128, 4], F32, f"omlcol{d}") for d in range(2)]
    hn_col = [P.sb([128, 4], F32, f"hncol{j}") for j in range(3)]
    gb_row = P.sb([128, 16], F32, "gbrow")
    lgB = P.sb([128, 8], F32, "lgB")
    nlgB = P.sb([128, 8], F32, "nlgB")
    cw = P.sb([128, 8, 9], F32, "cw")
    cbias = P.sb([128, 8], F32, "cbias")
    s5d_col = P.sb([128, 4], F32, "s5dcol")
    glub_col = P.sb([128, 4], F32, "glubcol")
    gluw = P.sb([128, 4, BR], BF16, "gluw")
    ssq = P.sb([128, 4], F32, "ssq")
    P.arena_init(44500)

    def act(out_ap, in_ap, func, reads, writes, **kw):
        P.op("act", lambda e: e.activation(out_ap, in_ap, func, **kw), reads=reads, writes=writes)

    def tt(out_ap, a, b, op, reads, writes, eng="dve"):
        P.op(eng, lambda e: e.tensor_tensor(out_ap, a, b, op), reads=reads, writes=writes)

    def ts(out_ap, a, s1, s2, op0, op1, reads, writes):
        if op1 is None:
            P.op("dve", lambda e: e.tensor_scalar(out_ap, a, s1, None, op0), reads=reads, writes=writes)
        else:
            P.op("dve", lambda e: e.tensor_scalar(out_ap, a, s1, s2, op0, op1), reads=reads, writes=writes)

    def cp(out_ap, in_ap, reads, writes, eng="dve"):
        P.op(eng, lambda e: e.tensor_copy(out_ap, in_ap), reads=reads, writes=writes)

    def mm(out_ap, lhsT, rhs, start, stop, reads, writes):
        P.op("pe", lambda e: e.matmul(out_ap, lhsT=lhsT, rhs=rhs, start=start, stop=stop), reads=reads, writes=writes, sig=bool(stop))

    def memset(tile_, ap, val, eng="dve"):
        P.op(eng, lambda e: e.memset(ap, val), writes=[tile_])

    def recip(tile_, ap):
        P.op("dve", lambda e: e.reciprocal(ap, ap), reads=[tile_], writes=[tile_])

    def small_dma(out_ap, in_ap, tile_, q="sp"):
        P.dma(q, out_ap, in_ap, writes=[tile_], allow_slow_non_contiguous=True)

    P.dma("sp", ident[:], k_ident, writes=[ident])
    cp(ident_b[:], ident[:], [ident], [ident_b])
    memset(ones_f, ones_f[:], 1.0)
    memset(ones_b, ones_b[:], 1.0)
    for d in range(2):
        P.dma("sp", tri[d][:], k_tri[d], writes=[tri[d]])
        ts(ntri[d][:], tri[d][:], -1.0, None, ALU.mult, None, [tri[d]], [ntri[d]])
        P.dma("sp", iotaF[d][:], k_iotaF[d], writes=[iotaF[d]])
        P.dma("sp", iotaK[d][:], k_iotaK[d], writes=[iotaK[d]])
    P.dma("sp", rst[:], k_rst, writes=[rst])
    P.dma("sp", xres, xin)
    P.barrier()

    def gemm(A, Kd, ntok, Wap, jobs, TBLK):
        P.arena_reset()
        ablk = P.ar([128, 36864], BF16, "ablk")
        wbuf = [P.ar([128, 8192], BF16, f"wbuf{i}") for i in range(2)]
        KC = Kd // 128
        for tb0 in range(0, ntok, TBLK):
            tbn = min(TBLK, ntok - tb0)
            av = ablk.t[:, 0:KC * tbn].rearrange("p (k n) -> p k n", k=KC)
            P.dma("sp", av, A[:, tb0:tb0 + tbn].rearrange("(k p) n -> p k n", p=128), writes=[ablk])
            for (col0, ncols, mode, epi) in jobs:
                wt = wbuf[st["w"] % 2]
                st["w"] += 1
                wv = wt.t[:, 0:KC * ncols].rearrange("p (k n) -> p k n", k=KC)
                P.dma("pool", wv, Wap[:, col0:col0 + ncols].rearrange("(k p) n -> p k n", p=128), writes=[wt])
                if mode == "F":
                    for cg in range(0, ncols, 128):
                        cgn = min(128, ncols - cg)
                        for t0 in range(0, tbn, 512):
                            tn = min(512, tbn - t0)
                            pt, pv = next_ps()
                            for k in range(KC):
                                mm(pv[0:cgn, 0:tn], wv[:, k, cg:cg + cgn], av[:, k, t0:t0 + tn], k == 0, k == KC - 1,
                                   [wt, ablk], [pt])
                            epi(pt, pv, cg, cgn, tb0 + t0, tn)
                else:
                    for t0 in range(0, tbn, 128):
                        tn = min(128, tbn - t0)
                        pt, pv = next_ps()
                        for k in range(KC):
                            mm(pv[0:tn, 0:ncols], av[:, k, t0:t0 + tn], wv[:, k, 0:ncols], k == 0, k == KC - 1,
                               [wt, ablk], [pt])
                        epi(pt, pv, tb0 + t0, tn)
        P.barrier()

    def epiF_simple(dst, row0, func, scale=1.0, dtype=BF16):
        def epi(pt, pv, cg, cgn, tok0, tn):
            s = sbb() if dtype == BF16 else sf()
            act(s[0:cgn, 0:tn], pv[0:cgn, 0:tn], func, [pt], [s], scale=scale)
            P.dma("sp", dst[row0 + cg:row0 + cg + cgn, tok0:tok0 + tn], s[0:cgn, 0:tn], reads=[s])
        return epi

    def epiK_simple(dst, func, scale=1.0, dtype=BF16):
        def epi(pt, pv, tok0, tn):
            s = sbb() if dtype == BF16 else sf()
            nco = dst.shape[1]
            act(s[0:tn, 0:nco], pv[0:tn, 0:nco], func, [pt], [s], scale=scale)
            P.dma("sp", dst[tok0:tok0 + tn, :], s[0:tn, 0:nco], reads=[s])
        return epi

    def layer_setup(l):
        cc = sf()
        ccv = cc[:, 0:32].rearrange("p (j v) -> p j v", v=2)
        for v in range(2):
            small_dma(ccv[:, :, v], c2[v].rearrange("(j p) -> p j", p=128), cc)
        cb = sbb()
        cbv = cb[:, 0:32].rearrange("p (j v) -> p j v", v=2)
        act(cbv, ccv, AF.Silu, [cc], [cb])
        P.dma("sp", scF.rearrange("(j p) v -> p j v", p=128), cbv, reads=[cb], allow_slow_non_contiguous=True)
        P.barrier()

        def epi_mod_factory(col0):
            def epi(pt, pv, tok0, tn):
                bm = sf()
                P.dma("act", bm[0:2, :], b_mod[l:l + 1, col0:col0 + 512].partition_broadcast(2), writes=[bm])
                s = sf()
                tt(s[0:2, 0:512], pv[0:2, 0:512], bm[0:2, :], ALU.add, [pt, bm], [s])
                P.dma("sp", mod_d[:, col0:col0 + 512], s[0:2, 0:512], reads=[s])
            return epi
        jobs = [(c0, 512, "K", epi_mod_factory(c0)) for c0 in range(0, 6 * D, 512)]
        gemm(scF, D, 2, w_mod[l], jobs, 128)
        for v in range(2):
            for m_ in range(6):
                small_dma(modcol[:, v, m_, :], mod_d[v, m_ * D:(m_ + 1) * D].rearrange("(j p) -> p j", p=128), modcol)
        small_dma(ngc[0][:], norm_mix[l].rearrange("(j p) -> p j", p=128), ngc[0])
        small_dma(ngc[1][:], norm_mlp[l].rearrange("(j p) -> p j", p=128), ngc[1])
        for d in range(2):
            if l == 0:
                memset(lbrow[d], lbrow[d][:], 0.0)
                memset(omlrow[d], omlrow[d][:], 1.0)
                memset(omlcol[d], omlcol[d][:], 1.0)
            else:
                a0 = sf()
                a1 = sf()
                P.dma("sp", a0[:], lb_logits[0, d:d + 1, :].partition_broadcast(128), writes=[a0])
                P.dma("sp", a1[:], lb_logits[1, d:d + 1, :].partition_broadcast(128), writes=[a1])
                tt(a1[:], a1[:], a0[:], ALU.subtract, [a0, a1], [a1])
                act(lbrow[d][:], a1[:], AF.Sigmoid, [a1], [lbrow[d]])
                ts(omlrow[d][:], lbrow[d][:], -1.0, 1.0, ALU.mult, ALU.add, [lbrow[d]], [omlrow[d]])
                c0_ = sf()
                small_dma(c0_[:, 0:4], lb_logits[0, d].rearrange("(h p) -> p h", p=128), c0_)
                small_dma(c0_[:, 4:8], lb_logits[1, d].rearrange("(h p) -> p h", p=128), c0_)
                tt(c0_[:, 8:12], c0_[:, 0:4], c0_[:, 4:8], ALU.subtract, [c0_], [c0_])
                act(omlcol[d][:], c0_[:, 8:12], AF.Sigmoid, [c0_], [omlcol[d]])
        for j, g in enumerate((hgrn_norm, ret_norm, mlstm_norm)):
            small_dma(hn_col[j][:], g[l].rearrange("(h p) -> p h", p=128), hn_col[j])
        P.dma("sp", gb_row[:], gate_b[l:l + 1, :].partition_broadcast(128), writes=[gb_row])
        P.dma("sp", lgB[:], ret_decay[l:l + 1, :].partition_broadcast(128), writes=[lgB])
        act(lgB[:], lgB[:], AF.Exp, [lgB], [lgB])
        ts(lgB[:], lgB[:], -1.0, 1.0, ALU.mult, ALU.add, [lgB], [lgB])
        act(lgB[:], lgB[:], AF.Ln, [lgB], [lgB])
        ts(nlgB[:], lgB[:], -1.0, None, ALU.mult, None, [lgB], [nlgB])
        for cc_ in range(8):
            small_dma(cw[:, cc_, :], conv_w[l][:, :, cc_ * 128:(cc_ + 1) * 128].rearrange("a b p -> p (a b)"), cw)
        small_dma(cbias[:], conv_b[l].rearrange("(cc p) -> p cc", p=128), cbias)
        small_dma(s5d_col[:], s5_d[l].rearrange("(cc p) -> p cc", p=128), s5d_col)
        small_dma(glub_col[:], s5_glu_b[l].rearrange("(cc p) -> p cc", p=128), glub_col)
        P.dma("pool", gluw[:], s5_glu_w[l].rearrange("(k p) n -> p k n", p=128), writes=[gluw])
        P.barrier()

    def rstd_of(tile_x, junk):
        memset(ssq, ssq[:, 0:1], 0.0)
        act(junk[:], tile_x[:], AF.Square, [tile_x, ssq], [junk, ssq], accum_out=ssq[:, 0:1])
        act(ssq[:, 1:2], ssq[:, 0:1], AF.Ln, [ssq], [ssq], scale=1.0 / D, bias=EPS)
        act(ssq[:, 2:3], ssq[:, 1:2], AF.Exp, [ssq], [ssq], scale=-0.5)
        return ssq[:, 2:3]

    def norm_phase(which):
        P.arena_reset()
        xt = [P.ar([128, D], F32, f"xt{i}") for i in range(2)]
        xn = P.ar([128, D], F32, "xn")
        hT = [P.ar([128, 16, 128], BF16, f"hT{i}") for i in range(2)]
        junk = P.ar([128, D], BF16, "junk")
        m_shift, m_scale = (0, 1) if which == 0 else (3, 4)
        for v in range(2):
            ts(gcolA[:, v, :], modcol[:, v, m_scale, :], 1.0, None, ALU.add, None, [modcol], [gcolA])
            tt(gcolA[:, v, :], gcolA[:, v, :], ngc[which][:], ALU.mult, [gcolA, ngc[which]], [gcolA])
        for ti in range(NT // 128):
            v = 1 if ti < 2 else 0
            x_ = xt[ti % 2]
            P.dma("sp", x_[:], xres[ti * 128:(ti + 1) * 128, :], writes=[x_])
            r = rstd_of(x_, junk)
            ts(xn[:], x_[:], r, None, ALU.mult, None, [x_, ssq], [xn])
            h_ = hT[ti % 2]
            for j in range(16):
                pt, pv = next_ps()
                mm(pv[:, 0:128], xn[:, j * 128:(j + 1) * 128], ident[:], True, True, [xn, ident], [pt])
                act(h_[:, j, :], pv[:, 0:128], AF.Identity, [pt, gcolA, modcol], [h_],
                    scale=gcolA[:, v, j:j + 1], bias=modcol[:, v, m_shift, j:j + 1])
            P.dma("sp", hF[:, ti * 128:(ti + 1) * 128].rearrange("(j p) n -> p j n", p=128), h_[:], reads=[h_])
        P.barrier()

    def win_phase(l):
        jobs = []
        jobs.append((0, 512, "F", epiF_simple(FB["hq"], 0, AF.Silu)))

        def epi_hk(d):
            def epi(pt, pv, cg, cgn, tok0, tn):
                s = sf()
                act(s[:, 0:tn], pv[:, 0:tn], AF.Sigmoid, [pt], [s], scale=-1.0)
                sb_ = sbb()
                h = cg // 128
                ts(sb_[:, 0:tn], s[:, 0:tn], omlcol[d][:, h:h + 1], None, ALU.mult, None, [s, omlcol[d]], [sb_])
                P.dma("sp", FB[f"hkF{d}"][cg:cg + 128, tok0:tok0 + tn], sb_[:, 0:tn], reads=[sb_])
            return epi
        jobs.append((512, 512, "F", epi_hk(0)))
        jobs.append((1024, 512, "F", epi_hk(1)))
        jobs.append((2048, 512, "F", epiF_simple(FB["hog"], 0, AF.Silu)))
        jobs.append((2560, 512, "F", epiF_simple(FB["rq"], 0, AF.Identity)))
        jobs.append((3072, 512, "F", epiF_simple(FB["rkF"], 0, AF.Identity, scale=128 ** -0.5)))
        jobs.append((4096, 512, "F", epiF_simple(FB["rog"], 0, AF.Silu)))
        jobs.append((4608, 512, "F", epiF_simple(FB["uF"], 0, AF.Identity)))
        jobs.append((5120, 512, "F", epiF_simple(mqk, 0, AF.Identity, dtype=F32)))
        jobs.append((5632, 512, "F", epiF_simple(mqk, 512, AF.Identity, dtype=F32)))
        jobs.append((6656, 512, "F", epiF_simple(FB["mz"], 0, AF.Silu)))
        for gblk in range(16):
            jobs.append((7184 + gblk * 512, 512, "F", epiF_simple(gF, gblk * 512, AF.Sigmoid)))

        def epi_hf(d):
            def epi(pt, pv, tok0, tn):
                s = sf()
                act(s[0:tn, :], pv[0:tn, :], AF.Sigmoid, [pt], [s])
                tt(s[0:tn, :], s[0:tn, :], omlrow[d][0:tn, :], ALU.mult, [s, omlrow[d]], [s])
                tt(s[0:tn, :], s[0:tn, :], lbrow[d][0:tn, :], ALU.add, [s, lbrow[d]], [s])
                kb = sbb()
                ts(kb[0:tn, :], s[0:tn, :], -1.0, 1.0, ALU.mult, ALU.add, [s], [kb])
                P.dma("sp", KB[f"hkK{d}"][tok0:tok0 + tn, :], kb[0:tn, :], reads=[kb])
                ts(s[0:tn, :], s[0:tn, :], 1e-30, None, ALU.max, None, [s], [s])
                s2 = sf()
                act(s2[0:tn, :], s[0:tn, :], AF.Ln, [s], [s2])
                P.dma("sp", hlogf[d][tok0:tok0 + tn, :], s2[0:tn, :], reads=[s2])
            return epi
        jobs.append((512, 512, "K", epi_hf(0)))
        jobs.append((1024, 512, "K", epi_hf(1)))
        jobs.append((1536, 512, "K", epiK_simple(KB["hv"], AF.Identity)))
        jobs.append((3072, 512, "K", epiK_simple(KB["rkK"], AF.Identity, scale=128 ** -0.5)))
        jobs.append((3584, 512, "K", epiK_simple(KB["rv"], AF.Identity)))
        jobs.append((6144, 512, "K", epiK_simple(KB["mv"], AF.Identity)))

        def epi_mg(pt, pv, tok0, tn):
            s = sf()
            tt(s[0:tn, 0:16], pv[0:tn, 0:16], gb_row[0:tn, :], ALU.add, [pt, gb_row], [s])
            sv = s[0:tn, 0:16].rearrange("p (d i h) -> p d i h", d=2, i=2)
            s2 = sf()
            s2v = s2[0:tn, 0:16].rearrange("p (d i h) -> p d i h", d=2, i=2)
            act(s2v[:, :, 1, :], sv[:, :, 1, :], AF.Sigmoid, [s], [s2])
            act(sv[:, :, 1, :], s2v[:, :, 1, :], AF.Ln, [s2], [s])
            P.dma("sp", mg[tok0:tok0 + tn, :], s[0:tn, 0:16], reads=[s])
        jobs.append((7168, 16, "K", epi_mg))
        TB = 2176 if NT >= 2176 else NT
        gemm(hF, D, NT, w_in[l], jobs, TB)

    def conv_phase():
        P.arena_reset()
        cin = P.ar([128, NT], F32, "cin")
        cout = P.ar([128, NT], F32, "cout")
        cob = P.ar([128, NT], BF16, "cob")
        for cc in range(8):
            P.dma("sp", cin[:], mqk[cc * 128:(cc + 1) * 128, :], writes=[cin])

            def w(a, b, cc=cc):
                return cw[:, cc, a * 3 + b:a * 3 + b + 1]
            ts(cout[:], cin[:], w(1, 1), cbias[:, cc:cc + 1], ALU.mult, ALU.add, [cin, cw, cbias], [cout])

            def acc(o_ap, i_ap, wap):
                P.op("dve", lambda e: e.scalar_tensor_tensor(o_ap, i_ap, wap, o_ap, ALU.mult, ALU.add),
                     reads=[cin, cout, cw], writes=[cout])
            acc(cout[:, 1:CTX], cin[:, 0:CTX - 1], w(1, 0))
            acc(cout[:, 0:CTX - 1], cin[:, 1:CTX], w(1, 2))
            ci = cin[:, CTX:NT].rearrange("p (r c) -> p r c", c=64)
            co = cout[:, CTX:NT].rearrange("p (r c) -> p r c", c=64)
            for dr in (-1, 0, 1):
                for dc in (-1, 0, 1):
                    if dr == 0 and dc == 0:
                        continue
                    ro = slice(max(0, -dr), ROWS - max(0, dr))
                    ri = slice(max(0, dr), ROWS - max(0, -dr))
                    co_ = slice(max(0, -dc), 64 - max(0, dc))
                    ci_ = slice(max(0, dc), 64 - max(0, -dc))
                    acc(co[:, ro, co_], ci[:, ri, ci_], w(dr + 1, dc + 1))
            act(cob[:], cout[:], AF.Silu, [cout], [cob])
            if cc < 4:
                P.dma("sp", FB["mq"][cc * 128:(cc + 1) * 128, :], cob[:], reads=[cob])
            else:
                ts(cob[:], cob[:], 128 ** -0.5, None, ALU.mult, None, [cob], [cob])
                P.dma("sp", FB["mk"][(cc - 4) * 128:(cc - 3) * 128, :], cob[:], reads=[cob])
        P.barrier()

    def sc_order(d):
        if d == 0:
            return list(range(NSC))
        return [0] + list(range(NSC - 1, 0, -1))

    def cmul(or_, oi_, ar, ai, br, bi, t1, t2, reads, writes):
        wr, wi = writes[0:1], writes[1:2]
        if writes[0] is writes[1]:
            tt(t1[1], ar, br, ALU.mult, reads, [t1[0]])
            tt(t2[1], ai, bi, ALU.mult, reads, [t2[0]])
            tt(or_, t1[1], t2[1], ALU.subtract, [t1[0], t2[0]], wr)
            tt(t1[1], ar, bi, ALU.mult, reads, [t1[0]])
            tt(t2[1], ai, br, ALU.mult, reads, [t2[0]])
            tt(oi_, t1[1], t2[1], ALU.add, [t1[0], t2[0]], wi)
            return
        tt(or_, ar, br, ALU.mult, reads, wr)
        tt(t1[1], ai, bi, ALU.mult, reads, [t1[0]])
        tt(oi_, ar, bi, ALU.mult, reads, wi)
        tt(t2[1], ai, br, ALU.mult, reads, [t2[0]])
        tt(or_, or_, t1[1], ALU.subtract, [writes[0], t1[0]], wr)
        tt(oi_, oi_, t2[1], ALU.add, [writes[1], t2[0]], wi)

    def scan_phase(l, d):
        P.arena_reset()
        tl = 63 if d == 0 else 0
        A = P.ar
        qF = A([128, 4, 256], BF16, "qF")
        kF = A([128, 4, 256], BF16, "kF")
        kK = A([64, 4, 512], BF16, "kK")
        vK = A([64, 4, 4, 129], BF16, "vK")
        lf = A([64, 4, 512], F32, "lf")
        g = A([64, 4, 16], F32, "gK")
        EpF = A([128, 4, 256], F32, "EpF")
        EmF = A([128, 4, 256], F32, "EmF")
        EkK = A([64, 4, 512], F32, "EkK")
        Ekm = A([64, 4, 4], F32, "Ekm")
        Et = A([128, 4, 4], F32, "Et")
        qp = A([128, 4, 256], BF16, "qp")
        kpF = A([128, 4, 256], BF16, "kpF")
        kpK = A([64, 4, 512], BF16, "kpK")
        PT = A([64, 16, 64], BF16, "PT")
        Sst = A([128, 4, 129], F32, "Sst")
        Sbf = A([128, 4, 129], BF16, "Sbf")
        nbc = A([128, 4, 128], BF16, "nbc")
        Otile = A([128, 4, 256], F32, "Otile")
        Dtile = A([128, 4, 256], F32, "Dtile")
        rawp = A([128, 4, 256], F32, "rawp")
        sq_t = A([128, 4, 256], F32, "sq_t")
        gate_t = A([128, 4, 256], BF16, "gate_t")
        o_bf = A([128, 4, 256], BF16, "o_bf")
        CPt = A([128, 4, 16], F32, "CPt")
        stri = A([64, 64], F32, "stri")
        rEp = A([128, 4, 256], F32, "rEp")
        rEm = A([128, 4, 256], F32, "rEm")
        rEk = A([64, 4, 512], F32, "rEk")
        rEt = A([128, 4, 4], F32, "rEt")
        qpb = A([128, 4, 256], BF16, "qpb", at=kpF.off)
        kpI = [A([128, 4, 256], BF16, "kpI0", at=rEp.off), A([128, 4, 256], BF16, "kpI1", at=rEp.off + 512),
               A([128, 4, 256], BF16, "kpI2", at=rEm.off), A([128, 4, 256], BF16, "kpI3", at=rEm.off + 512)]
        LPr = A([128, 16, 64], F32, "LPr")
        LPi = A([128, 16, 64], F32, "LPi")
        LMr = A([128, 16, 64], F32, "LMr")
        LMi = A([128, 16, 64], F32, "LMi")
        Bpad = A([128, 16, 2, 128], BF16, "Bpad")
        Cpad = A([128, 16, 2, 128], BF16, "Cpad")
        Zr = A([128, 16, 64], F32, "Zr")
        Zi = A([128, 16, 64], F32, "Zi")
        Wr = A([128, 16, 64], F32, "Wr")
        Wi = A([128, 16, 64], F32, "Wi")
        T1 = A([128, 16, 64], F32, "T1")
        T2 = A([128, 16, 64], F32, "T2")
        Xrb = A([128, 16, 64], BF16, "Xrb")
        Xib = A([128, 16, 64], BF16, "Xib")
        xpr = A([128, 16], F32, "xpr")
        xpi = A([128, 16], F32, "xpi")
        tot = A([128, 16], F32, "tot")
        u = A([128, 4, 256], BF16, "u")
        yc_b = A([128, 4, 256], BF16, "yc_b")
        sm = [A([128, 16], F32, f"sm{i}") for i in range(14)]
        smi = A([128, 16], I32, "smi")

        def ret_tables():
            for h in range(4):
                c_ = d * 4 + h
                for c in range(4):
                    act(rEp[:, h, c * 64:(c + 1) * 64], iotaF[d][:], AF.Exp, [iotaF[d], lgB], [rEp], scale=lgB[:, c_:c_ + 1])
                    act(rEm[:, h, c * 64:(c + 1) * 64], iotaF[d][:], AF.Exp, [iotaF[d], nlgB], [rEm], scale=nlgB[:, c_:c_ + 1])
                    act(rEk[:, c, h * 128:(h + 1) * 128], iotaK[d][:], AF.Exp, [iotaK[d], nlgB], [rEk], scale=nlgB[0:64, c_:c_ + 1])
                    cp(rEt[:, c, h:h + 1], rEp[:, h, tl:tl + 1], [rEp], [rEt])

        tt(stri[:], tri[1 - d][:], ident[0:64, 0:64], ALU.subtract, [tri[1 - d], ident], [stri])

        def sin_of(dst_t, theta_t, tmp):
            ts(tmp[:], theta_t[:], 1.0 / (2 * math.pi), None, ALU.mult, None, [theta_t], [tmp])
            cp(smi[:], tmp[:], [tmp], [smi])
            cp(tmp[:], smi[:], [smi], [tmp])
            P.op("dve", lambda e: e.scalar_tensor_tensor(tmp[:], tmp[:], -2 * math.pi, theta_t[:], ALU.mult, ALU.add),
                 reads=[tmp, theta_t], writes=[tmp])
            act(dst_t[:], tmp[:], AF.Sin, [tmp], [dst_t])

        def pow_table(Tr, Ti, br, bi, rev):
            c0 = 63 if rev else 0
            cp(Tr[:, :, c0], br[:], [br], [Tr])
            cp(Ti[:, :, c0], bi[:], [bi], [Ti])
            n = 1
            while n < 64:
                if not rev:
                    src, dst, top = slice(0, n), slice(n, 2 * n), n - 1
                else:
                    src, dst, top = slice(64 - n, 64), slice(64 - 2 * n, 64 - n), 64 - n
                fr_ = Tr[:, :, top:top + 1].to_broadcast([128, 16, n])
                fi_ = Ti[:, :, top:top + 1].to_broadcast([128, 16, n])
                cmul(Tr[:, :, dst], Ti[:, :, dst], Tr[:, :, src], Ti[:, :, src], fr_, fi_,
                     (T1, T1[:, :, 0:n]), (T2, T2[:, :, 0:n]), [Tr, Ti], [Tr, Ti])
                n *= 2

        are, aim, dt_, t0_, t1_, lre, lim, den, fr, fi, ire, iim, t2_, t3_ = sm
        for (tile_, src) in ((are, s5_a_re), (aim, s5_a_im)):
            v_ = src[l, d].rearrange("(j two) p -> two p j", two=2)
            for two in range(2):
                small_dma(tile_[64 * two:64 * two + 64, :], v_[two], tile_)
        ldt = s5_log_dt[l, d].rearrange("(j two) -> two j", two=2)
        for two in range(2):
            small_dma(dt_[64 * two:64 * two + 64, :], ldt[two:two + 1, :].partition_broadcast(64), dt_)
        act(dt_[:], dt_[:], AF.Exp, [dt_], [dt_])
        tt(t0_[:], are[:], dt_[:], ALU.mult, [are, dt_], [t0_])
        act(t0_[:], t0_[:], AF.Exp, [t0_], [t0_])
        tt(t1_[:], aim[:], dt_[:], ALU.mult, [aim, dt_], [t1_])
        sin_of(lim, t1_, t2_)
        ts(t3_[:], t1_[:], math.pi / 2, None, ALU.add, None, [t1_], [t3_])
        sin_of(lre, t3_, t2_)
        tt(lre[:], lre[:], t0_[:], ALU.mult, [lre, t0_], [lre])
        tt(lim[:], lim[:], t0_[:], ALU.mult, [lim, t0_], [lim])
        tt(den[:], are[:], are[:], ALU.mult, [are], [den])
        tt(t2_[:], aim[:], aim[:], ALU.mult, [aim], [t2_])
        tt(den[:], den[:], t2_[:], ALU.add, [den, t2_], [den])
        recip(den, den[:])
        ts(t0_[:], lre[:], -1.0, None, ALU.add, None, [lre], [t0_])
        tt(fr[:], t0_[:], are[:], ALU.mult, [t0_, are], [fr])
        tt(t2_[:], lim[:], aim[:], ALU.mult, [lim, aim], [t2_])
        tt(fr[:], fr[:], t2_[:], ALU.add, [fr, t2_], [fr])
        tt(fr[:], fr[:], den[:], ALU.mult, [fr, den], [fr])
        tt(fi[:], lim[:], are[:], ALU.mult, [lim, are], [fi])
        tt(t2_[:], t0_[:], aim[:], ALU.mult, [t0_, aim], [t2_])
        tt(fi[:], fi[:], t2_[:], ALU.subtract, [fi, t2_], [fi])
        tt(fi[:], fi[:], den[:], ALU.mult, [fi, den], [fi])
        tt(t0_[:], lre[:], lre[:], ALU.mult, [lre], [t0_])
        tt(t2_[:], lim[:], lim[:], ALU.mult, [lim], [t2_])
        tt(t0_[:], t0_[:], t2_[:], ALU.add, [t0_, t2_], [t0_])
        recip(t0_, t0_[:])
        tt(ire[:], lre[:], t0_[:], ALU.mult, [lre, t0_], [ire])
        tt(iim[:], lim[:], t0_[:], ALU.mult, [lim, t0_], [iim])
        ts(iim[:], iim[:], -1.0, None, ALU.mult, None, [iim], [iim])
        rev = (d == 1)
        pow_table(LPr, LPi, lre, lim, rev)
        pow_table(LMr, LMi, ire, iim, rev)
        bre = Tile(Zr.t[:, :, 0:16], "bre")
        bim = Tile(Zr.t[:, :, 16:32], "bim")
        bbr = Tile(Zr.t[:, :, 32:48], "bbr")
        bbi = Tile(Zr.t[:, :, 48:64], "bbi")
        tA = Tile(Zi.t[:, :, 0:16], "tA")
        tB = Tile(Zi.t[:, :, 16:32], "tB")
        bpadf = A([128, 16, 128], F32, "bpadf", at=Wr.off)
        assert Wi.off == Wr.off + 1024
        for (tile_, src) in ((bre, s5_b_re), (bim, s5_b_im)):
            v_ = src[l, d].rearrange("(j two) p h -> two p j h", two=2)
            for two in range(2):
                small_dma(tile_[64 * two:64 * two + 64, :, :], v_[two], tile_)
        frb = fr[:].unsqueeze(2).to_broadcast([128, 16, 16])
        fib = fi[:].unsqueeze(2).to_broadcast([128, 16, 16])
        cmul(bbr[:], bbi[:], bre[:], bim[:], frb, fib, (tA, tA[:]), (tB, tB[:]), [bre, bim, fr, fi], [bbr, bbi])
        for part, src_t in enumerate((bbr, bbi)):
            memset(bpadf, bpadf[:], 0.0)
            for half in range(2):
                for jj in range(4):
                    cp(bpadf[64 * half:64 * half + 64, jj::4, 32 * jj + 16 * half:32 * jj + 16 * half + 16],
                       src_t[64 * half:64 * half + 64, jj::4, :], [src_t], [bpadf])
            for j in range(16):
                pt, pv = next_ps()
                mm(pv[:, 0:128], bpadf[:, j, :], ident[:], True, True, [bpadf, ident], [pt])
                cp(Bpad[:, j, part, :], pv[:, 0:128], [pt], [Bpad])
        for part, src in enumerate((s5_c_re, s5_c_im)):
            for two in range(2):
                for j_ in range(16):
                    small_dma(bre[64 * two:64 * two + 64, j_, :], src[l, d, 2 * j_ + two].rearrange("h p -> p h"), bre)
            memset(bpadf, bpadf[:], 0.0)
            for half in range(2):
                for jj in range(4):
                    cp(bpadf[64 * half:64 * half + 64, jj::4, 32 * jj + 16 * half:32 * jj + 16 * half + 16],
                       bre[64 * half:64 * half + 64, jj::4, :], [bre], [bpadf])
            if part == 0:
                cp(Cpad[:, :, 0, :], bpadf[:], [bpadf], [Cpad])
            else:
                ts(Cpad[:, :, 1, :], bpadf[:], -1.0, None, ALU.mult, None, [bpadf], [Cpad])
        for al in (bre, bim, bbr, bbi):
            Zr.readers.update(al.readers); Zr.writers.update(al.writers)
        for al in (tA, tB):
            Zi.readers.update(al.readers); Zi.writers.update(al.writers)

        P.barrier()
        if stop_after == ("s_tab" + ("1" if d else ""), l):
            return True

        def gla_scan(mix, j_out, gcol):
            memset(Sst, Sst[:], 0.0)
            memset(Sbf, Sbf[:], 0.0)
            memset(nbc, nbc[:], 0.0)
            memset(vK, vK[:], 1.0)
            if mix == "h":
                qs, kFs, kKs, vs, gs = FB["hq"], FB[f"hkF{d}"], KB[f"hkK{d}"], KB["hv"], FB["hog"]
            elif mix == "r":
                qs, kFs, kKs, vs, gs = FB["rq"], FB["rkF"], KB["rkK"], KB["rv"], FB["rog"]
            else:
                qs, kFs, kKs, vs, gs = FB["mq"], FB["mk"], None, KB["mv"], FB["mz"]
            for sc in sc_order(d):
                n0 = sc * 256
                P.dma("sp", qF[:], qs[:, n0:n0 + 256].rearrange("(h p) n -> p h n", p=128), writes=[qF])
                P.dma("sp", kF[:], kFs[:, n0:n0 + 256].rearrange("(h p) n -> p h n", p=128), writes=[kF])
                vsrc = vs[n0:n0 + 256, :].rearrange("(c s) (h v) -> s c h v", s=64, v=128)
                for c in range(4):
                    P.dma("act", vK[:, c, :, 0:128], vsrc[:, c], writes=[vK])
                if kKs is not None:
                    P.dma("act", kK[:], kKs[n0:n0 + 256, :].rearrange("(c s) x -> s c x", s=64), writes=[kK])
                else:
                    for c in range(4):
                        pt, pv = next_ps()
                        for h in range(4):
                            mm(pv[0:64, h * 128:(h + 1) * 128], kF[:, h, c * 64:(c + 1) * 64], ident_b[:], True, True, [kF, ident_b], [pt])
                        act(kK[:, c, :], pv[0:64, :], AF.Identity, [pt], [kK])
                if mix == "r":
                    EP, EM, EK, ET = rEp, rEm, rEk, rEt
                    EK_ap = rEk[:]
                elif mix == "h":
                    P.dma("act", lf[:], hlogf[d][n0:n0 + 256, :].rearrange("(c s) x -> s c x", s=64), writes=[lf])
                    pe_ = PB[2]
                    for h in range(4):
                        for c in range(4):
                            mm(pe_[:, h * 256 + c * 64:h * 256 + c * 64 + 64], lf[:, c, h * 128:(h + 1) * 128], tri[d][:], True, True, [lf, tri[d]], [pe_])
                    act(EpF[:].rearrange("p h n -> p (h n)"), pe_[:], AF.Exp, [pe_], [EpF])
                    cumS = sq_t
                    act(cumS[:].rearrange("p h n -> p (h n)"), pe_[:], AF.Identity, [pe_], [cumS])
                    for cp_ in range(2):
                        pk = PB[3]
                        for c2_ in range(2):
                            c = cp_ * 2 + c2_
                            mm(pk[0:64, c2_ * 512:(c2_ + 1) * 512], stri[:], lf[:, c, :], True, True, [lf, stri], [pk])
                        act(EkK[:, cp_ * 2:cp_ * 2 + 2, :].rearrange("p c x -> p (c x)"), pk[0:64, :], AF.Exp, [pk], [EkK])
                    cp(Et[:].rearrange("p c h -> p h c"), EpF[:, :, tl::64], [EpF], [Et])
                    memset(CPt, CPt[:], 0.0)
                    c5 = cumS[:].rearrange("p h (c b t) -> p h c b t", c=4, b=4)
                    cpv = CPt[:].rearrange("p h (c b) -> p h c b", c=4)
                    if d == 0:
                        cp(cpv[:, :, :, 1:4], c5[:, :, :, 0:3, 15], [cumS], [CPt])
                    else:
                        cp(cpv[:, :, :, 0:3], c5[:, :, :, 1:4, 0], [cumS], [CPt])
                    Rt = Dtile
                    cp(Rt[:].rearrange("p h (k t) -> p h k t", t=16), CPt[:].unsqueeze(3).to_broadcast([128, 4, 16, 16]), [CPt], [Rt])
                    tt(Rt[:], cumS[:], Rt[:], ALU.subtract, [cumS, Rt], [Rt])
                    act(Rt[:], Rt[:], AF.Exp, [Rt], [Rt])
                    tt(qpb[:], qF[:], Rt[:], ALU.mult, [qF, Rt], [qpb])
                    for I in range(4):
                        tmpE = EmF
                        tt(tmpE[:].rearrange("p h (c t) -> p h c t", t=64),
                           cpv[:, :, :, I:I + 1].to_broadcast([128, 4, 4, 64]),
                           cumS[:].rearrange("p h (c t) -> p h c t", t=64), ALU.subtract, [CPt, cumS], [tmpE])
                        ts(tmpE[:], tmpE[:], 80.0, None, ALU.min, None, [tmpE], [tmpE])
                        act(tmpE[:], tmpE[:], AF.Exp, [tmpE], [tmpE])
                        tt(kpI[I][:], kF[:], tmpE[:], ALU.mult, [kF, tmpE], [kpI[I]])
                    EP, EM, EK, ET = EpF, EmF, EkK, Et
                    EK_ap = EkK[:]
                else:
                    P.dma("act", g[:], mg[n0:n0 + 256, :].rearrange("(c s) x -> s c x", s=64), writes=[g])
                    pe_ = PB[2]
                    pm_ = PB[3]
                    for h in range(4):
                        fcol = d * 8 + 4 + h
                        icol = d * 8 + h
                        for c in range(4):
                            sl = slice(h * 256 + c * 64, h * 256 + c * 64 + 64)
                            mm(pe_[:, sl], g[:, c, fcol:fcol + 1].to_broadcast([64, 128]), tri[d][:], True, True, [g, tri[d]], [pe_])
                            mm(pm_[:, sl], g[:, c, fcol:fcol + 1].to_broadcast([64, 128]), ntri[d][:], True, False, [g, ntri[d]], [pm_])
                            mm(pm_[:, sl], g[:, c, icol:icol + 1].to_broadcast([64, 128]), ident[0:64, 0:64], False, True, [g, ident], [pm_])
                    act(EpF[:].rearrange("p h n -> p (h n)"), pe_[:], AF.Exp, [pe_], [EpF])
                    act(EmF[:].rearrange("p h n -> p (h n)"), pm_[:], AF.Exp, [pm_], [EmF])
                    pt, pv = next_ps()
                    for c in range(4):
                        mm(pv[0:64, c * 16:(c + 1) * 16], tri[d][:], g[:, c, :], True, True, [g, tri[d]], [pt])
                    pvv = pv[0:64, 0:64].rearrange("p (c x) -> p c x", x=16)
                    tt(Ekm[:], g[:, :, d * 8:d * 8 + 4], pvv[:, :, d * 8 + 4:d * 8 + 8], ALU.subtract, [g, pt], [Ekm])
                    act(Ekm[:], Ekm[:], AF.Exp, [Ekm], [Ekm])
                    cp(Et[:].rearrange("p c h -> p h c"), EpF[:, :, tl::64], [EpF], [Et])
                    EP, EM, EK, ET = EpF, EmF, Ekm, Et
                    EK_ap = Ekm[:].unsqueeze(3).to_broadcast([64, 4, 4, 128])
                tt(qp[:], qF[:], EP[:], ALU.mult, [qF, EP], [qp])
                if mix != "h":
                    tt(kpF[:], kF[:], EM[:], ALU.mult, [kF, EM], [kpF])
                if mix == "m":
                    tt(kpK[:].rearrange("p c (h x) -> p c h x", x=128), kK[:].rearrange("p c (h x) -> p c h x", x=128), EK_ap, ALU.mult, [kK, EK], [kpK])
                else:
                    tt(kpK[:], kK[:], EK_ap, ALU.mult, [kK, EK], [kpK])
                pa = PB[0]
                for c in range(4):
                    for h in range(4):
                        blk = c * 4 + h
                        if mix == "h":
                            for I in range(4):
                                mm(pa[0:64, blk * 64 + 16 * I:blk * 64 + 16 * I + 16], kpI[I][:, h, c * 64:(c + 1) * 64],
                                   qpb[:, h, c * 64 + 16 * I:c * 64 + 16 * I + 16], True, True, [kpI[I], qpb], [pa])
                        else:
                            mm(pa[0:64, blk * 64:(blk + 1) * 64], kpF[:, h, c * 64:(c + 1) * 64], qp[:, h, c * 64:(c + 1) * 64], True, True, [kpF, qp], [pa])
                tt(PT[:], pa[0:64, :].rearrange("p (b t) -> p b t", t=64), tri[d][:].unsqueeze(1).to_broadcast([64, 16, 64]), ALU.mult, [pa, tri[d]], [PT])
                po = PB[1]
                pd = PB[2]
                corder = range(4) if d == 0 else range(3, -1, -1)
                for c in corder:
                    pu = PB[3]
                    for h in range(4):
                        mm(pu[:, h * 256:h * 256 + 129], kpK[:, c, h * 128:(h + 1) * 128], vK[:, c, h, :], True, True, [kpK, vK], [pu])
                    for h in range(4):
                        sl = slice(h * 256 + c * 64, h * 256 + c * 64 + 64)
                        mm(po[:, sl], vK[:, c, h, 0:128], PT[:, c * 4 + h, :], True, False, [vK, PT], [po])
                        mm(po[:, sl], Sbf[:, h, 0:128], qp[:, h, c * 64:(c + 1) * 64], False, True, [Sbf, qp], [po])
                        if mix == "m":
                            mm(pd[:, sl], ones_b[0:64, :], PT[:, c * 4 + h, :], True, False, [ones_b, PT], [pd])
                            mm(pd[:, sl], nbc[:, h, :], qp[:, h, c * 64:(c + 1) * 64], False, True, [nbc, qp], [pd])
                    puv = pu[:].rearrange("p (h x) -> p h x", x=256)[:, :, 0:129]
                    if mix == "h":
                        tt(Sst[:], Sst[:], ET[:, c, :].unsqueeze(2).to_broadcast([128, 4, 129]), ALU.mult, [Sst, ET], [Sst])
                        tt(Sst[:], Sst[:], puv, ALU.add, [Sst, pu], [Sst])
                    else:
                        tt(Sst[:], Sst[:], puv, ALU.add, [Sst, pu], [Sst])
                        tt(Sst[:], Sst[:], ET[:, c, :].unsqueeze(2).to_broadcast([128, 4, 129]), ALU.mult, [Sst, ET], [Sst])
                    act(Sbf[:], Sst[:], AF.Identity, [Sst], [Sbf])
                    if mix == "m":
                        cp(nbc[:], Sst[:, :, 128:129].to_broadcast([128, 4, 128]), [Sst], [nbc])
                Ov = Otile[:].rearrange("p h n -> p (h n)")
                if mix == "m":
                    act(Dtile[:].rearrange("p h n -> p (h n)"), pd[:], AF.Abs, [pd], [Dtile])
                    ts(Dtile[:], Dtile[:], 1.0, None, ALU.max, None, [Dtile], [Dtile])
                    recip(Dtile, Dtile[:])
                    tt(Ov, po[:], Dtile[:].rearrange("p h n -> p (h n)"), ALU.mult, [po, Dtile], [Otile])
                else:
                    act(Ov, po[:], AF.Identity, [po], [Otile])
                dst = rawf[j_out][:, n0:n0 + 256].rearrange("(h p) n -> p h n", p=128)
                if d == 0:
                    P.dma("sp", dst, Otile[:], reads=[Otile])
                else:
                    P.dma("sp", rawp[:], dst, writes=[rawp])
                    P.dma("sp", gate_t[:], gs[:, n0:n0 + 256].rearrange("(h p) n -> p h n", p=128), writes=[gate_t])
                    tt(rawp[:], rawp[:], Otile[:], ALU.add, [rawp, Otile], [rawp])
                    tt(sq_t[:], rawp[:], rawp[:], ALU.mult, [rawp], [sq_t])
                    pss = PB[0]
                    sqv = sq_t[:].rearrange("p h n -> p (h n)")
                    for hf in range(2):
                        mm(pss[:, hf * 512:(hf + 1) * 512], ones_f[:], sqv[:, hf * 512:(hf + 1) * 512], True, True, [ones_f, sq_t], [pss])
                    act(sqv, pss[:], AF.Ln, [pss], [sq_t], scale=1.0 / 128, bias=EPS)
                    act(sqv, sqv, AF.Exp, [sq_t], [sq_t], scale=-0.5)
                    tt(rawp[:], rawp[:], sq_t[:], ALU.mult, [rawp, sq_t], [rawp])
                    tt(rawp[:], rawp[:], gcol[:].unsqueeze(2).to_broadcast([128, 4, 256]), ALU.mult, [rawp, gcol], [rawp])
                    tt(o_bf[:], rawp[:], gate_t[:], ALU.mult, [rawp, gate_t], [o_bf])
                    P.dma("sp", oF[j_out * BR:(j_out + 1) * BR, n0:n0 + 256].rearrange("(h p) n -> p h n", p=128), o_bf[:], reads=[o_bf])

        def s5_scan():
            memset(xpr, xpr[:], 0.0)
            memset(xpi, xpi[:], 0.0)
            Xr, Xi = Zr, Zi
            for sc in sc_order(d):
                n0 = sc * 256
                P.dma("sp", u[:], FB["uF"][:, n0:n0 + 256].rearrange("(cc p) n -> p cc n", p=128), writes=[u])
                py = PB[1]
                corder = range(4) if d == 0 else range(3, -1, -1)
                pbr, pbi = PB[2], PB[3]

                def emit_bu(c):
                    for j in range(16):
                        cc = j // 4
                        mm(pbr[:, j * 64:(j + 1) * 64], Bpad[:, j, 0, :], u[:, cc, c * 64:(c + 1) * 64], True, True, [Bpad, u], [pbr])
                        mm(pbi[:, j * 64:(j + 1) * 64], Bpad[:, j, 1, :], u[:, cc, c * 64:(c + 1) * 64], True, True, [Bpad, u], [pbi])
                corder = list(corder)
                emit_bu(corder[0])
                for ci, c in enumerate(corder):
                    bur = pbr[:].rearrange("p (j t) -> p j t", t=64)
                    bui = pbi[:].rearrange("p (j t) -> p j t", t=64)
                    cmul(Zr[:], Zi[:], bur, bui, LMr[:], LMi[:], (T1, T1[:]), (T2, T2[:]), [pbr, pbi, LMr, LMi], [Zr, Zi])
                    tin = 0 if d == 0 else 63
                    tt(Zr[:, :, tin], Zr[:, :, tin], xpr[:], ALU.add, [Zr, xpr], [Zr])
                    tt(Zi[:, :, tin], Zi[:, :, tin], xpi[:], ALU.add, [Zi, xpi], [Zi])
                    for (W_, Z_) in ((Wr, Zr), (Wi, Zi)):
                        P.op("dve", lambda e, W_=W_, Z_=Z_: e.tensor_tensor_scan(
                            W_[:].rearrange("p j t -> p (j t)"), rst[:].rearrange("p j t -> p (j t)"),
                            Z_[:].rearrange("p j t -> p (j t)"), 0.0, ALU.mult, ALU.add), reads=[rst, Z_], writes=[W_])
                        if d == 1:
                            cp(tot[:], W_[:, :, 63], [W_], [tot])
                            tt(W_[:], Z_[:], W_[:], ALU.subtract, [Z_, W_], [W_])
                            tt(W_[:], W_[:], tot[:].unsqueeze(2).to_broadcast([128, 16, 64]), ALU.add, [W_, tot], [W_])
                    cmul(Xr[:], Xi[:], Wr[:], Wi[:], LPr[:], LPi[:], (T1, T1[:]), (T2, T2[:]), [Wr, Wi, LPr, LPi], [Xr, Xi])
                    cp(xpr[:], Xr[:, :, tl], [Xr], [xpr])
                    cp(xpi[:], Xi[:, :, tl], [Xi], [xpi])
                    act(Xrb[:], Xr[:], AF.Identity, [Xr], [Xrb])
                    act(Xib[:], Xi[:], AF.Identity, [Xi], [Xib])
                    if ci + 1 < len(corder):
                        emit_bu(corder[ci + 1])
                    for cc in range(4):
                        sl = slice(cc * 256 + c * 64, cc * 256 + c * 64 + 64)
                        k_ = 0
                        for jj in range(4):
                            j = cc * 4 + jj
                            for part, X_ in enumerate((Xrb, Xib)):
                                mm(py[:, sl], Cpad[:, j, part, :], X_[:, j, :], k_ == 0, k_ == 7, [Cpad, X_], [py])
                                k_ += 1
                Ov = Otile[:].rearrange("p h n -> p (h n)")
                act(Ov, py[:], AF.Identity, [py], [Otile])
                dst = rawf[2][:, n0:n0 + 256].rearrange("(h p) n -> p h n", p=128)
                if d == 0:
                    P.dma("sp", dst, Otile[:], reads=[Otile])
                else:
                    P.dma("sp", rawp[:], dst, writes=[rawp])
                    tt(rawp[:], rawp[:], Otile[:], ALU.add, [rawp, Otile], [rawp])
                    cp(Dtile[:], u[:], [u], [Dtile])
                    tt(Dtile[:], Dtile[:], s5d_col[:].unsqueeze(2).to_broadcast([128, 4, 256]), ALU.mult, [Dtile, s5d_col], [Dtile])
                    tt(rawp[:], rawp[:], Dtile[:], ALU.add, [rawp, Dtile], [rawp])
                    act(rawp[:], rawp[:], AF.Gelu, [rawp], [rawp])
                    cp(yc_b[:], rawp[:], [rawp], [yc_b])
                    pg = PB[0]
                    for cco in range(4):
                        for cci in range(4):
                            mm(pg[:, cco * 256:(cco + 1) * 256], gluw[:, cci, cco * 128:(cco + 1) * 128], yc_b[:, cci, :], cci == 0, cci == 3, [gluw, yc_b], [pg])
                    for cco in range(4):
                        act(sq_t[:, cco, :], pg[:, cco * 256:(cco + 1) * 256], AF.Sigmoid, [pg, glub_col], [sq_t], bias=glub_col[:, cco:cco + 1])
                    tt(o_bf[:], rawp[:], sq_t[:], ALU.mult, [rawp, sq_t], [o_bf])
                    P.dma("sp", oF[2 * BR:3 * BR, n0:n0 + 256].rearrange("(h p) n -> p h n", p=128), o_bf[:], reads=[o_bf])

        gla_scan("h", 0, hn_col[0])
        if stop_after == ("s_h" + ("1" if d else ""), l):
            P.barrier()
            return True
        P.barrier()
        ret_tables()
        gla_scan("r", 1, hn_col[1])
        if stop_after == ("s_r" + ("1" if d else ""), l):
            P.barrier()
            return True
        s5_scan()
        if stop_after == ("s_s5" + ("1" if d else ""), l):
            P.barrier()
            return True
        gla_scan("m", 3, hn_col[2])
        P.barrier()
        return False

    def merge_phase(l):
        for j in range(4):
            def epi_cb(cb, j=j):
                def e_(pt, pv, cg, cgn, tok0, tn):
                    row = cb * 512 + cg
                    gt = sbb()
                    P.dma("act", gt[:, 0:tn], gF[j * D + row:j * D + row + 128, tok0:tok0 + tn], writes=[gt])
                    s = sf()
                    tt(s[:, 0:tn], pv[:, 0:tn], gt[:, 0:tn], ALU.mult, [pt, gt], [s])
                    dsl = yaccd[row:row + 128, tok0:tok0 + tn]
                    if j > 0:
                        s2 = sf()
                        P.dma("act", s2[:, 0:tn], dsl, writes=[s2])
                        tt(s[:, 0:tn], s[:, 0:tn], s2[:, 0:tn], ALU.add, [s, s2], [s])
                    if j < 3:
                        P.dma("sp", dsl, s[:, 0:tn], reads=[s])
                    else:
                        sb_ = sbb()
                        cp(sb_[:, 0:tn], s[:, 0:tn], [s], [sb_])
                        P.dma("sp", yF[row:row + 128, tok0:tok0 + tn], sb_[:, 0:tn], reads=[sb_])
                return e_
            jobs = [(cb * 512, 512, "F", epi_cb(cb)) for cb in range(4)]
            gemm(oF[j * BR:(j + 1) * BR, :], BR, NT, w_branch[l, j], jobs, 4352 if NT >= 4352 else NT)

    def resid_epi(gate_m, c0, ncols):
        def e_(pt, pv, tok0, tn):
            v = 1 if tok0 < CTX else 0
            gr = sf()
            P.dma("act", gr[:, 0:ncols], mod_d[v:v + 1, gate_m * D + c0:gate_m * D + c0 + ncols].partition_broadcast(128), writes=[gr])
            s = sf()
            tt(s[0:tn, 0:ncols], pv[0:tn, 0:ncols], gr[0:tn, 0:ncols], ALU.mult, [pt, gr], [s])
            s2 = sf()
            P.dma("act", s2[0:tn, 0:ncols], xres[tok0:tok0 + tn, c0:c0 + ncols], writes=[s2])
            tt(s[0:tn, 0:ncols], s[0:tn, 0:ncols], s2[0:tn, 0:ncols], ALU.add, [s, s2], [s])
            P.dma("sp", xres[tok0:tok0 + tn, c0:c0 + ncols], s[0:tn, 0:ncols], reads=[s])
        return e_

    def wout_phase(l):
        jobs = [(c0, 512, "K", resid_epi(2, c0, 512)) for c0 in range(0, D, 512)]
        TB = 2176 if NT >= 2176 else NT
        gemm(yF, D, NT, w_out[l], jobs, TB)

    def ffn_phase(l):
        def epi1(row0):
            def e_(pt, pv, cg, cgn, tok0, tn):
                s = sf()
                act(s[:, 0:tn], pv[:, 0:tn], AF.Relu, [pt], [s])
                sb_ = sbb()
                tt(sb_[:, 0:tn], s[:, 0:tn], s[:, 0:tn], ALU.mult, [s], [sb_])
                P.dma("sp", aF[row0 + cg:row0 + cg + 128, tok0:tok0 + tn], sb_[:, 0:tn], reads=[sb_])
            return e_
        jobs = [(c0, 512, "F", epi1(c0)) for c0 in range(0, DFF, 512)]
        TB = 2176 if NT >= 2176 else NT
        gemm(hF, D, NT, w_ff1[l], jobs, TB)
        for kh in range(2):
            jobs = [(c0, 256, "K", resid_epi(5, c0, 256)) for c0 in range(0, D, 256)]
            gemm(aF[kh * 4096:(kh + 1) * 4096, :], DFF // 2, NT, w_ff2[l][kh * 4096:(kh + 1) * 4096, :], jobs, 1152)

    def final_phase():
        P.arena_reset()
        fn = P.ar([128, D], F32, "fn")
        xt = [P.ar([128, D], F32, f"fxt{i}") for i in range(2)]
        xn = [P.ar([128, D], F32, f"fxn{i}") for i in range(2)]
        junk = P.ar([128, D], BF16, "fjunk")
        P.dma("sp", fn[:], final_norm.partition_broadcast(128), writes=[fn])
        for ti in range(2, NT // 128):
            x_ = xt[ti % 2]
            xn_ = xn[ti % 2]
            P.dma("sp", x_[:], xres[ti * 128:(ti + 1) * 128, :], writes=[x_])
            r = rstd_of(x_, junk)
            ts(xn_[:], x_[:], r, None, ALU.mult, None, [x_, ssq], [xn_])
            tt(xn_[:], xn_[:], fn[:], ALU.mult, [xn_, fn], [xn_])
            P.dma("sp", out[(ti - 2) * 128:(ti - 1) * 128, :], xn_[:], reads=[xn_])

    def program():
        for l in range(2):
            layer_setup(l)
            if stop_after == ("setup", l):
                return
            norm_phase(0)
            if stop_after == ("norm", l):
                return
            win_phase(l)
            if stop_after == ("win", l):
                return
            conv_phase()
            if stop_after == ("conv", l):
                return
            for d in range(2):
                if scan_phase(l, d):
                    return
                if stop_after == ("s_d0", l):
                    return
            if stop_after == ("scan", l):
                return
            merge_phase(l)
            if stop_after == ("merge", l):
                return
            wout_phase(l)
            if stop_after == ("wout", l):
                return
            norm_phase(1)
            ffn_phase(l)
            if stop_after == ("ffn", l):
                return
        final_phase()

    program()
    P.barrier()
    for name in dump:
        t, shape, dtype = SCR[name]
        o = nc.dram_tensor("dbg_" + name, list(shape), dtype, kind="ExternalOutput").ap()
        P.dma("sp", o, t)
    P.finish()
    return nc


_NC_CACHE = {}


def kernel(**inputs):
    x = np.asarray(inputs["x"], np.float32)
    B, NLAT, _ = x.shape
    ctx = np.asarray(inputs["ctx"], np.float32)
    c = np.asarray(inputs["c"], np.float32)
    c_ctx = np.asarray(inputs["c_ctx"], np.float32)
    if NLAT not in _NC_CACHE:
        _NC_CACHE[NLAT] = build(NLAT)
    nc = _NC_CACHE[NLAT]
    consts = host_consts()
    shared = {}
    for k in ("w_mod", "b_mod", "norm_mix", "norm_mlp", "w_in", "hgrn_lb_logits", "hgrn_norm", "ret_decay",
              "ret_norm", "s5_a_re", "s5_a_im", "s5_log_dt", "s5_b_re", "s5_b_im", "s5_c_re", "s5_c_im", "s5_d",
              "s5_glu_w", "s5_glu_b", "mlstm_conv_w", "mlstm_conv_b", "mlstm_norm", "w_branch", "w_out",
              "w_ff1", "w_ff2"):
        shared[k] = np.ascontiguousarray(np.asarray(inputs[k], np.float32))
    shared["ret_decay"] = np.ascontiguousarray(np.asarray(inputs["ret_decay"], np.float32).reshape(2, 8))
    shared["mlstm_gate_b"] = np.ascontiguousarray(np.asarray(inputs["mlstm_gate_b"], np.float32).reshape(2, 16))
    shared["final_norm"] = np.ascontiguousarray(np.asarray(inputs["final_norm"], np.float32).reshape(1, D))
    shared.update(consts)
    in_maps = []
    for core in range(8):
        b = core % B
        m = dict(shared)
        m["xin"] = np.ascontiguousarray(np.concatenate([ctx[b], x[b]], axis=0))
        m["c2"] = np.ascontiguousarray(np.stack([c[b], c_ctx]))
        in_maps.append(m)
    res = run_bass_kernel_spmd(nc, in_maps, core_ids=list(range(8)))
    outs = [np.asarray(res.results[b]["out"], np.float32) for b in range(B)]
    return np.stack(outs, axis=0)
```
